# Optimizing a Trainium2 kernel written in Bass

```python
import jax, jax.numpy as jnp
from jax import lax
import numpy as np

D_MODEL = 2048
BATCH = 4
SEQ = 2048
DEPTH = 2
DEC_BATCH = 128
DEC_SEQ = 8
PAST_LEN = 16384
PAGE_SIZE = 128

GLA_HEADS = 4
GLA_DK = D_MODEL // 2 // GLA_HEADS
GLA_DV = D_MODEL // GLA_HEADS
GLA_GATE_RANK = 16
GLA_GATE_TAU = 16.0
GLA_CHUNK = 64
SSD_INNER = 2 * D_MODEL
SSD_HEAD_DIM = 64
SSD_HEADS = SSD_INNER // SSD_HEAD_DIM
SSD_GROUPS = 8
SSD_HPG = SSD_HEADS // SSD_GROUPS
SSD_STATE = 128
SSD_CONV = 4
SSD_CONV_DIM = SSD_INNER + 2 * SSD_GROUPS * SSD_STATE
SSD_CHUNK = 64
D_FF = ((8 * D_MODEL // 3 + 255) // 256) * 256
N_ADA = 6
EPS = 1e-6
SPLIT_SIZES = (GLA_HEADS * GLA_DK, GLA_HEADS * GLA_DK, GLA_HEADS * GLA_DV, GLA_HEADS * GLA_DV,
               GLA_GATE_RANK, SSD_INNER, SSD_CONV_DIM, SSD_HEADS, 2 * D_MODEL)
N_IN = sum(SPLIT_SIZES)

kernel_name = "hybrid_gla_ssd_adaln_decoder_step"


def rms_norm(x):
    xf = x.astype(jnp.float32)
    return (xf * lax.rsqrt(jnp.mean(xf * xf, axis=-1, keepdims=True) + EPS)).astype(x.dtype)


def _chunk(a, L, n):
    B, T = a.shape[:2]
    a = jnp.pad(a.astype(jnp.float32), [(0, 0), (0, n * L - T)] + [(0, 0)] * (a.ndim - 2))
    return jnp.moveaxis(a.reshape(B, n, L, *a.shape[2:]), 1, 0)


def gla_chunked(q, k, v, log_a, S0):
    B, T, H, _ = q.shape
    L = min(GLA_CHUNK, T)
    n = -(-T // L)
    qc, kc, vc, gc = [jnp.swapaxes(_chunk(a, L, n), 2, 3) for a in (q, k, v, log_a)]
    mask = jnp.tril(jnp.ones((L, L), dtype=bool))

    def step(S, inp):
        qi, ki, vi, gi = inp
        b = jnp.cumsum(gi, axis=2)
        q_ = qi * jnp.exp(b)
        k_ = ki * jnp.exp(-b)
        att = jnp.where(mask, jnp.einsum('bhtk,bhsk->bhts', q_, k_), 0.0)
        o = jnp.einsum('bhtk,bhkv->bhtv', q_, S) + jnp.einsum('bhts,bhsv->bhtv', att, vi)
        bL = b[:, :, -1:, :]
        S = jnp.exp(bL[:, :, 0, :])[..., None] * S + jnp.einsum('bhsk,bhsv->bhkv', ki * jnp.exp(bL - b), vi)
        return S, o

    S, o = lax.scan(step, S0.astype(jnp.float32), (qc, kc, vc, gc))
    o = jnp.moveaxis(jnp.swapaxes(o, 2, 3), 0, 1).reshape(B, n * L, H, GLA_DV)[:, :T]
    return o, S


def ssd_chunked(xh, dt, A, Bm, Cm, h0):
    Bsz, T = xh.shape[:2]
    L = min(SSD_CHUNK, T)
    n = -(-T // L)
    xc, dtc, Bc, Cc = [_chunk(a, L, n) for a in (xh, dt, Bm, Cm)]
    mask = jnp.tril(jnp.ones((L, L), dtype=bool))[None, :, :, None, None]

    def step(h, inp):
        xi, dti, Bi, Ci = inp
        cs = jnp.cumsum(dti * A, axis=1)
        seg = cs[:, :, None] - cs[:, None, :]
        decay = jnp.exp(jnp.where(mask, seg, -jnp.inf))
        cb = jnp.einsum('btgn,bsgn->btsg', Ci, Bi)
        w = cb[..., None] * decay * dti[:, None]
        y = jnp.einsum('btsgh,bsghp->btghp', w, xi)
        y = y + jnp.exp(cs)[..., None] * jnp.einsum('btgn,bghpn->btghp', Ci, h)
        dL = jnp.exp(cs[:, -1:] - cs) * dti
        h = jnp.exp(cs[:, -1])[..., None, None] * h + jnp.einsum('bsgh,bsghp,bsgn->bghpn', dL, xi, Bi)
        return h, y

    h, y = lax.scan(step, h0.astype(jnp.float32), (xc, dtc, Bc, Cc))
    y = jnp.moveaxis(y, 0, 1).reshape(Bsz, n * L, SSD_GROUPS, SSD_HPG, SSD_HEAD_DIM)[:, :T]
    return y, h


def causal_conv(u, buf, w, b):
    T = u.shape[1]
    full = jnp.concatenate([buf.astype(u.dtype), u], axis=1)
    out = b + sum(full[:, i:i + T] * w[i] for i in range(SSD_CONV))
    return out, full[:, -(SSD_CONV - 1):]


def mixer(h, S_gla, h_ssm, conv_buf, w_in, w_gla_gate, b_gla_gate, gla_norm, w_gla_proj,
          conv_w, conv_b, dt_bias, A_log, d_skip, ssd_norm, w_ssd_proj, w_mix_out):
    Bsz, T, _ = h.shape
    dtype = h.dtype
    proj = h @ w_in
    points, acc = [], 0
    for s in SPLIT_SIZES[:-1]:
        acc += s
        points.append(acc)
    q, k, v, r, glr, z, xbc, dt, gates = jnp.split(proj, points, axis=-1)
    q = q.reshape(Bsz, T, GLA_HEADS, GLA_DK) * (GLA_DK ** -0.5)
    k = k.reshape(Bsz, T, GLA_HEADS, GLA_DK)
    v = v.reshape(Bsz, T, GLA_HEADS, GLA_DV)
    log_a = jax.nn.log_sigmoid((glr @ w_gla_gate + b_gla_gate).astype(jnp.float32)) / GLA_GATE_TAU
    o, S_new = gla_chunked(q, k, v, log_a.reshape(Bsz, T, GLA_HEADS, GLA_DK), S_gla)
    o = (rms_norm(o) * gla_norm).reshape(Bsz, T, GLA_HEADS * GLA_DV).astype(dtype) * jax.nn.silu(r)
    y_a = o @ w_gla_proj
    xbc, conv_new = causal_conv(xbc, conv_buf, conv_w, conv_b)
    xbc = jax.nn.silu(xbc)
    xs, Bm, Cm = jnp.split(xbc, [SSD_INNER, SSD_INNER + SSD_GROUPS * SSD_STATE], axis=-1)
    xh = xs.reshape(Bsz, T, SSD_GROUPS, SSD_HPG, SSD_HEAD_DIM)
    Bm = Bm.reshape(Bsz, T, SSD_GROUPS, SSD_STATE)
    Cm = Cm.reshape(Bsz, T, SSD_GROUPS, SSD_STATE)
    dtp = jax.nn.softplus((dt + dt_bias).astype(jnp.float32)).reshape(Bsz, T, SSD_GROUPS, SSD_HPG)
    A = -jnp.exp(A_log.astype(jnp.float32)).reshape(SSD_GROUPS, SSD_HPG)
    y, h_new = ssd_chunked(xh, dtp, A, Bm, Cm,
                           h_ssm.reshape(Bsz, SSD_GROUPS, SSD_HPG, SSD_HEAD_DIM, SSD_STATE))
    y = y + d_skip.reshape(SSD_GROUPS, SSD_HPG)[..., None].astype(jnp.float32) * xh.astype(jnp.float32)
    y = y.reshape(Bsz, T, SSD_INNER) * jax.nn.silu(z.astype(jnp.float32))
    y = rms_norm(y.reshape(Bsz, T, SSD_GROUPS, SSD_INNER // SSD_GROUPS)).reshape(Bsz, T, SSD_INNER)
    y_b = (y * ssd_norm).astype(dtype) @ w_ssd_proj
    g_a, g_b = jnp.split(gates, 2, axis=-1)
    merged = jax.nn.sigmoid(g_a) * y_a + jax.nn.sigmoid(g_b) * y_b
    return (merged @ w_mix_out, S_new,
            h_new.reshape(Bsz, SSD_HEADS, SSD_HEAD_DIM, SSD_STATE), conv_new)


def trunk(x, c, S_gla, h_ssm, conv_buf, params):
    (w_ada, b_ada, w_in, w_gla_gate, b_gla_gate, gla_norm, w_gla_proj, conv_w, conv_b, dt_bias,
     A_log, d_skip, ssd_norm, w_ssd_proj, w_mix_out, w_ffn_in, w_ffn_out, final_norm) = params
    new_gla, new_ssm, new_conv = [], [], []
    for l in range(DEPTH):
        mod = (jax.nn.silu(c) @ w_ada[l] + b_ada[l])[:, None, :]
        sh1, sc1, g1, sh2, sc2, g2 = jnp.split(mod, N_ADA, axis=-1)
        h = rms_norm(x) * (1 + sc1) + sh1
        m, S, hs, cb = mixer(h, S_gla[l], h_ssm[l], conv_buf[l], w_in[l], w_gla_gate[l], b_gla_gate[l],
                             gla_norm[l], w_gla_proj[l], conv_w[l], conv_b[l], dt_bias[l], A_log[l],
                             d_skip[l], ssd_norm[l], w_ssd_proj[l], w_mix_out[l])
        x = x + g1 * m
        h = rms_norm(x) * (1 + sc2) + sh2
        gt, up = jnp.split(h @ w_ffn_in[l], 2, axis=-1)
        x = x + g2 * ((jax.nn.silu(gt) * up) @ w_ffn_out[l])
        new_gla.append(S)
        new_ssm.append(hs)
        new_conv.append(cb)
    y = rms_norm(x) * final_norm
    return y, jnp.stack(new_gla), jnp.stack(new_ssm), jnp.stack(new_conv)


def setup_inputs(seed: int = 0) -> dict:
    key = jax.random.key(seed)
    ks = jax.random.split(key, 32)
    f32 = jnp.float32
    nrm = lambda k, shape, s: jax.random.normal(k, shape, f32) * s
    u = jax.random.uniform(ks[20], (DEPTH, SSD_HEADS), f32)
    dt0 = jnp.exp(u * (jnp.log(0.1) - jnp.log(0.001)) + jnp.log(0.001))
    return {
        "x_prompt": nrm(ks[0], (BATCH, SEQ, D_MODEL), 1.0),
        "x_sample": nrm(ks[1], (DEC_BATCH, DEC_SEQ, D_MODEL), 1.0),
        "c_prompt": nrm(ks[2], (BATCH, D_MODEL), 1.0),
        "c_sample": nrm(ks[3], (DEC_BATCH, D_MODEL), 1.0),
        "state_gla": nrm(ks[4], (DEPTH, DEC_BATCH, GLA_HEADS, GLA_DK, GLA_DV), 0.5),
        "state_ssm": nrm(ks[5], (DEPTH, DEC_BATCH, SSD_HEADS, SSD_HEAD_DIM, SSD_STATE), 0.1),
        "state_conv": nrm(ks[6], (DEPTH, DEC_BATCH, SSD_CONV - 1, SSD_CONV_DIM), 1.0),
        "w_ada": nrm(ks[7], (DEPTH, D_MODEL, N_ADA * D_MODEL), D_MODEL ** -0.5),
        "b_ada": nrm(ks[8], (DEPTH, N_ADA * D_MODEL), 0.02),
        "w_in": nrm(ks[9], (DEPTH, D_MODEL, N_IN), D_MODEL ** -0.5),
        "w_gla_gate": nrm(ks[10], (DEPTH, GLA_GATE_RANK, GLA_HEADS * GLA_DK), GLA_GATE_RANK ** -0.5),
        "b_gla_gate": nrm(ks[11], (DEPTH, GLA_HEADS * GLA_DK), 0.1),
        "gla_norm": 1.0 + nrm(ks[12], (DEPTH, GLA_DV), 0.02),
        "w_gla_proj": nrm(ks[13], (DEPTH, GLA_HEADS * GLA_DV, D_MODEL), (GLA_HEADS * GLA_DV) ** -0.5),
        "conv_w": nrm(ks[14], (DEPTH, SSD_CONV, SSD_CONV_DIM), SSD_CONV ** -0.5),
        "conv_b": nrm(ks[15], (DEPTH, SSD_CONV_DIM), 0.02),
        "dt_bias": dt0 + jnp.log(-jnp.expm1(-dt0)),
        "A_log": jnp.log(jax.random.uniform(ks[16], (DEPTH, SSD_HEADS), f32, 1.0, 16.0)),
        "d_skip": 1.0 + nrm(ks[17], (DEPTH, SSD_HEADS), 0.02),
        "ssd_norm": 1.0 + nrm(ks[18], (DEPTH, SSD_INNER), 0.02),
        "w_ssd_proj": nrm(ks[19], (DEPTH, SSD_INNER, D_MODEL), SSD_INNER ** -0.5),
        "w_mix_out": nrm(ks[21], (DEPTH, D_MODEL, D_MODEL), D_MODEL ** -0.5),
        "w_ffn_in": nrm(ks[22], (DEPTH, D_MODEL, 2 * D_FF), D_MODEL ** -0.5),
        "w_ffn_out": nrm(ks[23], (DEPTH, D_FF, D_MODEL), D_FF ** -0.5),
        "final_norm": 1.0 + nrm(ks[24], (D_MODEL,), 0.02),
    }


def reference(x_prompt, x_sample, c_prompt, c_sample, state_gla, state_ssm, state_conv,
              w_ada, b_ada, w_in, w_gla_gate, b_gla_gate, gla_norm, w_gla_proj, conv_w, conv_b,
              dt_bias, A_log, d_skip, ssd_norm, w_ssd_proj, w_mix_out, w_ffn_in, w_ffn_out, final_norm):
    params = (w_ada, b_ada, w_in, w_gla_gate, b_gla_gate, gla_norm, w_gla_proj, conv_w, conv_b, dt_bias,
              A_log, d_skip, ssd_norm, w_ssd_proj, w_mix_out, w_ffn_in, w_ffn_out, final_norm)
    bp = x_prompt.shape[0]
    zero_gla = jnp.zeros((DEPTH, bp, GLA_HEADS, GLA_DK, GLA_DV), jnp.float32)
    zero_ssm = jnp.zeros((DEPTH, bp, SSD_HEADS, SSD_HEAD_DIM, SSD_STATE), jnp.float32)
    zero_conv = jnp.zeros((DEPTH, bp, SSD_CONV - 1, SSD_CONV_DIM), x_prompt.dtype)
    y_prompt, gla_p, ssm_p, conv_p = trunk(x_prompt, c_prompt, zero_gla, zero_ssm, zero_conv, params)
    y_sample, gla_s, ssm_s, conv_s = trunk(x_sample, c_sample, state_gla, state_ssm, state_conv, params)
    return (y_prompt, y_sample, gla_p, ssm_p, conv_p, gla_s, ssm_s, conv_s)
```

```python
import numpy as np
import concourse.bass as bass
import concourse.mybir as mybir
from concourse.bass_utils import run_bass_kernel_spmd
from contextlib import ExitStack

F32 = mybir.dt.float32
BF16 = mybir.dt.bfloat16
AF = mybir.ActivationFunctionType
ALU = mybir.AluOpType
AX = mybir.AxisListType

D = 2048
NT = 17
NPT = 16
TOK = NT * 128
NSEG = 16
SL = 8
DEPTH = 2
DFF = 5632
NIN = 20560
EPS = 1e-6
C_Q, C_K, C_V, C_R, C_GLR, C_Z, C_XBC, C_DT, C_G = 0, 1024, 2048, 4096, 6144, 6160, 10256, 16400, 16464
NEG = -30000.0


class DSem:
    def __init__(self, sem, name):
        self.sem = sem
        self.cnt = 0
        self.name = name


class Buf:
    def __init__(self, name, t, dsem=None):
        self.name = name
        self.t = t
        self.wr = {}
        self.rd = {}
        self.dsem = dsem

    def __getitem__(self, k):
        return self.t[k]


class Sched:
    ENGS = ("pe", "act", "dve", "pool", "sp")

    def __init__(self, nc, es, ndsem=40):
        self.nc = nc
        self.sem = {e: es.enter_context(nc.semaphore("s_" + e)) for e in ("pe", "act", "dve", "pool")}
        self.cnt = {e: 0 for e in self.ENGS}
        self.seen = {e: {} for e in self.ENGS}
        self.prog = {e: [] for e in self.ENGS}
        self.dram = {}
        self.dpool = [DSem(es.enter_context(nc.semaphore("d%d" % i)), "d%d" % i) for i in range(ndsem)]
        self.dfree = list(self.dpool)
        self.nops = 0

    def sb(self, pes, name, shape, dtype=F32, dma=False):
        self.nops += 1
        name = "%s_u%d" % (name, self.nops)
        t = pes.enter_context(self.nc.sbuf_tensor(name, list(shape), dtype))
        ds = None
        if dma:
            ds = self.dfree.pop()
            pes.callback(self.dfree.append, ds)
        return Buf(name, t, ds)

    def ps(self, pes, name, shape, dtype=F32):
        self.nops += 1
        name = "%s_u%d" % (name, self.nops)
        t = pes.enter_context(self.nc.psum_tensor(name, list(shape), dtype))
        return Buf(name, t)

    def dbuf(self, pes, name):
        ds = self.dfree.pop()
        pes.callback(self.dfree.append, ds)
        return Buf(name, None, ds)

    @staticmethod
    def _add(need, d, skip=None, pe=False):
        for k, (sem, val) in d.items():
            if pe and k == "pe":
                continue
            if skip is not None and k == skip:
                continue
            if k not in need or need[k][1] < val:
                need[k] = (sem, val)

    def _waits(self, eng, need):
        waits = []
        seen = self.seen[eng]
        for k, (sem, val) in need.items():
            if seen.get(k, 0) >= val:
                continue
            seen[k] = val
            waits.append((sem, val))
        return waits

    def op(self, eng, fn, reads=(), writes=()):
        need = {}
        pe = eng == "pe"
        for b in reads:
            self._add(need, b.wr, pe=pe)
        for b in writes:
            self._add(need, b.wr, pe=pe)
            self._add(need, b.rd, skip=eng, pe=pe)
        waits = self._waits(eng, need)
        self.cnt[eng] += 1
        ev = (self.sem[eng], self.cnt[eng])
        self.prog[eng].append((waits, fn, ev[0], 1))
        for b in writes:
            b.wr = {eng: ev}
            b.rd = {}
        for b in reads:
            if b not in writes:
                b.rd[eng] = ev
        self.nops += 1

    def dma(self, q, out, in_, sb_w=None, sb_r=None, after=(), produces=None, evbuf=None):
        need = {}
        if sb_r is not None:
            self._add(need, sb_r.wr)
        if sb_w is not None:
            self._add(need, sb_w.wr)
            self._add(need, sb_w.rd)
        for key in after:
            for (k, sem, val) in self.dram.get(key, ()):
                if k not in need or need[k][1] < val:
                    need[k] = (sem, val)
        waits = self._waits(q, need)
        eb = evbuf if evbuf is not None else (sb_w if sb_w is not None else sb_r)
        ds = eb.dsem
        ds.cnt += 1
        ev = (ds.sem, 16 * ds.cnt)
        k = ds.name
        self.prog[q].append((waits, (lambda e, o=out, i=in_: e.dma_start(out=o, in_=i)), ds.sem, 16))
        if sb_w is not None:
            sb_w.wr = {k: ev}
            sb_w.rd = {}
        if sb_r is not None:
            sb_r.rd[k] = ev
        if produces is not None:
            self.dram.setdefault(produces, []).append((k, ev[0], ev[1]))
        self.nops += 1

    def barrier(self):
        need = {}
        for e in ("pe", "act", "dve", "pool"):
            if self.cnt[e] > 0:
                need[e] = (self.sem[e], self.cnt[e])
        for ds in self.dpool:
            if ds.cnt > 0:
                need[ds.name] = (ds.sem, 16 * ds.cnt)
        for e in self.ENGS:
            waits = self._waits(e, dict(need))
            if waits:
                self.prog[e].append((waits, None, None, 0))

    def emit(self):
        nc = self.nc
        prog = self.prog
        with nc.Block() as block:
            def run(eng_obj, items):
                for (waits, fn, sem, inc) in items:
                    for (ws, wv) in waits:
                        eng_obj.wait_ge(ws, wv)
                    if fn is not None:
                        fn(eng_obj).then_inc(sem, inc)

            @block.tensor
            def _(e):
                run(e, prog["pe"])

            @block.scalar
            def _(e):
                run(e, prog["act"])

            @block.vector
            def _(e):
                run(e, prog["dve"])

            @block.gpsimd
            def _(e):
                run(e, prog["pool"])

            @block.sync
            def _(e):
                run(e, prog["sp"])

    def mm(self, out, lhsT, rhs, start, stop, reads, writes):
        self.op("pe", lambda e: e.matmul(out, lhsT=lhsT, rhs=rhs, start=start, stop=stop), reads, writes)

    def tr(self, out, in_, ident, reads, writes):
        self.op("pe", lambda e: e.transpose(out=out, in_=in_, identity=ident), reads, writes)

    def act(self, out, in_, func, reads, writes, bias=0.0, scale=1.0, accum=None):
        if accum is None:
            self.op("act", lambda e: e.activation(out=out, in_=in_, func=func, bias=bias, scale=scale), reads, writes)
        else:
            self.op("act", lambda e: e.activation(out=out, in_=in_, func=func, bias=bias, scale=scale, accum_out=accum), reads, writes)

    def tt(self, eng, out, in0, in1, op, reads, writes):
        self.op(eng, lambda e: e.tensor_tensor(out=out, in0=in0, in1=in1, op=op), reads, writes)

    def ts(self, eng, out, in0, s1, s2, op0, op1, reads, writes):
        if s2 is None:
            self.op(eng, lambda e: e.tensor_scalar(out=out, in0=in0, scalar1=s1, scalar2=None, op0=op0), reads, writes)
        else:
            self.op(eng, lambda e: e.tensor_scalar(out=out, in0=in0, scalar1=s1, scalar2=s2, op0=op0, op1=op1), reads, writes)

    def stt(self, eng, out, in0, scalar, in1, op0, op1, reads, writes):
        self.op(eng, lambda e: e.scalar_tensor_tensor(out=out, in0=in0, scalar=scalar, in1=in1, op0=op0, op1=op1), reads, writes)

    def cp(self, eng, out, in_, reads, writes):
        if eng == "act":
            self.op("act", lambda e: e.copy(out=out, in_=in_), reads, writes)
        else:
            self.op(eng, lambda e: e.tensor_copy(out=out, in_=in_), reads, writes)

    def memset(self, eng, ap, val, writes):
        self.op(eng, lambda e: e.memset(ap, val), (), writes)

    def recip(self, out, in_, reads, writes):
        self.op("dve", lambda e: e.reciprocal(out=out, in_=in_), reads, writes)


K_ID, K_TRIP, K_TRIS, K_SELP, K_SELS, K_ONES, K_NEGP, K_NEGS, K_ROW0, K_ROWP, K_SEGCOL, K_NTRIP, K_NTRIS = 0, 1, 2, 3, 4, 5, 6, 7, 8, 24, 25, 26, 27
NCONST = 28


def make_consts():
    c = np.zeros((NCONST, 128, 128), np.float32)
    idx = np.arange(128)
    s = idx[:, None]
    t = idx[None, :]
    c[K_ID] = (s == t)
    c[K_TRIP] = (s <= t)
    same = (s // SL) == (t // SL)
    c[K_TRIS] = same & (s <= t)
    c[K_SELP] = (s == 127).astype(np.float32) - (s == t)
    last = (t // SL) * SL + SL - 1
    c[K_SELS] = (s == last).astype(np.float32) - (s == t)
    c[K_ONES] = 1.0
    c[K_NEGP] = np.where(s <= t, 0.0, NEG)
    c[K_NEGS] = np.where(same & (s <= t), 0.0, NEG)
    for j in range(NSEG):
        c[K_ROW0 + j] = (s == (SL * j + SL - 1)) * np.ones((1, 128))
    c[K_ROWP] = (s == 127) * np.ones((1, 128))
    c[K_SEGCOL][:, :NSEG] = ((idx[:, None] // SL) == np.arange(NSEG)[None, :])
    c[K_NTRIP] = -c[K_TRIP] / 16.0
    c[K_NTRIS] = -c[K_TRIS] / 16.0
    return np.ascontiguousarray(c.transpose(1, 0, 2))


def w_blocks():
    bl = []
    for (c0, n) in ((0, 6144), (C_GLR, 16), (C_Z, 4096), (C_XBC, 6144), (C_DT, 64), (C_G, 4096)):
        o = 0
        while o < n:
            w = min(512, n - o)
            bl.append((c0 + o, w))
            o += w
    return bl


def build_nc():
    nc = bass.Bass("TRN2", target_bir_lowering=False)

    def din(name, shape, dt=F32):
        return nc.dram_tensor(name, list(shape), dt, kind="ExternalInput").ap()

    def dout(name, shape, dt=F32):
        return nc.dram_tensor(name, list(shape), dt, kind="ExternalOutput").ap()

    def dint(name, shape, dt=F32):
        return nc.dram_tensor(name, list(shape), dt, kind="Internal").ap()

    xin = din("xin", [TOK, D])
    cc = din("cc", [2, 128, D])
    consts = din("consts", [128, NCONST, 128])
    st_gla = din("st_gla", [DEPTH, NSEG, 4, 256, 512])
    st_ssm = din("st_ssm", [DEPTH, NSEG, 64, 64, 128])
    st_conv = din("st_conv", [DEPTH, NSEG, 3, 6144])
    w_ada = din("w_ada", [DEPTH, D, 6 * D])
    b_ada = din("b_ada", [DEPTH, 6 * D])
    w_in = din("w_in", [DEPTH, D, NIN])
    w_gate = din("w_gla_gate", [DEPTH, 16, 1024])
    b_gate = din("b_gla_gate", [DEPTH, 1024])
    gla_norm = din("gla_norm", [DEPTH, 512])
    w_gproj = din("w_gla_proj", [DEPTH, D, D])
    conv_w = din("conv_w", [DEPTH, 4, 6144])
    conv_b = din("conv_b", [DEPTH, 6144])
    dt_bias = din("dt_bias", [DEPTH, 64])
    A_log = din("A_log", [DEPTH, 64])
    d_skip = din("d_skip", [DEPTH, 64])
    ssd_norm = din("ssd_norm", [DEPTH, 4096])
    w_sproj = din("w_ssd_proj", [DEPTH, 4096, D])
    w_mix = din("w_mix_out", [DEPTH, D, D])
    w_fin = din("w_ffn_in", [DEPTH, D, 2 * DFF])
    w_fout = din("w_ffn_out", [DEPTH, DFF, D])
    fnorm = din("final_norm", [1, D])

    y_out = dout("y", [TOK, D])
    gla_p = dout("gla_p", [DEPTH, 4, 256, 512])
    ssm_p = dout("ssm_p", [DEPTH, 64, 64, 128])
    conv_p = dout("conv_p", [DEPTH, 3, 6144])
    gla_s = dout("gla_s", [DEPTH, NSEG, 4, 256, 512])
    ssm_s = dout("ssm_s", [DEPTH, NSEG, 64, 64, 128])
    conv_s = dout("conv_s", [DEPTH, NSEG, 3, 6144])

    proj = [dint("proj%d" % l, [TOK, NIN]) for l in range(DEPTH)]
    xbp = [dint("xbp%d" % l, [3 + NPT * 128, 6144]) for l in range(DEPTH)]
    xbs = [dint("xbs%d" % l, [NSEG, 3 + SL, 6144]) for l in range(DEPTH)]
    xcs = [dint("xc%d" % l, [TOK, 6144]) for l in range(DEPTH)]
    mod = [dint("mod%d" % l, [2, 128, 6 * D]) for l in range(DEPTH)]
    xv = [[dint("xv%d_%d" % (l, v), [TOK, D]) for v in range(2)] for l in range(DEPTH)]
    oaT = [dint("oaT%d" % l, [NT, 128, 16, 128], BF16) for l in range(DEPTH)]
    ybT = [dint("ybT%d" % l, [NT, 128, 32, 128], BF16) for l in range(DEPTH)]
    actT = [dint("actT%d" % l, [44, 128, TOK], BF16) for l in range(DEPTH)]

    with ExitStack() as es:
        S = Sched(nc, es)
        cst = S.sb(es, "cst", [128, NCONST, 128], F32, dma=True)
        identb = S.sb(es, "identb", [128, 128], BF16)
        S.dma("sp", cst[:], consts, sb_w=cst)
        S.cp("dve", identb[:], cst[:, K_ID, :], [cst], [identb])
        ident = cst[:, K_ID, :]

        def rows(i):
            return slice(i * 128, (i + 1) * 128)

        def grp(i):
            return 0 if i < NPT else 1

        def hview(hT, i, k):
            return hT[i // 4][:, k, (i % 4) * 128:(i % 4 + 1) * 128]

        def phase_mod():
            with ExitStack() as pes:
                cct = S.sb(pes, "cct", [128, D], F32, dma=True)
                scb = S.sb(pes, "scb", [128, D], BF16)
                scT = [S.sb(pes, "scT%d" % g, [128, 16, 128], BF16) for g in range(2)]
                ptr = [S.ps(pes, "mptr%d" % i, [128, 8, 128], BF16) for i in range(2)]
                pm = [S.ps(pes, "pm%d" % i, [128, 512], F32) for i in range(4)]
                for g in range(2):
                    S.dma("sp", cct[:], cc[g], sb_w=cct)
                    S.act(scb[:], cct[:], AF.Silu, [cct], [scb])
                    for half in range(2):
                        p = ptr[half]
                        for k in range(8):
                            kk = half * 8 + k
                            S.tr(p[:, k, :], scb[:, kk * 128:(kk + 1) * 128], identb[:], [scb, identb], [p])
                        S.cp("dve" if half == 0 else "act", scT[g][:, half * 8:(half + 1) * 8, :], p[:], [p], [scT[g]])
                wa = [S.sb(pes, "wa%d" % i, [128, 16, 512], BF16, dma=True) for i in range(2)]
                bb = [S.sb(pes, "bb%d" % i, [128, 512], F32, dma=True) for i in range(2)]
                stg = [S.sb(pes, "mst%d" % i, [128, 512], F32, dma=True) for i in range(4)]
                it = 0
                for l in range(DEPTH):
                    wv = w_ada[l].rearrange("(kt p) n -> p kt n", p=128)
                    for cb in range(24):
                        cs_ = slice(cb * 512, (cb + 1) * 512)
                        w = wa[it % 2]
                        b = bb[it % 2]
                        S.dma("pool", w[:], wv[:, :, cs_], sb_w=w)
                        S.dma("sp", b[:], b_ada[l:l + 1, cs_].partition_broadcast(128), sb_w=b)
                        for g in range(2):
                            n = it * 2 + g
                            p = pm[n % 4]
                            s_ = stg[n % 4]
                            for k in range(16):
                                S.mm(p[:], scT[g][:, k, :], w[:, k, :], k == 0, k == 15, [scT[g], w], [p])
                            S.tt("dve", s_[:], p[:], b[:], ALU.add, [p, b], [s_])
                            S.dma("sp", mod[l][g, :, cs_], s_[:], sb_r=s_)
                        it += 1
            S.barrier()

        def phase_norm(xsrc, l, sh_i, sc_i, hT):
            with ExitStack() as pes:
                xt = [S.sb(pes, "nx%d" % i, [128, D], F32, dma=True) for i in range(2)]
                shb = S.sb(pes, "shb", [128, D], F32, dma=True)
                scb = S.sb(pes, "nscb", [128, D], F32, dma=True)
                junk = S.sb(pes, "njunk", [128, D], F32)
                ss = S.sb(pes, "nss", [128, 2], F32)
                hb = [S.sb(pes, "hb%d" % i, [128, D], BF16) for i in range(2)]
                ptr = [S.ps(pes, "nptr%d" % i, [128, 8, 128], BF16) for i in range(2)]
                n = 0
                for i in range(NT):
                    g = grp(i)
                    if i == 0 or i == NPT:
                        S.dma("sp", shb[:], mod[l][g, :, sh_i * D:(sh_i + 1) * D], sb_w=shb)
                        S.dma("sp", scb[:], mod[l][g, :, sc_i * D:(sc_i + 1) * D], sb_w=scb)
                        S.ts("dve", scb[:], scb[:], 1.0, None, ALU.add, None, [scb], [scb])
                    x = xt[i % 2]
                    S.dma("sp", x[:], xsrc[rows(i), :], sb_w=x)
                    S.memset("pool", ss[:], 0.0, [ss])
                    S.act(junk[:], x[:], AF.Square, [x, ss], [junk, ss], accum=ss[:, 0:1])
                    S.ts("dve", ss[:, 1:2], ss[:, 0:1], 1.0 / D, EPS, ALU.mult, ALU.add, [ss], [ss])
                    S.act(ss[:, 1:2], ss[:, 1:2], AF.Sqrt, [ss], [ss])
                    S.recip(ss[:, 1:2], ss[:, 1:2], [ss], [ss])
                    S.stt("dve", junk[:], x[:], ss[:, 1:2], scb[:], ALU.mult, ALU.mult, [x, ss, scb], [junk])
                    h = hb[i % 2]
                    S.tt("pool", h[:], junk[:], shb[:], ALU.add, [junk, shb], [h])
                    for half in range(2):
                        p = ptr[n % 2]
                        n += 1
                        for k in range(8):
                            kk = half * 8 + k
                            S.tr(p[:, k, :], h[:, kk * 128:(kk + 1) * 128], identb[:], [h, identb], [p])
                        S.cp("act" if half == 0 else "dve", hT[i // 4][:, half * 8:(half + 1) * 8, (i % 4) * 128:(i % 4 + 1) * 128],
                             p[:], [p], [hT[i // 4]])
            S.barrier()

        def phase_proj(l, hT):
            with ExitStack() as pes:
                wb = [S.sb(pes, "wb%d" % i, [128, 16, 512], BF16, dma=True) for i in range(2)]
                pp = [S.ps(pes, "pp%d" % i, [128, 512], F32) for i in range(4)]
                stg = [S.sb(pes, "pst%d" % i, [128, 512], F32, dma=True) for i in range(4)]
                zt = S.sb(pes, "zt", [128, 144], F32, dma=True)
                dd = S.dbuf(pes, "dd_conv")
                S.memset("pool", zt[:], 0.0, [zt])
                S.dma("sp", xbp[l][0:3, :].rearrange("r c -> (r c)").rearrange("(p f) -> p f", p=128), zt[:], sb_r=zt)
                S.dma("sp", xbs[l][:, 0:3, :], st_conv[l], evbuf=dd)
                wv = w_in[l].rearrange("(kt p) n -> p kt n", p=128)
                n = 0
                for bi, (c0, w) in enumerate(w_blocks()):
                    W = wb[bi % 2]
                    S.dma("pool", W[:, :, 0:w], wv[:, :, c0:c0 + w], sb_w=W)
                    isx = C_XBC <= c0 < C_DT
                    for i in range(NT):
                        p = pp[n % 4]
                        s_ = stg[n % 4]
                        n += 1
                        for k in range(16):
                            S.mm(p[:, 0:w], hview(hT, i, k), W[:, k, 0:w], k == 0, k == 15, [hT[i // 4], W], [p])
                        S.cp("act" if n % 2 else "dve", s_[:, 0:w], p[:, 0:w], [p], [s_])
                        if isx:
                            xc0 = c0 - C_XBC
                            if i < NPT:
                                S.dma("sp", xbp[l][3 + i * 128:3 + (i + 1) * 128, xc0:xc0 + w], s_[:, 0:w], sb_r=s_)
                            else:
                                S.dma("sp", xbs[l][:, 3:3 + SL, xc0:xc0 + w], s_[:, 0:w], sb_r=s_)
                        else:
                            S.dma("sp", proj[l][rows(i), c0:c0 + w], s_[:, 0:w], sb_r=s_)
            S.barrier()
            with ExitStack() as pes:
                dd = S.dbuf(pes, "dd_conv2")
                S.dma("sp", conv_p[l], xbp[l][NPT * 128:NPT * 128 + 3, :], evbuf=dd)
                S.dma("sp", conv_s[l], xbs[l][:, SL:SL + 3, :], evbuf=dd)

        def phase_conv(l):
            CH = 1536
            with ExitStack() as pes:
                cw = S.sb(pes, "cw", [128, 4, CH], F32, dma=True)
                cbs = S.sb(pes, "cbs", [128, CH], F32, dma=True)
                xs_ = [[S.sb(pes, "cx%d_%d" % (b, i), [128, CH], F32, dma=True) for i in range(4)] for b in range(2)]
                acc = [S.sb(pes, "cacc%d" % b, [128, CH], F32, dma=True) for b in range(2)]
                for c in range(4):
                    cs_ = slice(c * CH, (c + 1) * CH)
                    for t4 in range(4):
                        S.dma("sp", cw[:, t4, :], conv_w[l, t4:t4 + 1, cs_].partition_broadcast(128), sb_w=cw)
                    S.dma("sp", cbs[:], conv_b[l:l + 1, cs_].partition_broadcast(128), sb_w=cbs)
                    for i in range(NT):
                        X = xs_[i % 2]
                        a = acc[i % 2]
                        for t4 in range(4):
                            if i < NPT:
                                src = xbp[l][i * 128 + t4:i * 128 + t4 + 128, cs_]
                            else:
                                src = xbs[l][:, t4:t4 + SL, cs_]
                            S.dma("sp", X[t4][:], src, sb_w=X[t4])
                        S.tt("dve", X[3][:], X[3][:], cw[:, 3, :], ALU.mult, [X[3], cw], [X[3]])
                        S.tt("pool", X[2][:], X[2][:], cw[:, 2, :], ALU.mult, [X[2], cw], [X[2]])
                        S.tt("dve", X[1][:], X[1][:], cw[:, 1, :], ALU.mult, [X[1], cw], [X[1]])
                        S.tt("pool", X[0][:], X[0][:], cw[:, 0, :], ALU.mult, [X[0], cw], [X[0]])
                        S.tt("dve", X[3][:], X[3][:], X[2][:], ALU.add, [X[3], X[2]], [X[3]])
                        S.tt("pool", X[1][:], X[1][:], X[0][:], ALU.add, [X[1], X[0]], [X[1]])
                        S.tt("dve", X[3][:], X[3][:], X[1][:], ALU.add, [X[3], X[1]], [X[3]])
                        S.tt("pool", X[3][:], X[3][:], cbs[:], ALU.add, [X[3], cbs], [X[3]])
                        S.act(a[:], X[3][:], AF.Silu, [X[3]], [a])
                        S.dma("sp", xcs[l][rows(i), cs_], a[:], sb_r=a)
            S.barrier()

        def phase_gla(l, sample):
            tiles = [NPT] if sample else list(range(NPT))
            K_TRI = K_TRIS if sample else K_TRIP
            K_NTRI = K_NTRIS if sample else K_NTRIP
            K_SEL = K_SELS if sample else K_SELP
            nseg = NSEG if sample else 1
            with ExitStack() as pes:
                pin = S.sb(pes, "gpin", [128, 6160], F32, dma=True)
                wga = S.sb(pes, "wga", [33, 1024], F32, dma=True)
                gln = S.sb(pes, "gln", [128, 512], F32, dma=True)
                glrT = S.sb(pes, "glrT", [33, 128], F32)
                l1 = S.sb(pes, "gl1", [128, 1024], F32)
                btok = S.sb(pes, "gbtok", [128, 1024], F32)
                kkb = S.sb(pes, "gkkb", [128, 1024], BF16)
                eb = S.sb(pes, "geb", [128, 8, 128], F32)
                enb = S.sb(pes, "genb", [128, 8, 128], F32)
                qTb = S.sb(pes, "gqTb", [128, 8, 128], BF16)
                kTb = S.sb(pes, "gkTb", [128, 8, 128], BF16)
                attb = S.sb(pes, "gattb", [128, 4, 128], BF16)
                vb = S.sb(pes, "gvb", [128, 2048], BF16)
                junk = S.sb(pes, "gjunk", [128, 512], F32)
                ss = S.sb(pes, "gss", [128, 8], F32)
                oa = S.sb(pes, "goa", [128, 2048], F32)
                oTb = S.sb(pes, "goTb", [128, 16, 128], BF16, dma=True)
                Sb = S.sb(pes, "gSb", [128, 8, 512], BF16)
                if sample:
                    Sst = [S.sb(pes, "gS%d" % i, [128, 8, 512], F32, dma=True) for i in range(2)]
                    qm = [S.sb(pes, "gqm%d" % i, [128, 8, 128], BF16) for i in range(2)]
                    kkm = [S.sb(pes, "gkkm%d" % i, [128, 1024], BF16) for i in range(2)]
                    for b_ in qm:
                        S.memset("pool", b_[:], 0.0, [b_])
                else:
                    Sst = [S.sb(pes, "gS", [128, 8, 512], F32, dma=True)]
                    S.memset("pool", Sst[0][:], 0.0, [Sst[0]])
                    S.memset("pool", Sb[:], 0.0, [Sb])
                P = [S.ps(pes, "gP%d" % i, [128, 512], F32) for i in range(8)]
                rot = [0]

                def nb():
                    b_ = P[rot[0] % 4]
                    rot[0] += 1
                    return b_
                PO = P[4:8]
                S.memset("pool", wga[:], 0.0, [wga])
                S.dma("sp", wga[0:16, :], w_gate[l], sb_w=wga)
                S.dma("sp", wga[32:33, :], b_gate[l:l + 1, :], sb_w=wga)
                S.dma("sp", gln[:], gla_norm[l:l + 1, :].partition_broadcast(128), sb_w=gln)
                S.memset("pool", glrT[:], 0.0, [glrT])
                S.memset("pool", glrT[32:33, :], 1.0, [glrT])
                tri = cst[:, K_TRI, :]
                ntri = cst[:, K_NTRI, :]
                selI = cst[:, K_SEL, :]
                for i in tiles:
                    S.dma("sp", pin[:], proj[l][rows(i), 0:6160], sb_w=pin)
                    q_ = lambda c: pin[:, C_Q + c * 128:C_Q + (c + 1) * 128]
                    k_ = lambda c: pin[:, C_K + c * 128:C_K + (c + 1) * 128]
                    p = nb()
                    S.tr(p[0:16, 0:128], pin[:, C_GLR:C_GLR + 16], ident, [pin, cst], [p])
                    S.cp("dve", glrT[0:16, :], p[0:16, 0:128], [p], [glrT])
                    for hf in range(2):
                        p = nb()
                        S.mm(p[:], glrT[:, :], wga[:, hf * 512:(hf + 1) * 512], True, True, [glrT, wga], [p])
                        S.act(l1[:, hf * 512:(hf + 1) * 512], p[:], AF.Exp, [p], [l1], scale=-1.0)
                    S.act(l1[:], l1[:], AF.Ln, [l1], [l1], bias=1.0)
                    for hf in range(2):
                        p = nb()
                        S.mm(p[:], ntri, l1[:, hf * 512:(hf + 1) * 512], True, True, [cst, l1], [p])
                        S.cp("act", btok[:, hf * 512:(hf + 1) * 512], p[:], [p], [btok])
                    for hf in range(2):
                        p = nb()
                        for c in range(4):
                            cc_ = hf * 4 + c
                            S.mm(p[:, c * 128:(c + 1) * 128], l1[:, cc_ * 128:(cc_ + 1) * 128], ntri, True, True, [l1, cst], [p])
                        pv = p[:].rearrange("p (c t) -> p c t", c=4)
                        S.act(eb[:, hf * 4:(hf + 1) * 4, :], pv, AF.Exp, [p], [eb])
                        S.act(enb[:, hf * 4:(hf + 1) * 4, :], pv, AF.Exp, [p], [enb], scale=-1.0)
                    for hf in range(2):
                        p = nb()
                        S.mm(p[:], selI, btok[:, hf * 512:(hf + 1) * 512], True, True, [cst, btok], [p])
                        S.act(l1[:, hf * 512:(hf + 1) * 512], p[:], AF.Exp, [p], [l1])
                    S.tt("dve", kkb[:], pin[:, C_K:C_K + 1024], l1[:], ALU.mult, [pin, l1], [kkb])
                    for hf in range(2):
                        p = nb()
                        for c in range(4):
                            S.tr(p[:, c * 128:(c + 1) * 128], q_(hf * 4 + c), ident, [pin, cst], [p])
                        S.stt("dve", qTb[:, hf * 4:(hf + 1) * 4, :], p[:].rearrange("p (c t) -> p c t", c=4), 0.0625,
                              eb[:, hf * 4:(hf + 1) * 4, :], ALU.mult, ALU.mult, [p, eb], [qTb])
                        p = nb()
                        for c in range(4):
                            S.tr(p[:, c * 128:(c + 1) * 128], k_(hf * 4 + c), ident, [pin, cst], [p])
                        S.tt("dve", kTb[:, hf * 4:(hf + 1) * 4, :], p[:].rearrange("p (c t) -> p c t", c=4),
                             enb[:, hf * 4:(hf + 1) * 4, :], ALU.mult, [p, enb], [kTb])
                    p = nb()
                    for h in range(4):
                        for dk in range(2):
                            S.mm(p[:, h * 128:(h + 1) * 128], kTb[:, h * 2 + dk, :], qTb[:, h * 2 + dk, :], dk == 0, dk == 1, [kTb, qTb], [p])
                    S.tt("dve", attb[:], p[:].rearrange("p (h t) -> p h t", h=4), tri.unsqueeze(1).to_broadcast([128, 4, 128]),
                         ALU.mult, [p, cst], [attb])
                    S.cp("pool", vb[:], pin[:, C_V:C_V + 2048], [pin], [vb])
                    for h in range(4):
                        S.mm(PO[h][:], attb[:, h, :], vb[:, h * 512:(h + 1) * 512], True, False, [attb, vb], [PO[h]])
                    for j in range(nseg):
                        if sample:
                            Sj = Sst[j % 2]
                            S.dma("sp", Sj[:].rearrange("p (h t) v -> p h t v", h=4),
                                  st_gla[l, j].rearrange("h (t p) v -> p h t v", p=128), sb_w=Sj)
                            S.cp("pool", Sb[:], Sj[:], [Sj], [Sb])
                            qmj = qm[j % 2]
                            if j >= 2:
                                jo = j - 2
                                S.memset("pool", qmj[:, :, jo * SL:(jo + 1) * SL], 0.0, [qmj])
                            S.cp("pool", qmj[:, :, j * SL:(j + 1) * SL], qTb[:, :, j * SL:(j + 1) * SL], [qTb], [qmj])
                            kkj = kkm[j % 2]
                            S.ts("dve", kkj[:], kkb[:], cst[:, K_SEGCOL, j:j + 1], None, ALU.mult, None, [kkb, cst], [kkj])
                            ql, kl = qmj, kkj
                        else:
                            Sj = Sst[0]
                            ql, kl = qTb, kkb
                        last = (j == nseg - 1)
                        for h in range(4):
                            for dk in range(2):
                                S.mm(PO[h][:], ql[:, h * 2 + dk, :], Sb[:, h * 2 + dk, :], False, last and dk == 1, [ql, Sb], [PO[h]])
                        tl = (j * SL + SL - 1) if sample else 127
                        for c in range(8):
                            h = c // 2
                            p = nb()
                            S.mm(p[:], kl[:, c * 128:(c + 1) * 128], vb[:, h * 512:(h + 1) * 512], True, True, [kl, vb], [p])
                            S.stt("dve", Sj[:, c, :], Sj[:, c, :], eb[:, c, tl:tl + 1], p[:], ALU.mult, ALU.add, [Sj, eb, p], [Sj])
                        if sample:
                            S.dma("sp", gla_s[l, j].rearrange("h (t p) v -> p h t v", p=128),
                                  Sj[:].rearrange("p (h t) v -> p h t v", h=4), sb_r=Sj)
                        else:
                            S.cp("pool", Sb[:], Sj[:], [Sj], [Sb])
                            if i == NPT - 1:
                                S.dma("sp", gla_p[l].rearrange("h (t p) v -> p h t v", p=128),
                                      Sj[:].rearrange("p (h t) v -> p h t v", h=4), sb_r=Sj)
                    S.memset("pool", ss[:], 0.0, [ss])
                    for h in range(4):
                        S.act(junk[:], PO[h][:], AF.Square, [PO[h], ss], [junk, ss], accum=ss[:, h:h + 1])
                    S.ts("dve", ss[:, 4:8], ss[:, 0:4], 1.0 / 512, EPS, ALU.mult, ALU.add, [ss], [ss])
                    S.act(ss[:, 4:8], ss[:, 4:8], AF.Sqrt, [ss], [ss])
                    S.recip(ss[:, 4:8], ss[:, 4:8], [ss], [ss])
                    S.act(pin[:, C_R:C_R + 2048], pin[:, C_R:C_R + 2048], AF.Silu, [pin], [pin])
                    S.tt("pool", pin[:, C_R:C_R + 2048].rearrange("p (h v) -> p h v", h=4),
                         pin[:, C_R:C_R + 2048].rearrange("p (h v) -> p h v", h=4),
                         gln[:].unsqueeze(1).to_broadcast([128, 4, 512]), ALU.mult, [pin, gln], [pin])
                    for h in range(4):
                        S.stt("dve", oa[:, h * 512:(h + 1) * 512], PO[h][:], ss[:, 4 + h:5 + h],
                              pin[:, C_R + h * 512:C_R + (h + 1) * 512], ALU.mult, ALU.mult, [PO[h], ss, pin], [oa])
                    for q4 in range(4):
                        p = nb()
                        for c in range(4):
                            cc_ = q4 * 4 + c
                            S.tr(p[:, c * 128:(c + 1) * 128], oa[:, cc_ * 128:(cc_ + 1) * 128], ident, [oa, cst], [p])
                        S.cp("act", oTb[:, q4 * 4:(q4 + 1) * 4, :], p[:].rearrange("p (c t) -> p c t", c=4), [p], [oTb])
                    S.dma("sp", oaT[l][i], oTb[:], sb_r=oTb)
            S.barrier()

        def phase_ssd(l, sample):
            tiles = [NPT] if sample else list(range(NPT))
            K_TRI = K_TRIS if sample else K_TRIP
            K_SEL = K_SELS if sample else K_SELP
            K_NEGM = K_NEGS if sample else K_NEGP
            nseg = NSEG if sample else 1
            with ExitStack() as pes:
                xc = S.sb(pes, "dxc", [128, 6144], F32, dma=True)
                zt = S.sb(pes, "dz", [128, 4096], F32, dma=True)
                dtt = S.sb(pes, "ddt", [128, 64], F32, dma=True)
                dtb = S.sb(pes, "ddtb", [128, 64], F32, dma=True)
                aneg = S.sb(pes, "daneg", [128, 64], F32, dma=True)
                dsk = S.sb(pes, "ddsk", [128, 64], F32, dma=True)
                ssn = S.sb(pes, "dssn", [128, 4096], F32, dma=True)
                negm8 = S.sb(pes, "dnegm8", [128, 4, 128], F32)
                sm = S.sb(pes, "dsm", [128, 10, 64], F32)
                xsb = S.sb(pes, "dxsb", [128, 4096], BF16)
                Bb = S.sb(pes, "dBb", [128, 1024], BF16)
                BTb = S.sb(pes, "dBTb", [128, 8, 128], BF16)
                CTb = S.sb(pes, "dCTb", [128, 8, 128], BF16)
                xdl = S.sb(pes, "dxdl", [128, 4096], BF16)
                nd_ = 1 if sample else 2
                diag = [S.sb(pes, "ddiag%d" % i, [128, 8, 128], F32) for i in range(nd_)]
                seg = [S.sb(pes, "dseg%d" % i, [128, 8, 128], F32) for i in range(nd_)]
                WTb = [S.sb(pes, "dWTb%d" % i, [128, 8, 128], BF16) for i in range(nd_)]
                yy = S.sb(pes, "dy", [128, 4096], F32, dma=True)
                tmp = S.sb(pes, "dtmp", [128, 512], F32)
                junk = S.sb(pes, "djunk", [128, 512], F32)
                ss = S.sb(pes, "dss", [128, 16], F32)
                yTb = S.sb(pes, "dyTb", [128, 32, 128], BF16, dma=True)
                hT = S.sb(pes, "dhT", [128, 4096], F32)
                hTb = S.sb(pes, "dhTb", [128, 4096], BF16)
                if sample:
                    hj = S.sb(pes, "dhj", [128, 32, 128], F32, dma=True)
                    cm = [S.sb(pes, "dcm%d" % i, [128, 8, 128], BF16) for i in range(2)]
                    Bm = [S.sb(pes, "dBm%d" % i, [128, 1024], BF16) for i in range(2)]
                    for b_ in cm:
                        S.memset("pool", b_[:], 0.0, [b_])
                else:
                    hj = yy
                    S.memset("pool", hT[:], 0.0, [hT])
                    S.memset("pool", hTb[:], 0.0, [hTb])
                P = [S.ps(pes, "dP%d" % i, [128, 512], F32) for i in range(8)]
                rot = [0]

                def nb():
                    b_ = P[rot[0] % 8]
                    rot[0] += 1
                    return b_
                DTP, LND, AA, CS, CS2, ECS, DL, DEC, T1, T2 = range(10)
                S.dma("sp", dtb[:], dt_bias[l:l + 1, :].partition_broadcast(128), sb_w=dtb)
                S.dma("sp", aneg[:], A_log[l:l + 1, :].partition_broadcast(128), sb_w=aneg)
                S.dma("sp", dsk[:], d_skip[l:l + 1, :].partition_broadcast(128), sb_w=dsk)
                S.dma("sp", ssn[:], ssd_norm[l:l + 1, :].partition_broadcast(128), sb_w=ssn)
                S.act(aneg[:], aneg[:], AF.Exp, [aneg], [aneg])
                S.ts("dve", aneg[:], aneg[:], -1.0, None, ALU.mult, None, [aneg], [aneg])
                for c in range(4):
                    S.cp("dve", negm8[:, c, :], cst[:, K_NEGM, :], [cst], [negm8])
                tri = cst[:, K_TRI, :]
                selI = cst[:, K_SEL, :]
                ones = cst[:, K_ONES, :]

                def load_h(j):
                    S.dma("sp", hj[:], st_ssm[l, j].rearrange("h p n -> (h p) n").rearrange("(c q) n -> q c n", q=128), sb_w=hj)
                    for q4 in range(8):
                        p = nb()
                        for c in range(4):
                            S.tr(p[:, c * 128:(c + 1) * 128], hj[:, q4 * 4 + c, :], ident, [hj, cst], [p])
                        S.cp("act" if q4 % 2 else "dve", hT[:, q4 * 512:(q4 + 1) * 512], p[:], [p], [hT])
                    S.cp("pool", hTb[:], hT[:], [hT], [hTb])

                def store_h(dst):
                    for q4 in range(8):
                        p = nb()
                        for c in range(4):
                            cc_ = q4 * 4 + c
                            S.tr(p[:, c * 128:(c + 1) * 128], hT[:, cc_ * 128:(cc_ + 1) * 128], ident, [hT, cst], [p])
                        S.cp("act" if q4 % 2 else "dve", hj[:, q4 * 4:(q4 + 1) * 4, :] if sample else
                             hj[:, q4 * 512:(q4 + 1) * 512].rearrange("p (c n) -> p c n", c=4),
                             p[:].rearrange("p (c n) -> p c n", c=4), [p], [hj])
                    src = hj[:] if sample else hj[:].rearrange("p (c n) -> p c n", c=32)
                    S.dma("sp", dst.rearrange("h p n -> (h p) n").rearrange("(c q) n -> q c n", q=128), src, sb_r=hj)

                for i in tiles:
                    S.dma("sp", xc[:], xcs[l][rows(i), :], sb_w=xc)
                    S.dma("sp", zt[:], proj[l][rows(i), C_Z:C_Z + 4096], sb_w=zt)
                    S.dma("sp", dtt[:], proj[l][rows(i), C_DT:C_DT + 64], sb_w=dtt)
                    S.tt("dve", sm[:, T1, :], dtt[:], dtb[:], ALU.add, [dtt, dtb], [sm])
                    S.act(sm[:, T1, :], sm[:, T1, :], AF.Exp, [sm], [sm])
                    S.act(sm[:, DTP, :], sm[:, T1, :], AF.Ln, [sm], [sm], bias=1.0)
                    S.act(sm[:, LND, :], sm[:, DTP, :], AF.Ln, [sm], [sm])
                    S.tt("dve", sm[:, AA, :], sm[:, DTP, :], aneg[:], ALU.mult, [sm, aneg], [sm])
                    p = nb()
                    S.mm(p[:, 0:64], tri, sm[:, AA, :], True, True, [cst, sm], [p])
                    S.cp("dve", sm[:, CS, :], p[:, 0:64], [p], [sm])
                    S.tt("dve", sm[:, CS2, :], sm[:, CS, :], sm[:, LND, :], ALU.subtract, [sm], [sm])
                    S.act(sm[:, ECS, :], sm[:, CS, :], AF.Exp, [sm], [sm])
                    p = nb()
                    S.mm(p[:, 0:64], selI, sm[:, CS, :], True, True, [cst, sm], [p])
                    S.act(sm[:, DL, :], p[:, 0:64], AF.Exp, [p], [sm])
                    S.tt("dve", sm[:, DL, :], sm[:, DL, :], sm[:, DTP, :], ALU.mult, [sm], [sm])
                    S.cp("pool", xsb[:], xc[:, 0:4096], [xc], [xsb])
                    S.cp("act", Bb[:], xc[:, 4096:5120], [xc], [Bb])
                    for (src0, dstb) in ((4096, BTb), (5120, CTb)):
                        for hf in range(2):
                            p = nb()
                            for c in range(4):
                                g = hf * 4 + c
                                S.tr(p[:, c * 128:(c + 1) * 128], xc[:, src0 + g * 128:src0 + (g + 1) * 128], ident, [xc, cst], [p])
                            S.cp("act" if hf else "dve", dstb[:, hf * 4:(hf + 1) * 4, :], p[:].rearrange("p (c t) -> p c t", c=4), [p], [dstb])
                    S.tt("dve", xdl[:].rearrange("p (h d) -> p h d", h=64), xc[:, 0:4096].rearrange("p (h d) -> p h d", h=64),
                         sm[:, DL, :].unsqueeze(2).to_broadcast([128, 64, 64]), ALU.mult, [xc, sm], [xdl])
                    if sample:
                        S.memset("pool", yy[:], 0.0, [yy])
                    for j in range(nseg):
                        if sample:
                            load_h(j)
                            cmj = cm[j % 2]
                            if j >= 2:
                                jo = j - 2
                                S.memset("pool", cmj[:, :, jo * SL:(jo + 1) * SL], 0.0, [cmj])
                            S.cp("pool", cmj[:, :, j * SL:(j + 1) * SL], CTb[:, :, j * SL:(j + 1) * SL], [CTb], [cmj])
                            Bmj = Bm[j % 2]
                            S.ts("dve", Bmj[:], Bb[:], cst[:, K_SEGCOL, j:j + 1], None, ALU.mult, None, [Bb, cst], [Bmj])
                            for g in range(8):
                                p = nb()
                                S.mm(p[:], cmj[:, g, :], hTb[:, g * 512:(g + 1) * 512], True, True, [cmj, hTb], [p])
                                S.tt("dve", yy[:, g * 512:(g + 1) * 512], yy[:, g * 512:(g + 1) * 512], p[:], ALU.add, [yy, p], [yy])
                            Bl = Bmj
                            krow = K_ROW0 + j
                        else:
                            Bl = Bb
                            krow = K_ROWP
                        if sample:
                            p = nb()
                            S.mm(p[:, 0:64], cst[:, krow, :], sm[:, CS, :], True, True, [cst, sm], [p])
                            S.act(sm[:, DEC, :], p[:, 0:64], AF.Exp, [p], [sm])
                            for g in range(8):
                                p = nb()
                                S.mm(p[:], Bl[:, g * 128:(g + 1) * 128], xdl[:, g * 512:(g + 1) * 512], True, True, [Bl, xdl], [p])
                                hv = hT[:, g * 512:(g + 1) * 512].rearrange("p (h d) -> p h d", h=8)
                                S.tt("dve", hv, hv, sm[:, DEC, g * 8:(g + 1) * 8].unsqueeze(2).to_broadcast([128, 8, 64]), ALU.mult, [hT, sm], [hT])
                                S.tt("dve", hT[:, g * 512:(g + 1) * 512], hT[:, g * 512:(g + 1) * 512], p[:], ALU.add, [hT, p], [hT])
                            store_h(ssm_s[l, j])
                    for g in range(8):
                        pc = nb()
                        S.mm(pc[:, 0:128], BTb[:, g, :], CTb[:, g, :], True, True, [BTb, CTb], [pc])
                        dg = diag[g % nd_]
                        S.tt("pool", dg[:], ident.unsqueeze(1).to_broadcast([128, 8, 128]),
                             sm[:, CS, g * 8:(g + 1) * 8].unsqueeze(2).to_broadcast([128, 8, 128]), ALU.mult, [cst, sm], [dg])
                        sg = seg[g % nd_]
                        for hf in range(2):
                            p = nb()
                            S.mm(p[:], ones, dg[:, hf * 4:(hf + 1) * 4, :], True, False, [cst, dg], [p])
                            S.mm(p[:], ident, negm8[:], False, True, [cst, negm8], [p])
                            S.tt("dve", sg[:, hf * 4:(hf + 1) * 4, :], p[:].rearrange("p (h t) -> p h t", h=4),
                                 sm[:, CS2, g * 8 + hf * 4:g * 8 + hf * 4 + 4].unsqueeze(2).to_broadcast([128, 4, 128]), ALU.subtract, [p, sm], [sg])
                        S.act(sg[:], sg[:], AF.Exp, [sg], [sg])
                        wt = WTb[g % nd_]
                        S.tt("dve", wt[:], sg[:], pc[:, 0:128].unsqueeze(1).to_broadcast([128, 8, 128]), ALU.mult, [sg, pc], [wt])
                        py = nb()
                        for h in range(8):
                            hh = g * 8 + h
                            S.mm(py[:, h * 64:(h + 1) * 64], wt[:, h, :], xsb[:, hh * 64:(hh + 1) * 64], True, True, [wt, xsb], [py])
                        ecsb = sm[:, ECS, g * 8:(g + 1) * 8].unsqueeze(2).to_broadcast([128, 8, 64])
                        yg = yy[:, g * 512:(g + 1) * 512]
                        if sample:
                            S.tt("pool", yg.rearrange("p (h d) -> p h d", h=8), yg.rearrange("p (h d) -> p h d", h=8), ecsb, ALU.mult, [yy, sm], [yy])
                            S.tt("dve", yg, yg, py[:], ALU.add, [yy, py], [yy])
                        else:
                            pz = nb()
                            S.mm(pz[:], CTb[:, g, :], hTb[:, g * 512:(g + 1) * 512], True, True, [CTb, hTb], [pz])
                            S.tt("dve", tmp[:].rearrange("p (h d) -> p h d", h=8), pz[:].rearrange("p (h d) -> p h d", h=8), ecsb, ALU.mult, [pz, sm], [tmp])
                            S.tt("dve", yg, tmp[:], py[:], ALU.add, [tmp, py], [yy])
                    if not sample:
                        p = nb()
                        S.mm(p[:, 0:64], cst[:, K_ROWP, :], sm[:, CS, :], True, True, [cst, sm], [p])
                        S.act(sm[:, DEC, :], p[:, 0:64], AF.Exp, [p], [sm])
                        for g in range(8):
                            p = nb()
                            S.mm(p[:], Bb[:, g * 128:(g + 1) * 128], xdl[:, g * 512:(g + 1) * 512], True, True, [Bb, xdl], [p])
                            hv = hT[:, g * 512:(g + 1) * 512].rearrange("p (h d) -> p h d", h=8)
                            S.tt("pool", hv, hv, sm[:, DEC, g * 8:(g + 1) * 8].unsqueeze(2).to_broadcast([128, 8, 64]), ALU.mult, [hT, sm], [hT])
                            S.tt("dve", hT[:, g * 512:(g + 1) * 512], hT[:, g * 512:(g + 1) * 512], p[:], ALU.add, [hT, p], [hT])
                        S.cp("pool", hTb[:], hT[:], [hT], [hTb])
                    xv_ = xc[:, 0:4096].rearrange("p (h d) -> p h d", h=64)
                    S.tt("pool", xv_, xv_, dsk[:].unsqueeze(2).to_broadcast([128, 64, 64]), ALU.mult, [xc, dsk], [xc])
                    S.tt("pool", yy[:], yy[:], xc[:, 0:4096], ALU.add, [yy, xc], [yy])
                    S.act(zt[:], zt[:], AF.Silu, [zt], [zt])
                    S.tt("dve", yy[:], yy[:], zt[:], ALU.mult, [yy, zt], [yy])
                    S.memset("pool", ss[:], 0.0, [ss])
                    for g in range(8):
                        S.act(junk[:], yy[:, g * 512:(g + 1) * 512], AF.Square, [yy, ss], [junk, ss], accum=ss[:, g:g + 1])
                    S.ts("dve", ss[:, 8:16], ss[:, 0:8], 1.0 / 512, EPS, ALU.mult, ALU.add, [ss], [ss])
                    S.act(ss[:, 8:16], ss[:, 8:16], AF.Sqrt, [ss], [ss])
                    S.recip(ss[:, 8:16], ss[:, 8:16], [ss], [ss])
                    for g in range(8):
                        S.stt("dve", yy[:, g * 512:(g + 1) * 512], yy[:, g * 512:(g + 1) * 512], ss[:, 8 + g:9 + g],
                              ssn[:, g * 512:(g + 1) * 512], ALU.mult, ALU.mult, [yy, ss, ssn], [yy])
                    for q4 in range(8):
                        p = nb()
                        for c in range(4):
                            cc_ = q4 * 4 + c
                            S.tr(p[:, c * 128:(c + 1) * 128], yy[:, cc_ * 128:(cc_ + 1) * 128], ident, [yy, cst], [p])
                        S.cp("act" if q4 % 2 else "dve", yTb[:, q4 * 4:(q4 + 1) * 4, :], p[:].rearrange("p (c t) -> p c t", c=4), [p], [yTb])
                    S.dma("sp", ybT[l][i], yTb[:], sb_r=yTb)
                    if (not sample) and i == NPT - 1:
                        store_h(ssm_p[l])
            S.barrier()

        def phase_o1(l, mT):
            with ExitStack() as pes:
                wg = [S.sb(pes, "owg%d" % i, [128, 16, 512], BF16, dma=True) for i in range(1)]
                ws = [S.sb(pes, "ows%d" % i, [128, 32, 512], BF16, dma=True) for i in range(1)]
                oa = [S.sb(pes, "ooa%d" % i, [128, 16, 128], BF16, dma=True) for i in range(2)]
                yb = [S.sb(pes, "oyb%d" % i, [128, 32, 128], BF16, dma=True) for i in range(2)]
                gA = [S.sb(pes, "ogA%d" % i, [128, 512], F32, dma=True) for i in range(2)]
                gB = [S.sb(pes, "ogB%d" % i, [128, 512], F32, dma=True) for i in range(2)]
                mg = [S.sb(pes, "omg%d" % i, [128, 512], F32) for i in range(2)]
                mgb = [S.sb(pes, "omgb%d" % i, [128, 512], BF16) for i in range(2)]
                pa = [S.ps(pes, "opa%d" % i, [128, 512], F32) for i in range(2)]
                pb = [S.ps(pes, "opb%d" % i, [128, 512], F32) for i in range(2)]
                pt = [S.ps(pes, "opt%d" % i, [128, 4, 128], BF16) for i in range(2)]
                wgv = w_gproj[l].rearrange("(kt p) n -> p kt n", p=128)
                wsv = w_sproj[l].rearrange("(kt p) n -> p kt n", p=128)
                n = 0
                for cb in range(4):
                    cs_ = slice(cb * 512, (cb + 1) * 512)
                    Wg = wg[0]
                    Ws = ws[0]
                    S.dma("pool", Wg[:], wgv[:, :, cs_], sb_w=Wg)
                    S.dma("pool", Ws[:], wsv[:, :, cs_], sb_w=Ws)
                    for i in range(NT):
                        b2 = n % 2
                        n += 1
                        S.dma("sp", oa[b2][:], oaT[l][i], sb_w=oa[b2])
                        S.dma("sp", yb[b2][:], ybT[l][i], sb_w=yb[b2])
                        S.dma("sp", gA[b2][:], proj[l][rows(i), C_G + cb * 512:C_G + (cb + 1) * 512], sb_w=gA[b2])
                        S.dma("sp", gB[b2][:], proj[l][rows(i), C_G + 2048 + cb * 512:C_G + 2048 + (cb + 1) * 512], sb_w=gB[b2])
                        for k in range(16):
                            S.mm(pa[b2][:], oa[b2][:, k, :], Wg[:, k, :], k == 0, k == 15, [oa[b2], Wg], [pa[b2]])
                        for k in range(32):
                            S.mm(pb[b2][:], yb[b2][:, k, :], Ws[:, k, :], k == 0, k == 31, [yb[b2], Ws], [pb[b2]])
                        S.act(gA[b2][:], gA[b2][:], AF.Sigmoid, [gA[b2]], [gA[b2]])
                        S.act(gB[b2][:], gB[b2][:], AF.Sigmoid, [gB[b2]], [gB[b2]])
                        S.tt("dve", mg[b2][:], pa[b2][:], gA[b2][:], ALU.mult, [pa[b2], gA[b2]], [mg[b2]])
                        S.tt("dve", gB[b2][:], pb[b2][:], gB[b2][:], ALU.mult, [pb[b2], gB[b2]], [gB[b2]])
                        S.tt("pool", mgb[b2][:], mg[b2][:], gB[b2][:], ALU.add, [mg[b2], gB[b2]], [mgb[b2]])
                        for c in range(4):
                            S.tr(pt[b2][:, c, :], mgb[b2][:, c * 128:(c + 1) * 128], identb[:], [mgb[b2], identb], [pt[b2]])
                        S.cp("act", mT[i // 4][:, cb * 4:(cb + 1) * 4, (i % 4) * 128:(i % 4 + 1) * 128], pt[b2][:], [pt[b2]], [mT[i // 4]])
            S.barrier()

        def phase_res(l, mT, wsrc, nk, gate_i, xsrc, xdst, actsrc=None):
            with ExitStack() as pes:
                wm = [S.sb(pes, "rwm%d" % i, [128, nk, 512], BF16, dma=True) for i in range(2)]
                xb = [S.sb(pes, "rxb%d" % i, [128, 512], F32, dma=True) for i in range(2)]
                gb = [S.sb(pes, "rgb%d" % i, [128, 512], F32, dma=True) for i in range(2)]
                pp = [S.ps(pes, "rpp%d" % i, [128, 512], F32) for i in range(2)]
                if actsrc is not None:
                    ab = [S.sb(pes, "rab%d" % i, [128, nk, 128], BF16, dma=True) for i in range(2)]
                wv = wsrc.rearrange("(kt p) n -> p kt n", p=128)
                n = 0
                for cb in range(4):
                    cs_ = slice(cb * 512, (cb + 1) * 512)
                    W = wm[cb % 2]
                    S.dma("pool", W[:], wv[:, :, cs_], sb_w=W)
                    for i in range(NT):
                        b2 = n % 2
                        n += 1
                        g = grp(i)
                        S.dma("sp", xb[b2][:], xsrc[rows(i), cs_], sb_w=xb[b2])
                        S.dma("sp", gb[b2][:], mod[l][g, :, gate_i * D + cb * 512:gate_i * D + (cb + 1) * 512], sb_w=gb[b2])
                        if actsrc is not None:
                            S.dma("sp", ab[b2][:], actsrc.rearrange("j p t -> p j t")[:, :, rows(i)], sb_w=ab[b2])
                            for k in range(nk):
                                S.mm(pp[b2][:], ab[b2][:, k, :], W[:, k, :], k == 0, k == nk - 1, [ab[b2], W], [pp[b2]])
                        else:
                            for k in range(nk):
                                S.mm(pp[b2][:], hview(mT, i, k), W[:, k, :], k == 0, k == nk - 1, [mT[i // 4], W], [pp[b2]])
                        S.tt("dve", gb[b2][:], pp[b2][:], gb[b2][:], ALU.mult, [pp[b2], gb[b2]], [gb[b2]])
                        S.tt("pool", xb[b2][:], xb[b2][:], gb[b2][:], ALU.add, [xb[b2], gb[b2]], [xb[b2]])
                        S.dma("sp", xdst[rows(i), cs_], xb[b2][:], sb_r=xb[b2])
            S.barrier()

        def phase_f1(l, hT):
            with ExitStack() as pes:
                wgt = [S.sb(pes, "fwg%d" % i, [128, 16, 512], BF16, dma=True) for i in range(2)]
                wup = [S.sb(pes, "fwu%d" % i, [128, 16, 512], BF16, dma=True) for i in range(2)]
                pg = [S.ps(pes, "fpg%d" % i, [128, 512], F32) for i in range(2)]
                pu = [S.ps(pes, "fpu%d" % i, [128, 512], F32) for i in range(2)]
                sg = [S.sb(pes, "fsg%d" % i, [128, 512], F32) for i in range(2)]
                at = [S.sb(pes, "fat%d" % i, [128, 512], BF16, dma=True) for i in range(2)]
                wv = w_fin[l].rearrange("(kt p) n -> p kt n", p=128)
                n = 0
                for jb in range(11):
                    Wg = wgt[jb % 2]
                    Wu = wup[jb % 2]
                    S.dma("pool", Wg[:], wv[:, :, jb * 512:(jb + 1) * 512], sb_w=Wg)
                    S.dma("pool", Wu[:], wv[:, :, DFF + jb * 512:DFF + (jb + 1) * 512], sb_w=Wu)
                    for jj in range(4):
                        j = jb * 4 + jj
                        for tg in range(5):
                            N = 512 if tg < 4 else 128
                            b2 = n % 2
                            n += 1
                            for k in range(16):
                                S.mm(pg[b2][:, 0:N], Wg[:, k, jj * 128:(jj + 1) * 128], hT[tg][:, k, :], k == 0, k == 15, [Wg, hT[tg]], [pg[b2]])
                            for k in range(16):
                                S.mm(pu[b2][:, 0:N], Wu[:, k, jj * 128:(jj + 1) * 128], hT[tg][:, k, :], k == 0, k == 15, [Wu, hT[tg]], [pu[b2]])
                            S.act(sg[b2][:, 0:N], pg[b2][:, 0:N], AF.Silu, [pg[b2]], [sg[b2]])
                            S.tt("dve", at[b2][:, 0:N], sg[b2][:, 0:N], pu[b2][:, 0:N], ALU.mult, [sg[b2], pu[b2]], [at[b2]])
                            S.dma("sp", actT[l][j, :, tg * 512:tg * 512 + N], at[b2][:, 0:N], sb_r=at[b2])
            S.barrier()

        def phase_final(xsrc):
            with ExitStack() as pes:
                xt = [S.sb(pes, "zx%d" % i, [128, D], F32, dma=True) for i in range(2)]
                fn = S.sb(pes, "zfn", [128, D], F32, dma=True)
                junk = S.sb(pes, "zjunk", [128, D], F32)
                ss = S.sb(pes, "zss", [128, 2], F32)
                S.dma("sp", fn[:], fnorm.partition_broadcast(128), sb_w=fn)
                for i in range(NT):
                    x = xt[i % 2]
                    S.dma("sp", x[:], xsrc[rows(i), :], sb_w=x)
                    S.memset("pool", ss[:], 0.0, [ss])
                    S.act(junk[:], x[:], AF.Square, [x, ss], [junk, ss], accum=ss[:, 0:1])
                    S.ts("dve", ss[:, 1:2], ss[:, 0:1], 1.0 / D, EPS, ALU.mult, ALU.add, [ss], [ss])
                    S.act(ss[:, 1:2], ss[:, 1:2], AF.Sqrt, [ss], [ss])
                    S.recip(ss[:, 1:2], ss[:, 1:2], [ss], [ss])
                    S.stt("dve", x[:], x[:], ss[:, 1:2], fn[:], ALU.mult, ALU.mult, [x, ss, fn], [x])
                    S.dma("sp", y_out[rows(i), :], x[:], sb_r=x)
            S.barrier()

        phase_mod()
        xcur = xin
        for l in range(DEPTH):
            with ExitStack() as ges:
                hT = [S.sb(ges, "hT%d_%d" % (l, g), [128, 16, 512 if g < 4 else 128], BF16) for g in range(5)]
                phase_norm(xcur, l, 0, 1, hT)
                phase_proj(l, hT)
            phase_conv(l)
            phase_gla(l, False)
            phase_gla(l, True)
            phase_ssd(l, False)
            phase_ssd(l, True)
            with ExitStack() as ges:
                mT = [S.sb(ges, "mT%d_%d" % (l, g), [128, 16, 512 if g < 4 else 128], BF16) for g in range(5)]
                phase_o1(l, mT)
                phase_res(l, mT, w_mix[l], 16, 2, xcur, xv[l][0])
            with ExitStack() as ges:
                hT = [S.sb(ges, "h2T%d_%d" % (l, g), [128, 16, 512 if g < 4 else 128], BF16) for g in range(5)]
                phase_norm(xv[l][0], l, 3, 4, hT)
                phase_f1(l, hT)
            phase_res(l, None, w_fout[l], 44, 5, xv[l][0], xv[l][1], actsrc=actT[l])
            xcur = xv[l][1]
        phase_final(xcur)
        S.emit()
        print("ops:", S.nops, {e: len(S.prog[e]) for e in S.prog})
    return nc


_NC_CACHE = {}


def kernel(x_prompt, x_sample, c_prompt, c_sample, state_gla, state_ssm, state_conv,
           w_ada, b_ada, w_in, w_gla_gate, b_gla_gate, gla_norm, w_gla_proj, conv_w, conv_b,
           dt_bias, A_log, d_skip, ssd_norm, w_ssd_proj, w_mix_out, w_ffn_in, w_ffn_out, final_norm):
    f = lambda a: np.ascontiguousarray(np.asarray(a, dtype=np.float32))
    x_prompt, x_sample, c_prompt, c_sample = f(x_prompt), f(x_sample), f(c_prompt), f(c_sample)
    state_gla, state_ssm, state_conv = f(state_gla), f(state_ssm), f(state_conv)
    if "nc" not in _NC_CACHE:
        _NC_CACHE["nc"] = build_nc()
    nc = _NC_CACHE["nc"]
    consts = make_consts()
    shared = {
        "consts": consts, "w_ada": f(w_ada), "b_ada": f(b_ada), "w_in": f(w_in), "w_gla_gate": f(w_gla_gate),
        "b_gla_gate": f(b_gla_gate), "gla_norm": f(gla_norm), "w_gla_proj": f(w_gla_proj), "conv_w": f(conv_w),
        "conv_b": f(conv_b), "dt_bias": f(dt_bias), "A_log": f(A_log), "d_skip": f(d_skip), "ssd_norm": f(ssd_norm),
        "w_ssd_proj": f(w_ssd_proj), "w_mix_out": f(w_mix_out), "w_ffn_in": f(w_ffn_in), "w_ffn_out": f(w_ffn_out),
        "final_norm": f(final_norm).reshape(1, D),
    }
    in_maps = []
    for c in range(8):
        ps = c % 4
        s0 = c * NSEG
        xin = np.concatenate([x_prompt[ps], x_sample[s0:s0 + NSEG].reshape(NSEG * SL, D)], axis=0)
        cc = np.stack([np.broadcast_to(c_prompt[ps][None, :], (128, D)),
                       np.repeat(c_sample[s0:s0 + NSEG], SL, axis=0)], axis=0)
        m = dict(shared)
        m["xin"] = np.ascontiguousarray(xin)
        m["cc"] = np.ascontiguousarray(cc)
        m["st_gla"] = np.ascontiguousarray(state_gla[:, s0:s0 + NSEG])
        m["st_ssm"] = np.ascontiguousarray(state_ssm[:, s0:s0 + NSEG])
        m["st_conv"] = np.ascontiguousarray(state_conv[:, s0:s0 + NSEG])
        in_maps.append(m)
    res = run_bass_kernel_spmd(nc, in_maps, core_ids=list(range(8)))
    R = res.results
    y_prompt = np.stack([R[c]["y"][:NPT * 128] for c in range(4)], axis=0)
    y_sample = np.concatenate([R[c]["y"][NPT * 128:].reshape(NSEG, SL, D) for c in range(8)], axis=0)
    gla_p = np.stack([R[c]["gla_p"] for c in range(4)], axis=1)
    ssm_p = np.stack([R[c]["ssm_p"] for c in range(4)], axis=1)
    conv_p = np.stack([R[c]["conv_p"] for c in range(4)], axis=1)
    gla_s = np.concatenate([R[c]["gla_s"] for c in range(8)], axis=1)
    ssm_s = np.concatenate([R[c]["ssm_s"] for c in range(8)], axis=1)
    conv_s = np.concatenate([R[c]["conv_s"] for c in range(8)], axis=1)
    return (y_prompt.astype(np.float32), y_sample.astype(np.float32), gla_p.astype(np.float32), ssm_p.astype(np.float32),
            conv_p.astype(np.float32), gla_s.astype(np.float32), ssm_s.astype(np.float32), conv_s.astype(np.float32))
```

```python
import numpy as np
import concourse.bass as bass
import concourse.mybir as mybir
from concourse.bass_utils import run_bass_kernel_spmd
from contextlib import ExitStack

F32 = mybir.dt.float32
BF16 = mybir.dt.bfloat16
AF = mybir.ActivationFunctionType
ALU = mybir.AluOpType
AX = mybir.AxisListType

D = 2048
NPT = 8
NT = NPT + 1
NG = NPT // 4
TOK = NT * 128
NSEG = 16
SL = 8
DEPTH = 2
DFF = 5632
NIN = 20560
EPS = 1e-6
C_Q, C_K, C_V, C_R, C_GLR, C_Z, C_XBC, C_DT, C_G = 0, 1024, 2048, 4096, 6144, 6160, 10256, 16400, 16464
NEG = -30000.0


class DSem:
    def __init__(self, sem, name):
        self.sem = sem
        self.cnt = 0
        self.name = name


class Buf:
    def __init__(self, name, t, dsem=None):
        self.name = name
        self.t = t
        self.wr = {}
        self.rd = {}
        self.dsem = dsem

    def __getitem__(self, k):
        return self.t[k]


class Sched:
    ENGS = ("pe", "act", "dve", "pool", "sp")

    def __init__(self, nc, es, ndsem=40):
        self.nc = nc
        self.sem = {e: es.enter_context(nc.semaphore("s_" + e)) for e in ("pe", "act", "dve", "pool")}
        self.cnt = {e: 0 for e in self.ENGS}
        self.seen = {e: {} for e in self.ENGS}
        self.prog = {e: [] for e in self.ENGS}
        self.dram = {}
        self.dpool = [DSem(es.enter_context(nc.semaphore("d%d" % i)), "d%d" % i) for i in range(ndsem)]
        self.dfree = list(self.dpool)
        self.nops = 0
        self.cc_sem = es.enter_context(nc.semaphore("s_cc"))
        self.cc_cnt = 0
        self.phase = "init"
        self.namemap = None

    def sb(self, pes, name, shape, dtype=F32, dma=False):
        self.nops += 1
        name = "%s_u%d" % (name, self.nops)
        t = pes.enter_context(self.nc.sbuf_tensor(name, list(shape), dtype))
        ds = None
        if dma:
            ds = self.dfree.pop()
            pes.callback(self.dfree.append, ds)
        return Buf(name, t, ds)

    def ps(self, pes, name, shape, dtype=F32):
        self.nops += 1
        name = "%s_u%d" % (name, self.nops)
        t = pes.enter_context(self.nc.psum_tensor(name, list(shape), dtype))
        return Buf(name, t)

    def dbuf(self, pes, name):
        ds = self.dfree.pop()
        pes.callback(self.dfree.append, ds)
        return Buf(name, None, ds)

    @staticmethod
    def _add(need, d, skip=None, pe=False):
        for k, (sem, val) in d.items():
            if pe and k == "pe":
                continue
            if skip is not None and k == skip:
                continue
            if k not in need or need[k][1] < val:
                need[k] = (sem, val)

    def _waits(self, eng, need):
        waits = []
        seen = self.seen[eng]
        for k, (sem, val) in need.items():
            if seen.get(k, 0) >= val:
                continue
            seen[k] = val
            waits.append((sem, val))
        return waits

    def op(self, eng, fn, reads=(), writes=()):
        need = {}
        pe = eng == "pe"
        for b in reads:
            self._add(need, b.wr, pe=pe)
        for b in writes:
            self._add(need, b.wr, pe=pe)
            self._add(need, b.rd, skip=eng, pe=pe)
        waits = self._waits(eng, need)
        self.cnt[eng] += 1
        ev = (self.sem[eng], self.cnt[eng])
        self.prog[eng].append((waits, fn, ev[0], 1, self.phase))
        for b in writes:
            b.wr = {eng: ev}
            b.rd = {}
        for b in reads:
            if b not in writes:
                b.rd[eng] = ev
        self.nops += 1

    def dma(self, q, out, in_, sb_w=None, sb_r=None, after=(), produces=None, evbuf=None):
        need = {}
        if sb_r is not None:
            self._add(need, sb_r.wr)
        if sb_w is not None:
            self._add(need, sb_w.wr)
            self._add(need, sb_w.rd)
        for key in after:
            for (k, sem, val) in self.dram.get(key, ()):
                if k not in need or need[k][1] < val:
                    need[k] = (sem, val)
        waits = self._waits(q, need)
        eb = evbuf if evbuf is not None else (sb_w if sb_w is not None else sb_r)
        ds = eb.dsem
        ds.cnt += 1
        ev = (ds.sem, 16 * ds.cnt)
        k = ds.name
        self.prog[q].append((waits, (lambda e, o=out, i=in_: e.dma_start(out=o, in_=i)), ds.sem, 16, self.phase))
        if sb_w is not None:
            sb_w.wr = {k: ev}
            sb_w.rd = {}
        if sb_r is not None:
            sb_r.rd[k] = ev
        if produces is not None:
            self.dram.setdefault(produces, []).append((k, ev[0], ev[1]))
        self.nops += 1

    def barrier(self):
        need = {}
        for e in ("pe", "act", "dve", "pool"):
            if self.cnt[e] > 0:
                need[e] = (self.sem[e], self.cnt[e])
        for ds in self.dpool:
            if ds.cnt > 0:
                need[ds.name] = (ds.sem, 16 * ds.cnt)
        for e in self.ENGS:
            waits = self._waits(e, dict(need))
            if waits:
                self.prog[e].append((waits, None, None, 0, self.phase))

    def collective(self, snd, rcv):
        self.barrier()
        self.cc_cnt += 1
        groups = [[0, 1], [2, 3], [4, 5], [6, 7]]
        self.prog["pool"].append(([], (lambda e: e.collective_compute("AllReduce", ALU.add, replica_groups=groups,
                                                                      ins=[snd.ap().opt()], outs=[rcv.ap().opt()])),
                                  self.cc_sem, 1, self.phase))
        for e in self.ENGS:
            self.prog[e].append(([(self.cc_sem, self.cc_cnt)], None, None, 0, self.phase))

    def emit(self):
        nc = self.nc
        prog = self.prog
        with nc.Block() as block:
            def run(eng_obj, items):
                for (waits, fn, sem, inc, ph) in items:
                    for (ws, wv) in waits:
                        eng_obj.wait_ge(ws, wv)
                    if fn is not None:
                        ins = fn(eng_obj)
                        ins.then_inc(sem, inc)
                        if self.namemap is not None:
                            self.namemap[ins.ins.name] = ph

            @block.tensor
            def _(e):
                run(e, prog["pe"])

            @block.scalar
            def _(e):
                run(e, prog["act"])

            @block.vector
            def _(e):
                run(e, prog["dve"])

            @block.gpsimd
            def _(e):
                run(e, prog["pool"])

            @block.sync
            def _(e):
                run(e, prog["sp"])

    def mm(self, out, lhsT, rhs, start, stop, reads, writes):
        self.op("pe", lambda e: e.matmul(out, lhsT=lhsT, rhs=rhs, start=start, stop=stop), reads, writes)

    def tr(self, out, in_, ident, reads, writes):
        self.op("pe", lambda e: e.transpose(out=out, in_=in_, identity=ident), reads, writes)

    def act(self, out, in_, func, reads, writes, bias=0.0, scale=1.0, accum=None):
        if accum is None:
            self.op("act", lambda e: e.activation(out=out, in_=in_, func=func, bias=bias, scale=scale), reads, writes)
        else:
            self.op("act", lambda e: e.activation(out=out, in_=in_, func=func, bias=bias, scale=scale, accum_out=accum), reads, writes)

    def tt(self, eng, out, in0, in1, op, reads, writes):
        self.op(eng, lambda e: e.tensor_tensor(out=out, in0=in0, in1=in1, op=op), reads, writes)

    def ts(self, eng, out, in0, s1, s2, op0, op1, reads, writes):
        if s2 is None:
            self.op(eng, lambda e: e.tensor_scalar(out=out, in0=in0, scalar1=s1, scalar2=None, op0=op0), reads, writes)
        else:
            self.op(eng, lambda e: e.tensor_scalar(out=out, in0=in0, scalar1=s1, scalar2=s2, op0=op0, op1=op1), reads, writes)

    def stt(self, eng, out, in0, scalar, in1, op0, op1, reads, writes):
        self.op(eng, lambda e: e.scalar_tensor_tensor(out=out, in0=in0, scalar=scalar, in1=in1, op0=op0, op1=op1), reads, writes)

    def cp(self, eng, out, in_, reads, writes):
        if eng == "act":
            self.op("act", lambda e: e.copy(out=out, in_=in_), reads, writes)
        else:
            self.op(eng, lambda e: e.tensor_copy(out=out, in_=in_), reads, writes)

    def memset(self, eng, ap, val, writes):
        self.op(eng, lambda e: e.memset(ap, val), (), writes)

    def recip(self, out, in_, reads, writes):
        self.op("dve", lambda e: e.reciprocal(out=out, in_=in_), reads, writes)


K_ID, K_TRIP, K_TRIS, K_SELP, K_SELS, K_ONES, K_NEGP, K_NEGS, K_ROW0, K_ROWP, K_SEGCOL, K_NTRIP, K_NTRIS = 0, 1, 2, 3, 4, 5, 6, 7, 8, 24, 25, 26, 27
NCONST = 28


def make_consts():
    c = np.zeros((NCONST, 128, 128), np.float32)
    idx = np.arange(128)
    s = idx[:, None]
    t = idx[None, :]
    c[K_ID] = (s == t)
    c[K_TRIP] = (s <= t)
    same = (s // SL) == (t // SL)
    c[K_TRIS] = same & (s <= t)
    c[K_SELP] = (s == 127).astype(np.float32) - (s == t)
    last = (t // SL) * SL + SL - 1
    c[K_SELS] = (s == last).astype(np.float32) - (s == t)
    c[K_ONES] = 1.0
    c[K_NEGP] = np.where(s <= t, 0.0, NEG)
    c[K_NEGS] = np.where(same & (s <= t), 0.0, NEG)
    for j in range(NSEG):
        c[K_ROW0 + j] = (s == (SL * j + SL - 1)) * np.ones((1, 128))
    c[K_ROWP] = (s == 127) * np.ones((1, 128))
    c[K_SEGCOL][:, :NSEG] = ((idx[:, None] // SL) == np.arange(NSEG)[None, :])
    c[K_NTRIP] = -c[K_TRIP] / 16.0
    c[K_NTRIS] = -c[K_TRIS] / 16.0
    return np.ascontiguousarray(c.transpose(1, 0, 2))


def w_blocks():
    bl = []
    for (c0, n) in ((0, 6144), (C_GLR, 16), (C_Z, 4096), (C_XBC, 6144), (C_DT, 64), (C_G, 4096)):
        o = 0
        while o < n:
            w = min(512, n - o)
            bl.append((c0 + o, w))
            o += w
    return bl


def build_nc():
    nc = bass.Bass("TRN2", target_bir_lowering=False)

    def din(name, shape, dt=F32):
        return nc.dram_tensor(name, list(shape), dt, kind="ExternalInput").ap()

    def dout(name, shape, dt=F32):
        return nc.dram_tensor(name, list(shape), dt, kind="ExternalOutput").ap()

    def dint(name, shape, dt=F32):
        return nc.dram_tensor(name, list(shape), dt, kind="Internal").ap()

    xin = din("xin", [TOK, D])
    cc = din("cc", [2, 128, D])
    consts = din("consts", [128, NCONST, 128])
    flags = din("flags", [128, 2])
    st_gla = din("st_gla", [DEPTH, NSEG, 4, 256, 512])
    st_ssm = din("st_ssm", [DEPTH, NSEG, 64, 64, 128])
    st_conv = din("st_conv", [DEPTH, NSEG, 3, 6144])
    w_ada = din("w_ada", [DEPTH, D, 6 * D])
    b_ada = din("b_ada", [DEPTH, 6 * D])
    w_in = din("w_in", [DEPTH, D, NIN])
    w_gate = din("w_gla_gate", [DEPTH, 16, 1024])
    b_gate = din("b_gla_gate", [DEPTH, 1024])
    gla_norm = din("gla_norm", [DEPTH, 512])
    w_gproj = din("w_gla_proj", [DEPTH, D, D])
    conv_w = din("conv_w", [DEPTH, 4, 6144])
    conv_b = din("conv_b", [DEPTH, 6144])
    dt_bias = din("dt_bias", [DEPTH, 64])
    A_log = din("A_log", [DEPTH, 64])
    d_skip = din("d_skip", [DEPTH, 64])
    ssd_norm = din("ssd_norm", [DEPTH, 4096])
    w_sproj = din("w_ssd_proj", [DEPTH, 4096, D])
    w_mix = din("w_mix_out", [DEPTH, D, D])
    w_fin = din("w_ffn_in", [DEPTH, D, 2 * DFF])
    w_fout = din("w_ffn_out", [DEPTH, DFF, D])
    fnorm = din("final_norm", [1, D])

    y_out = dout("y", [TOK, D])
    gla_p = dout("gla_p", [DEPTH, 4, 256, 512])
    ssm_p = dout("ssm_p", [DEPTH, 64, 64, 128])
    conv_p = dout("conv_p", [DEPTH, 3, 6144])
    gla_s = dout("gla_s", [DEPTH, NSEG, 4, 256, 512])
    ssm_s = dout("ssm_s", [DEPTH, NSEG, 64, 64, 128])
    conv_s = dout("conv_s", [DEPTH, NSEG, 3, 6144])

    proj = [dint("proj%d" % l, [TOK, NIN]) for l in range(DEPTH)]
    xbp = [dint("xbp%d" % l, [3 + NPT * 128, 6144]) for l in range(DEPTH)]
    xbs = [dint("xbs%d" % l, [NSEG, 3 + SL, 6144]) for l in range(DEPTH)]
    xcs = [dint("xc%d" % l, [TOK, 6144]) for l in range(DEPTH)]
    mod = [dint("mod%d" % l, [2, 128, 6 * D]) for l in range(DEPTH)]
    xv = [[dint("xv%d_%d" % (l, v), [TOK, D]) for v in range(2)] for l in range(DEPTH)]
    oaT = [dint("oaT%d" % l, [NT, 128, 16, 128], BF16) for l in range(DEPTH)]
    ybT = [dint("ybT%d" % l, [NT, 128, 32, 128], BF16) for l in range(DEPTH)]
    actT = [dint("actT%d" % l, [44, 128, TOK], BF16) for l in range(DEPTH)]
    o_loc = [dint("oloc%d" % l, [NPT * 128, 2048]) for l in range(DEPTH)]
    qg_d = [dint("qg%d" % l, [NPT, 128, 8, 128], BF16) for l in range(DEPTH)]
    y_loc = [dint("yloc%d" % l, [NPT * 128, 4096]) for l in range(DEPTH)]
    ct_d = [dint("ct%d" % l, [NPT, 128, 8, 128], BF16) for l in range(DEPTH)]
    eg_d = [dint("eg%d" % l, [NPT, 128, 64]) for l in range(DEPTH)]
    sloc_d = [dint("sloc%d" % l, [128, 4096]) for l in range(DEPTH)]
    gam_d = [dint("gam%d" % l, [128, 8]) for l in range(DEPTH)]
    hloc_d = [dint("hloc%d" % l, [128, 4096]) for l in range(DEPTH)]
    gamh_d = [dint("gamh%d" % l, [128, 64]) for l in range(DEPTH)]
    cv_snd = [nc.dram_tensor("cvsnd%d" % l, [128, 144], F32) for l in range(DEPTH)]
    cv_rcv = [nc.dram_tensor("cvrcv%d" % l, [128, 144], F32) for l in range(DEPTH)]
    gs_snd = [nc.dram_tensor("gssnd%d" % l, [128, 4096], F32) for l in range(DEPTH)]
    gs_rcv = [nc.dram_tensor("gsrcv%d" % l, [128, 4096], F32) for l in range(DEPTH)]
    hs_snd = [nc.dram_tensor("hssnd%d" % l, [128, 4096], F32) for l in range(DEPTH)]
    hs_rcv = [nc.dram_tensor("hsrcv%d" % l, [128, 4096], F32) for l in range(DEPTH)]

    with ExitStack() as es:
        S = Sched(nc, es)
        cst = S.sb(es, "cst", [128, NCONST, 128], F32, dma=True)
        identb = S.sb(es, "identb", [128, 128], BF16)
        S.dma("sp", cst[:], consts, sb_w=cst)
        flg = S.sb(es, "flg", [128, 2], F32, dma=True)
        S.dma("sp", flg[:], flags, sb_w=flg)
        S.cp("dve", identb[:], cst[:, K_ID, :], [cst], [identb])
        ident = cst[:, K_ID, :]

        def rows(i):
            return slice(i * 128, (i + 1) * 128)

        def grp(i):
            return 0 if i < NPT else 1

        def hgi(i):
            return (i // 4, (i % 4) * 128) if i < NPT else (NG, 0)

        def hview(hT, i, k):
            g_, o_ = hgi(i)
            return hT[g_][:, k, o_:o_ + 128]

        def phase_mod():
            with ExitStack() as pes:
                cct = S.sb(pes, "cct", [128, D], F32, dma=True)
                scb = S.sb(pes, "scb", [128, D], BF16)
                scT = [S.sb(pes, "scT%d" % g, [128, 16, 128], BF16) for g in range(2)]
                ptr = [S.ps(pes, "mptr%d" % i, [128, 8, 128], BF16) for i in range(2)]
                pm = [S.ps(pes, "pm%d" % i, [128, 512], F32) for i in range(4)]
                for g in range(2):
                    S.dma("sp", cct[:], cc[g], sb_w=cct)
                    S.act(scb[:], cct[:], AF.Silu, [cct], [scb])
                    for half in range(2):
                        p = ptr[half]
                        for k in range(8):
                            kk = half * 8 + k
                            S.tr(p[:, k, :], scb[:, kk * 128:(kk + 1) * 128], identb[:], [scb, identb], [p])
                        S.cp("dve" if half == 0 else "act", scT[g][:, half * 8:(half + 1) * 8, :], p[:], [p], [scT[g]])
                wa = [S.sb(pes, "wa%d" % i, [128, 16, 512], BF16, dma=True) for i in range(2)]
                bb = [S.sb(pes, "bb%d" % i, [128, 512], F32, dma=True) for i in range(2)]
                stg = [S.sb(pes, "mst%d" % i, [128, 512], F32, dma=True) for i in range(4)]
                it = 0
                for l in range(DEPTH):
                    wv = w_ada[l].rearrange("(kt p) n -> p kt n", p=128)
                    for cb in range(24):
                        cs_ = slice(cb * 512, (cb + 1) * 512)
                        w = wa[it % 2]
                        b = bb[it % 2]
                        S.dma("pool", w[:], wv[:, :, cs_], sb_w=w)
                        S.dma("sp", b[:], b_ada[l:l + 1, cs_].partition_broadcast(128), sb_w=b)
                        for g in range(2):
                            n = it * 2 + g
                            p = pm[n % 4]
                            s_ = stg[n % 4]
                            for k in range(16):
                                S.mm(p[:], scT[g][:, k, :], w[:, k, :], k == 0, k == 15, [scT[g], w], [p])
                            S.tt("dve", s_[:], p[:], b[:], ALU.add, [p, b], [s_])
                            S.dma("sp", mod[l][g, :, cs_], s_[:], sb_r=s_)
                        it += 1
            S.barrier()

        def phase_norm(xsrc, l, sh_i, sc_i, hT):
            with ExitStack() as pes:
                xt = [S.sb(pes, "nx%d" % i, [128, D], F32, dma=True) for i in range(2)]
                shb = S.sb(pes, "shb", [128, D], F32, dma=True)
                scb = S.sb(pes, "nscb", [128, D], F32, dma=True)
                junk = S.sb(pes, "njunk", [128, D], F32)
                ss = S.sb(pes, "nss", [128, 2], F32)
                hb = [S.sb(pes, "hb%d" % i, [128, D], BF16) for i in range(2)]
                ptr = [S.ps(pes, "nptr%d" % i, [128, 8, 128], BF16) for i in range(2)]
                n = 0
                for i in range(NT):
                    g = grp(i)
                    if i == 0 or i == NPT:
                        S.dma("sp", shb[:], mod[l][g, :, sh_i * D:(sh_i + 1) * D], sb_w=shb)
                        S.dma("sp", scb[:], mod[l][g, :, sc_i * D:(sc_i + 1) * D], sb_w=scb)
                        S.ts("dve", scb[:], scb[:], 1.0, None, ALU.add, None, [scb], [scb])
                    x = xt[i % 2]
                    S.dma("sp", x[:], xsrc[rows(i), :], sb_w=x)
                    S.memset("pool", ss[:], 0.0, [ss])
                    S.act(junk[:], x[:], AF.Square, [x, ss], [junk, ss], accum=ss[:, 0:1])
                    S.ts("dve", ss[:, 1:2], ss[:, 0:1], 1.0 / D, EPS, ALU.mult, ALU.add, [ss], [ss])
                    S.act(ss[:, 1:2], ss[:, 1:2], AF.Sqrt, [ss], [ss])
                    S.recip(ss[:, 1:2], ss[:, 1:2], [ss], [ss])
                    S.stt("dve", junk[:], x[:], ss[:, 1:2], scb[:], ALU.mult, ALU.mult, [x, ss, scb], [junk])
                    h = hb[i % 2]
                    S.tt("pool", h[:], junk[:], shb[:], ALU.add, [junk, shb], [h])
                    for half in range(2):
                        p = ptr[n % 2]
                        n += 1
                        for k in range(8):
                            kk = half * 8 + k
                            S.tr(p[:, k, :], h[:, kk * 128:(kk + 1) * 128], identb[:], [h, identb], [p])
                        g_, o_ = hgi(i)
                        S.cp("act" if half == 0 else "dve", hT[g_][:, half * 8:(half + 1) * 8, o_:o_ + 128],
                             p[:], [p], [hT[g_]])
            S.barrier()

        def phase_proj(l, hT):
            with ExitStack() as pes:
                wb = [S.sb(pes, "wb%d" % i, [128, 16, 512], BF16, dma=True) for i in range(2)]
                pp = [S.ps(pes, "pp%d" % i, [128, 512], F32) for i in range(4)]
                stg = [S.sb(pes, "pst%d" % i, [128, 512], F32, dma=True) for i in range(4)]
                dd = S.dbuf(pes, "dd_conv")
                S.dma("sp", xbs[l][:, 0:3, :], st_conv[l], evbuf=dd)
                wv = w_in[l].rearrange("(kt p) n -> p kt n", p=128)
                n = 0
                for bi, (c0, w) in enumerate(w_blocks()):
                    W = wb[bi % 2]
                    S.dma("pool", W[:, :, 0:w], wv[:, :, c0:c0 + w], sb_w=W)
                    isx = C_XBC <= c0 < C_DT
                    for i in range(NT):
                        p = pp[n % 4]
                        s_ = stg[n % 4]
                        n += 1
                        for k in range(16):
                            S.mm(p[:, 0:w], hview(hT, i, k), W[:, k, 0:w], k == 0, k == 15, [hT[hgi(i)[0]], W], [p])
                        S.cp("act" if n % 2 else "dve", s_[:, 0:w], p[:, 0:w], [p], [s_])
                        if isx:
                            xc0 = c0 - C_XBC
                            if i < NPT:
                                S.dma("sp", xbp[l][3 + i * 128:3 + (i + 1) * 128, xc0:xc0 + w], s_[:, 0:w], sb_r=s_)
                            else:
                                S.dma("sp", xbs[l][:, 3:3 + SL, xc0:xc0 + w], s_[:, 0:w], sb_r=s_)
                        else:
                            S.dma("sp", proj[l][rows(i), c0:c0 + w], s_[:, 0:w], sb_r=s_)
            S.barrier()
            fl3 = lambda ap: ap.rearrange("r c -> (r c)").rearrange("(p f) -> p f", p=128)
            with ExitStack() as pes:
                zt = S.sb(pes, "zt", [128, 144], F32, dma=True)
                S.dma("sp", zt[:], fl3(xbp[l][NPT * 128:NPT * 128 + 3, :]), sb_w=zt)
                S.ts("dve", zt[:], zt[:], flg[:, 1:2], None, ALU.mult, None, [zt, flg], [zt])
                S.dma("sp", cv_snd[l].ap(), zt[:], sb_r=zt)
            S.collective(cv_snd[l], cv_rcv[l])
            with ExitStack() as pes:
                zt = S.sb(pes, "zt2", [128, 144], F32, dma=True)
                S.dma("sp", zt[:], cv_rcv[l].ap(), sb_w=zt)
                S.ts("dve", zt[:], zt[:], flg[:, 0:1], None, ALU.mult, None, [zt, flg], [zt])
                S.dma("sp", fl3(xbp[l][0:3, :]), zt[:], sb_r=zt)
                dd = S.dbuf(pes, "dd_conv2")
                S.dma("sp", conv_p[l], xbp[l][NPT * 128:NPT * 128 + 3, :], evbuf=dd)
                S.dma("sp", conv_s[l], xbs[l][:, SL:SL + 3, :], evbuf=dd)

        def phase_conv(l):
            CH = 1536
            with ExitStack() as pes:
                cw = S.sb(pes, "cw", [128, 4, CH], F32, dma=True)
                cbs = S.sb(pes, "cbs", [128, CH], F32, dma=True)
                xs_ = [[S.sb(pes, "cx%d_%d" % (b, i), [128, CH], F32, dma=True) for i in range(4)] for b in range(2)]
                acc = [S.sb(pes, "cacc%d" % b, [128, CH], F32, dma=True) for b in range(2)]
                for c in range(4):
                    cs_ = slice(c * CH, (c + 1) * CH)
                    for t4 in range(4):
                        S.dma("sp", cw[:, t4, :], conv_w[l, t4:t4 + 1, cs_].partition_broadcast(128), sb_w=cw)
                    S.dma("sp", cbs[:], conv_b[l:l + 1, cs_].partition_broadcast(128), sb_w=cbs)
                    for i in range(NT):
                        X = xs_[i % 2]
                        a = acc[i % 2]
                        for t4 in range(4):
                            if i < NPT:
                                src = xbp[l][i * 128 + t4:i * 128 + t4 + 128, cs_]
                            else:
                                src = xbs[l][:, t4:t4 + SL, cs_]
                            S.dma("sp", X[t4][:], src, sb_w=X[t4])
                        S.tt("dve", X[3][:], X[3][:], cw[:, 3, :], ALU.mult, [X[3], cw], [X[3]])
                        S.tt("pool", X[2][:], X[2][:], cw[:, 2, :], ALU.mult, [X[2], cw], [X[2]])
                        S.tt("dve", X[1][:], X[1][:], cw[:, 1, :], ALU.mult, [X[1], cw], [X[1]])
                        S.tt("pool", X[0][:], X[0][:], cw[:, 0, :], ALU.mult, [X[0], cw], [X[0]])
                        S.tt("dve", X[3][:], X[3][:], X[2][:], ALU.add, [X[3], X[2]], [X[3]])
                        S.tt("pool", X[1][:], X[1][:], X[0][:], ALU.add, [X[1], X[0]], [X[1]])
                        S.tt("dve", X[3][:], X[3][:], X[1][:], ALU.add, [X[3], X[1]], [X[3]])
                        S.tt("pool", X[3][:], X[3][:], cbs[:], ALU.add, [X[3], cbs], [X[3]])
                        S.act(a[:], X[3][:], AF.Silu, [X[3]], [a])
                        S.dma("sp", xcs[l][rows(i), cs_], a[:], sb_r=a)
            S.barrier()

        def gla_post(l, i, osrc, obufs, rap, rbuf, gln, junk, ss, oa, oTb, nb):
            S.memset("pool", ss[:], 0.0, [ss])
            for h in range(4):
                S.act(junk[:], osrc[h], AF.Square, [obufs[h], ss], [junk, ss], accum=ss[:, h:h + 1])
            S.ts("dve", ss[:, 4:8], ss[:, 0:4], 1.0 / 512, EPS, ALU.mult, ALU.add, [ss], [ss])
            S.act(ss[:, 4:8], ss[:, 4:8], AF.Sqrt, [ss], [ss])
            S.recip(ss[:, 4:8], ss[:, 4:8], [ss], [ss])
            S.act(rap, rap, AF.Silu, [rbuf], [rbuf])
            S.tt("pool", rap.rearrange("p (h v) -> p h v", h=4), rap.rearrange("p (h v) -> p h v", h=4),
                 gln[:].unsqueeze(1).to_broadcast([128, 4, 512]), ALU.mult, [rbuf, gln], [rbuf])
            for h in range(4):
                S.stt("dve", oa[:, h * 512:(h + 1) * 512], osrc[h], ss[:, 4 + h:5 + h],
                      rap[:, h * 512:(h + 1) * 512], ALU.mult, ALU.mult, [obufs[h], ss, rbuf], [oa])
            for q4 in range(4):
                p = nb()
                for c in range(4):
                    cc_ = q4 * 4 + c
                    S.tr(p[:, c * 128:(c + 1) * 128], oa[:, cc_ * 128:(cc_ + 1) * 128], ident, [oa, cst], [p])
                S.cp("act", oTb[:, q4 * 4:(q4 + 1) * 4, :], p[:].rearrange("p (c t) -> p c t", c=4), [p], [oTb])
            S.dma("sp", oaT[l][i], oTb[:], sb_r=oTb)

        def phase_gla(l, sample):
            tiles = [NPT] if sample else list(range(NPT))
            K_TRI = K_TRIS if sample else K_TRIP
            K_NTRI = K_NTRIS if sample else K_NTRIP
            K_SEL = K_SELS if sample else K_SELP
            nseg = NSEG if sample else 1
            with ExitStack() as pes:
                pin = S.sb(pes, "gpin", [128, 6160], F32, dma=True)
                wga = S.sb(pes, "wga", [33, 1024], F32, dma=True)
                gln = S.sb(pes, "gln", [128, 512], F32, dma=True)
                glrT = S.sb(pes, "glrT", [33, 128], F32)
                l1 = S.sb(pes, "gl1", [128, 1024], F32)
                btok = S.sb(pes, "gbtok", [128, 1024], F32)
                kkb = S.sb(pes, "gkkb", [128, 1024], BF16)
                eb = S.sb(pes, "geb", [128, 8, 128], F32)
                enb = S.sb(pes, "genb", [128, 8, 128], F32)
                qTb = S.sb(pes, "gqTb", [128, 8, 128], BF16)
                kTb = S.sb(pes, "gkTb", [128, 8, 128], BF16)
                attb = S.sb(pes, "gattb", [128, 4, 128], BF16)
                vb = S.sb(pes, "gvb", [128, 2048], BF16)
                junk = S.sb(pes, "gjunk", [128, 512], F32)
                ss = S.sb(pes, "gss", [128, 8], F32)
                oa = S.sb(pes, "goa", [128, 2048], F32, dma=True)
                oTb = S.sb(pes, "goTb", [128, 16, 128], BF16, dma=True)
                if not sample:
                    qgb = S.sb(pes, "gqgb", [128, 8, 128], BF16, dma=True)
                    gam = S.sb(pes, "ggam", [128, 8], F32, dma=True)
                    S.memset("pool", gam[:], 1.0, [gam])
                Sb = S.sb(pes, "gSb", [128, 8, 512], BF16)
                if sample:
                    Sst = [S.sb(pes, "gS%d" % i, [128, 8, 512], F32, dma=True) for i in range(2)]
                    qm = [S.sb(pes, "gqm%d" % i, [128, 8, 128], BF16) for i in range(2)]
                    kkm = [S.sb(pes, "gkkm%d" % i, [128, 1024], BF16) for i in range(2)]
                    for b_ in qm:
                        S.memset("pool", b_[:], 0.0, [b_])
                else:
                    Sst = [S.sb(pes, "gS", [128, 8, 512], F32, dma=True)]
                    S.memset("pool", Sst[0][:], 0.0, [Sst[0]])
                    S.memset("pool", Sb[:], 0.0, [Sb])
                P = [S.ps(pes, "gP%d" % i, [128, 512], F32) for i in range(8)]
                rot = [0]

                def nb():
                    b_ = P[rot[0] % 4]
                    rot[0] += 1
                    return b_
                PO = P[4:8]
                S.memset("pool", wga[:], 0.0, [wga])
                S.dma("sp", wga[0:16, :], w_gate[l], sb_w=wga)
                S.dma("sp", wga[32:33, :], b_gate[l:l + 1, :], sb_w=wga)
                S.dma("sp", gln[:], gla_norm[l:l + 1, :].partition_broadcast(128), sb_w=gln)
                S.memset("pool", glrT[:], 0.0, [glrT])
                S.memset("pool", glrT[32:33, :], 1.0, [glrT])
                tri = cst[:, K_TRI, :]
                ntri = cst[:, K_NTRI, :]
                selI = cst[:, K_SEL, :]
                for i in tiles:
                    S.dma("sp", pin[:], proj[l][rows(i), 0:6160], sb_w=pin)
                    q_ = lambda c: pin[:, C_Q + c * 128:C_Q + (c + 1) * 128]
                    k_ = lambda c: pin[:, C_K + c * 128:C_K + (c + 1) * 128]
                    p = nb()
                    S.tr(p[0:16, 0:128], pin[:, C_GLR:C_GLR + 16], ident, [pin, cst], [p])
                    S.cp("dve", glrT[0:16, :], p[0:16, 0:128], [p], [glrT])
                    for hf in range(2):
                        p = nb()
                        S.mm(p[:], glrT[:, :], wga[:, hf * 512:(hf + 1) * 512], True, True, [glrT, wga], [p])
                        S.act(l1[:, hf * 512:(hf + 1) * 512], p[:], AF.Exp, [p], [l1], scale=-1.0)
                    S.act(l1[:], l1[:], AF.Ln, [l1], [l1], bias=1.0)
                    for hf in range(2):
                        p = nb()
                        S.mm(p[:], ntri, l1[:, hf * 512:(hf + 1) * 512], True, True, [cst, l1], [p])
                        S.cp("act", btok[:, hf * 512:(hf + 1) * 512], p[:], [p], [btok])
                    for hf in range(2):
                        p = nb()
                        for c in range(4):
                            cc_ = hf * 4 + c
                            S.mm(p[:, c * 128:(c + 1) * 128], l1[:, cc_ * 128:(cc_ + 1) * 128], ntri, True, True, [l1, cst], [p])
                        pv = p[:].rearrange("p (c t) -> p c t", c=4)
                        S.act(eb[:, hf * 4:(hf + 1) * 4, :], pv, AF.Exp, [p], [eb])
                        S.act(enb[:, hf * 4:(hf + 1) * 4, :], pv, AF.Exp, [p], [enb], scale=-1.0)
                    for hf in range(2):
                        p = nb()
                        S.mm(p[:], selI, btok[:, hf * 512:(hf + 1) * 512], True, True, [cst, btok], [p])
                        S.act(l1[:, hf * 512:(hf + 1) * 512], p[:], AF.Exp, [p], [l1])
                    S.tt("dve", kkb[:], pin[:, C_K:C_K + 1024], l1[:], ALU.mult, [pin, l1], [kkb])
                    for hf in range(2):
                        p = nb()
                        for c in range(4):
                            S.tr(p[:, c * 128:(c + 1) * 128], q_(hf * 4 + c), ident, [pin, cst], [p])
                        S.stt("dve", qTb[:, hf * 4:(hf + 1) * 4, :], p[:].rearrange("p (c t) -> p c t", c=4), 0.0625,
                              eb[:, hf * 4:(hf + 1) * 4, :], ALU.mult, ALU.mult, [p, eb], [qTb])
                        p = nb()
                        for c in range(4):
                            S.tr(p[:, c * 128:(c + 1) * 128], k_(hf * 4 + c), ident, [pin, cst], [p])
                        S.tt("dve", kTb[:, hf * 4:(hf + 1) * 4, :], p[:].rearrange("p (c t) -> p c t", c=4),
                             enb[:, hf * 4:(hf + 1) * 4, :], ALU.mult, [p, enb], [kTb])
                    p = nb()
                    for h in range(4):
                        for dk in range(2):
                            S.mm(p[:, h * 128:(h + 1) * 128], kTb[:, h * 2 + dk, :], qTb[:, h * 2 + dk, :], dk == 0, dk == 1, [kTb, qTb], [p])
                    S.tt("dve", attb[:], p[:].rearrange("p (h t) -> p h t", h=4), tri.unsqueeze(1).to_broadcast([128, 4, 128]),
                         ALU.mult, [p, cst], [attb])
                    S.cp("pool", vb[:], pin[:, C_V:C_V + 2048], [pin], [vb])
                    for h in range(4):
                        S.mm(PO[h][:], attb[:, h, :], vb[:, h * 512:(h + 1) * 512], True, False, [attb, vb], [PO[h]])
                    for j in range(nseg):
                        if sample:
                            Sj = Sst[j % 2]
                            S.dma("sp", Sj[:].rearrange("p (h t) v -> p h t v", h=4),
                                  st_gla[l, j].rearrange("h (t p) v -> p h t v", p=128), sb_w=Sj)
                            S.cp("pool", Sb[:], Sj[:], [Sj], [Sb])
                            qmj = qm[j % 2]
                            if j >= 2:
                                jo = j - 2
                                S.memset("pool", qmj[:, :, jo * SL:(jo + 1) * SL], 0.0, [qmj])
                            S.cp("pool", qmj[:, :, j * SL:(j + 1) * SL], qTb[:, :, j * SL:(j + 1) * SL], [qTb], [qmj])
                            kkj = kkm[j % 2]
                            S.ts("dve", kkj[:], kkb[:], cst[:, K_SEGCOL, j:j + 1], None, ALU.mult, None, [kkb, cst], [kkj])
                            ql, kl = qmj, kkj
                        else:
                            Sj = Sst[0]
                            ql, kl = qTb, kkb
                        last = (j == nseg - 1)
                        for h in range(4):
                            for dk in range(2):
                                S.mm(PO[h][:], ql[:, h * 2 + dk, :], Sb[:, h * 2 + dk, :], False, last and dk == 1, [ql, Sb], [PO[h]])
                        tl = (j * SL + SL - 1) if sample else 127
                        for c in range(8):
                            h = c // 2
                            p = nb()
                            S.mm(p[:], kl[:, c * 128:(c + 1) * 128], vb[:, h * 512:(h + 1) * 512], True, True, [kl, vb], [p])
                            S.stt("dve", Sj[:, c, :], Sj[:, c, :], eb[:, c, tl:tl + 1], p[:], ALU.mult, ALU.add, [Sj, eb, p], [Sj])
                        if sample:
                            S.dma("sp", gla_s[l, j].rearrange("h (t p) v -> p h t v", p=128),
                                  Sj[:].rearrange("p (h t) v -> p h t v", h=4), sb_r=Sj)
                        else:
                            S.cp("pool", Sb[:], Sj[:], [Sj], [Sb])
                    if sample:
                        gla_post(l, i, [PO[h][:] for h in range(4)], PO, pin[:, C_R:C_R + 2048], pin, gln, junk, ss, oa, oTb, nb)
                    else:
                        for h in range(4):
                            S.cp("act", oa[:, h * 512:(h + 1) * 512], PO[h][:], [PO[h]], [oa])
                        S.dma("sp", o_loc[l][rows(i), :], oa[:], sb_r=oa)
                        S.tt("dve", qgb[:], qTb[:], gam[:].unsqueeze(2).to_broadcast([128, 8, 128]), ALU.mult, [qTb, gam], [qgb])
                        S.dma("sp", qg_d[l][i], qgb[:], sb_r=qgb)
                        S.tt("dve", gam[:], gam[:], eb[:, :, 127], ALU.mult, [gam, eb], [gam])
                if not sample:
                    Sj = Sst[0]
                    S.dma("sp", sloc_d[l], Sj[:].rearrange("p c v -> p (c v)"), sb_r=Sj)
                    S.dma("sp", gam_d[l], gam[:], sb_r=gam)
                    S.ts("dve", Sj[:], Sj[:], flg[:, 1:2], None, ALU.mult, None, [Sj, flg], [Sj])
                    S.dma("sp", gs_snd[l].ap(), Sj[:].rearrange("p c v -> p (c v)"), sb_r=Sj)
            if not sample:
                S.collective(gs_snd[l], gs_rcv[l])
            else:
                S.barrier()

        def phase_gla2(l):
            with ExitStack() as pes:
                Sin = S.sb(pes, "g2Sin", [128, 8, 512], F32, dma=True)
                Sloc = S.sb(pes, "g2Sloc", [128, 8, 512], F32, dma=True)
                Sinb = S.sb(pes, "g2Sinb", [128, 8, 512], BF16)
                gam = S.sb(pes, "g2gam", [128, 8], F32, dma=True)
                gln = S.sb(pes, "g2gln", [128, 512], F32, dma=True)
                ot = [S.sb(pes, "g2ot%d" % i, [128, 2048], F32, dma=True) for i in range(2)]
                rt = [S.sb(pes, "g2rt%d" % i, [128, 2048], F32, dma=True) for i in range(2)]
                qgt = [S.sb(pes, "g2qg%d" % i, [128, 8, 128], BF16, dma=True) for i in range(2)]
                junk = S.sb(pes, "g2junk", [128, 512], F32)
                ss = S.sb(pes, "g2ss", [128, 8], F32)
                oa = S.sb(pes, "g2oa", [128, 2048], F32)
                oTb = S.sb(pes, "g2oTb", [128, 16, 128], BF16, dma=True)
                P = [S.ps(pes, "g2P%d" % i, [128, 512], F32) for i in range(8)]
                rot = [0]

                def nb():
                    b_ = P[rot[0] % 4]
                    rot[0] += 1
                    return b_
                S.dma("sp", gln[:], gla_norm[l:l + 1, :].partition_broadcast(128), sb_w=gln)
                S.dma("sp", Sin[:].rearrange("p c v -> p (c v)"), gs_rcv[l].ap(), sb_w=Sin)
                S.dma("sp", Sloc[:].rearrange("p c v -> p (c v)"), sloc_d[l], sb_w=Sloc)
                S.dma("sp", gam[:], gam_d[l], sb_w=gam)
                S.ts("dve", Sin[:], Sin[:], flg[:, 0:1], None, ALU.mult, None, [Sin, flg], [Sin])
                S.cp("pool", Sinb[:], Sin[:], [Sin], [Sinb])
                for c in range(8):
                    S.stt("dve", Sloc[:, c, :], Sin[:, c, :], gam[:, c:c + 1], Sloc[:, c, :], ALU.mult, ALU.add, [Sin, gam, Sloc], [Sloc])
                S.dma("sp", gla_p[l].rearrange("h (t p) v -> p h t v", p=128), Sloc[:].rearrange("p (h t) v -> p h t v", h=4), sb_r=Sloc)
                for i in range(NPT):
                    o_ = ot[i % 2]
                    r_ = rt[i % 2]
                    q_ = qgt[i % 2]
                    S.dma("sp", o_[:], o_loc[l][rows(i), :], sb_w=o_)
                    S.dma("sp", r_[:], proj[l][rows(i), C_R:C_R + 2048], sb_w=r_)
                    S.dma("sp", q_[:], qg_d[l][i], sb_w=q_)
                    for h in range(4):
                        for dk in range(2):
                            S.mm(P[4 + h][:], q_[:, h * 2 + dk, :], Sinb[:, h * 2 + dk, :], dk == 0, dk == 1, [q_, Sinb], [P[4 + h]])
                        S.tt("dve", o_[:, h * 512:(h + 1) * 512], o_[:, h * 512:(h + 1) * 512], P[4 + h][:], ALU.add, [o_, P[4 + h]], [o_])
                    gla_post(l, i, [o_[:, h * 512:(h + 1) * 512] for h in range(4)], [o_] * 4, r_[:], r_, gln, junk, ss, oa, oTb, nb)
            S.barrier()

        def ssd_post(l, i, yy, xap, xbuf, zt, dsk, ssn, junk, ss, yTb, nb):
            xv_ = xap.rearrange("p (h d) -> p h d", h=64)
            S.tt("pool", xv_, xv_, dsk[:].unsqueeze(2).to_broadcast([128, 64, 64]), ALU.mult, [xbuf, dsk], [xbuf])
            S.tt("pool", yy[:], yy[:], xap, ALU.add, [yy, xbuf], [yy])
            S.act(zt[:], zt[:], AF.Silu, [zt], [zt])
            S.tt("dve", yy[:], yy[:], zt[:], ALU.mult, [yy, zt], [yy])
            S.memset("pool", ss[:], 0.0, [ss])
            for g in range(8):
                S.act(junk[:], yy[:, g * 512:(g + 1) * 512], AF.Square, [yy, ss], [junk, ss], accum=ss[:, g:g + 1])
            S.ts("dve", ss[:, 8:16], ss[:, 0:8], 1.0 / 512, EPS, ALU.mult, ALU.add, [ss], [ss])
            S.act(ss[:, 8:16], ss[:, 8:16], AF.Sqrt, [ss], [ss])
            S.recip(ss[:, 8:16], ss[:, 8:16], [ss], [ss])
            for g in range(8):
                S.stt("dve", yy[:, g * 512:(g + 1) * 512], yy[:, g * 512:(g + 1) * 512], ss[:, 8 + g:9 + g],
                      ssn[:, g * 512:(g + 1) * 512], ALU.mult, ALU.mult, [yy, ss, ssn], [yy])
            for q4 in range(8):
                p = nb()
                for c in range(4):
                    cc_ = q4 * 4 + c
                    S.tr(p[:, c * 128:(c + 1) * 128], yy[:, cc_ * 128:(cc_ + 1) * 128], ident, [yy, cst], [p])
                S.cp("act" if q4 % 2 else "dve", yTb[:, q4 * 4:(q4 + 1) * 4, :], p[:].rearrange("p (c t) -> p c t", c=4), [p], [yTb])
            S.dma("sp", ybT[l][i], yTb[:], sb_r=yTb)

        def phase_ssd(l, sample):
            tiles = [NPT] if sample else list(range(NPT))
            K_TRI = K_TRIS if sample else K_TRIP
            K_SEL = K_SELS if sample else K_SELP
            K_NEGM = K_NEGS if sample else K_NEGP
            nseg = NSEG if sample else 1
            with ExitStack() as pes:
                xc = S.sb(pes, "dxc", [128, 6144], F32, dma=True)
                zt = S.sb(pes, "dz", [128, 4096], F32, dma=True)
                dtt = S.sb(pes, "ddt", [128, 64], F32, dma=True)
                dtb = S.sb(pes, "ddtb", [128, 64], F32, dma=True)
                aneg = S.sb(pes, "daneg", [128, 64], F32, dma=True)
                dsk = S.sb(pes, "ddsk", [128, 64], F32, dma=True)
                ssn = S.sb(pes, "dssn", [128, 4096], F32, dma=True)
                negm8 = S.sb(pes, "dnegm8", [128, 4, 128], F32)
                sm = S.sb(pes, "dsm", [128, 10, 64], F32)
                xsb = S.sb(pes, "dxsb", [128, 4096], BF16)
                Bb = S.sb(pes, "dBb", [128, 1024], BF16)
                BTb = S.sb(pes, "dBTb", [128, 8, 128], BF16)
                CTb = S.sb(pes, "dCTb", [128, 8, 128], BF16, dma=True)
                xdl = S.sb(pes, "dxdl", [128, 4096], BF16)
                nd_ = 1 if sample else 2
                diag = [S.sb(pes, "ddiag%d" % i, [128, 8, 128], F32) for i in range(nd_)]
                seg = [S.sb(pes, "dseg%d" % i, [128, 8, 128], F32) for i in range(nd_)]
                WTb = [S.sb(pes, "dWTb%d" % i, [128, 8, 128], BF16) for i in range(nd_)]
                yy = S.sb(pes, "dy", [128, 4096], F32, dma=True)
                tmp = S.sb(pes, "dtmp", [128, 512], F32)
                junk = S.sb(pes, "djunk", [128, 512], F32)
                ss = S.sb(pes, "dss", [128, 16], F32)
                yTb = S.sb(pes, "dyTb", [128, 32, 128], BF16, dma=True)
                hT = S.sb(pes, "dhT", [128, 4096], F32, dma=True)
                hTb = S.sb(pes, "dhTb", [128, 4096], BF16)
                if sample:
                    hj = S.sb(pes, "dhj", [128, 32, 128], F32, dma=True)
                    cm = [S.sb(pes, "dcm%d" % i, [128, 8, 128], BF16) for i in range(2)]
                    Bm = [S.sb(pes, "dBm%d" % i, [128, 1024], BF16) for i in range(2)]
                    for b_ in cm:
                        S.memset("pool", b_[:], 0.0, [b_])
                else:
                    hj = yy
                    egt = S.sb(pes, "degt", [128, 64], F32, dma=True)
                    gamh = S.sb(pes, "dgamh", [128, 64], F32, dma=True)
                    S.memset("pool", gamh[:], 1.0, [gamh])
                    S.memset("pool", hT[:], 0.0, [hT])
                    S.memset("pool", hTb[:], 0.0, [hTb])
                P = [S.ps(pes, "dP%d" % i, [128, 512], F32) for i in range(8)]
                rot = [0]

                def nb():
                    b_ = P[rot[0] % 8]
                    rot[0] += 1
                    return b_
                DTP, LND, AA, CS, CS2, ECS, DL, DEC, T1, T2 = range(10)
                S.dma("sp", dtb[:], dt_bias[l:l + 1, :].partition_broadcast(128), sb_w=dtb)
                S.dma("sp", aneg[:], A_log[l:l + 1, :].partition_broadcast(128), sb_w=aneg)
                S.dma("sp", dsk[:], d_skip[l:l + 1, :].partition_broadcast(128), sb_w=dsk)
                S.dma("sp", ssn[:], ssd_norm[l:l + 1, :].partition_broadcast(128), sb_w=ssn)
                S.act(aneg[:], aneg[:], AF.Exp, [aneg], [aneg])
                S.ts("dve", aneg[:], aneg[:], -1.0, None, ALU.mult, None, [aneg], [aneg])
                for c in range(4):
                    S.cp("dve", negm8[:, c, :], cst[:, K_NEGM, :], [cst], [negm8])
                tri = cst[:, K_TRI, :]
                selI = cst[:, K_SEL, :]
                ones = cst[:, K_ONES, :]

                def load_h(j):
                    S.dma("sp", hj[:], st_ssm[l, j].rearrange("h p n -> (h p) n").rearrange("(c q) n -> q c n", q=128), sb_w=hj)
                    for q4 in range(8):
                        p = nb()
                        for c in range(4):
                            S.tr(p[:, c * 128:(c + 1) * 128], hj[:, q4 * 4 + c, :], ident, [hj, cst], [p])
                        S.cp("act" if q4 % 2 else "dve", hT[:, q4 * 512:(q4 + 1) * 512], p[:], [p], [hT])
                    S.cp("pool", hTb[:], hT[:], [hT], [hTb])

                def store_h(dst):
                    for q4 in range(8):
                        p = nb()
                        for c in range(4):
                            cc_ = q4 * 4 + c
                            S.tr(p[:, c * 128:(c + 1) * 128], hT[:, cc_ * 128:(cc_ + 1) * 128], ident, [hT, cst], [p])
                        S.cp("act" if q4 % 2 else "dve", hj[:, q4 * 4:(q4 + 1) * 4, :] if sample else
                             hj[:, q4 * 512:(q4 + 1) * 512].rearrange("p (c n) -> p c n", c=4),
                             p[:].rearrange("p (c n) -> p c n", c=4), [p], [hj])
                    src = hj[:] if sample else hj[:].rearrange("p (c n) -> p c n", c=32)
                    S.dma("sp", dst.rearrange("h p n -> (h p) n").rearrange("(c q) n -> q c n", q=128), src, sb_r=hj)

                for i in tiles:
                    S.dma("sp", xc[:], xcs[l][rows(i), :], sb_w=xc)
                    if sample:
                        S.dma("sp", zt[:], proj[l][rows(i), C_Z:C_Z + 4096], sb_w=zt)
                    S.dma("sp", dtt[:], proj[l][rows(i), C_DT:C_DT + 64], sb_w=dtt)
                    S.tt("dve", sm[:, T1, :], dtt[:], dtb[:], ALU.add, [dtt, dtb], [sm])
                    S.act(sm[:, T1, :], sm[:, T1, :], AF.Exp, [sm], [sm])
                    S.act(sm[:, DTP, :], sm[:, T1, :], AF.Ln, [sm], [sm], bias=1.0)
                    S.act(sm[:, LND, :], sm[:, DTP, :], AF.Ln, [sm], [sm])
                    S.tt("dve", sm[:, AA, :], sm[:, DTP, :], aneg[:], ALU.mult, [sm, aneg], [sm])
                    p = nb()
                    S.mm(p[:, 0:64], tri, sm[:, AA, :], True, True, [cst, sm], [p])
                    S.cp("dve", sm[:, CS, :], p[:, 0:64], [p], [sm])
                    S.tt("dve", sm[:, CS2, :], sm[:, CS, :], sm[:, LND, :], ALU.subtract, [sm], [sm])
                    S.act(sm[:, ECS, :], sm[:, CS, :], AF.Exp, [sm], [sm])
                    p = nb()
                    S.mm(p[:, 0:64], selI, sm[:, CS, :], True, True, [cst, sm], [p])
                    S.act(sm[:, DL, :], p[:, 0:64], AF.Exp, [p], [sm])
                    S.tt("dve", sm[:, DL, :], sm[:, DL, :], sm[:, DTP, :], ALU.mult, [sm], [sm])
                    S.cp("pool", xsb[:], xc[:, 0:4096], [xc], [xsb])
                    S.cp("act", Bb[:], xc[:, 4096:5120], [xc], [Bb])
                    for (src0, dstb) in ((4096, BTb), (5120, CTb)):
                        for hf in range(2):
                            p = nb()
                            for c in range(4):
                                g = hf * 4 + c
                                S.tr(p[:, c * 128:(c + 1) * 128], xc[:, src0 + g * 128:src0 + (g + 1) * 128], ident, [xc, cst], [p])
                            S.cp("act" if hf else "dve", dstb[:, hf * 4:(hf + 1) * 4, :], p[:].rearrange("p (c t) -> p c t", c=4), [p], [dstb])
                    S.tt("dve", xdl[:].rearrange("p (h d) -> p h d", h=64), xc[:, 0:4096].rearrange("p (h d) -> p h d", h=64),
                         sm[:, DL, :].unsqueeze(2).to_broadcast([128, 64, 64]), ALU.mult, [xc, sm], [xdl])
                    if sample:
                        S.memset("pool", yy[:], 0.0, [yy])
                    for j in range(nseg):
                        if sample:
                            load_h(j)
                            cmj = cm[j % 2]
                            if j >= 2:
                                jo = j - 2
                                S.memset("pool", cmj[:, :, jo * SL:(jo + 1) * SL], 0.0, [cmj])
                            S.cp("pool", cmj[:, :, j * SL:(j + 1) * SL], CTb[:, :, j * SL:(j + 1) * SL], [CTb], [cmj])
                            Bmj = Bm[j % 2]
                            S.ts("dve", Bmj[:], Bb[:], cst[:, K_SEGCOL, j:j + 1], None, ALU.mult, None, [Bb, cst], [Bmj])
                            for g in range(8):
                                p = nb()
                                S.mm(p[:], cmj[:, g, :], hTb[:, g * 512:(g + 1) * 512], True, True, [cmj, hTb], [p])
                                S.tt("dve", yy[:, g * 512:(g + 1) * 512], yy[:, g * 512:(g + 1) * 512], p[:], ALU.add, [yy, p], [yy])
                            Bl = Bmj
                            krow = K_ROW0 + j
                        else:
                            Bl = Bb
                            krow = K_ROWP
                        if sample:
                            p = nb()
                            S.mm(p[:, 0:64], cst[:, krow, :], sm[:, CS, :], True, True, [cst, sm], [p])
                            S.act(sm[:, DEC, :], p[:, 0:64], AF.Exp, [p], [sm])
                            for g in range(8):
                                p = nb()
                                S.mm(p[:], Bl[:, g * 128:(g + 1) * 128], xdl[:, g * 512:(g + 1) * 512], True, True, [Bl, xdl], [p])
                                hv = hT[:, g * 512:(g + 1) * 512].rearrange("p (h d) -> p h d", h=8)
                                S.tt("dve", hv, hv, sm[:, DEC, g * 8:(g + 1) * 8].unsqueeze(2).to_broadcast([128, 8, 64]), ALU.mult, [hT, sm], [hT])
                                S.tt("dve", hT[:, g * 512:(g + 1) * 512], hT[:, g * 512:(g + 1) * 512], p[:], ALU.add, [hT, p], [hT])
                            store_h(ssm_s[l, j])
                    for g in range(8):
                        pc = nb()
                        S.mm(pc[:, 0:128], BTb[:, g, :], CTb[:, g, :], True, True, [BTb, CTb], [pc])
                        dg = diag[g % nd_]
                        S.tt("pool", dg[:], ident.unsqueeze(1).to_broadcast([128, 8, 128]),
                             sm[:, CS, g * 8:(g + 1) * 8].unsqueeze(2).to_broadcast([128, 8, 128]), ALU.mult, [cst, sm], [dg])
                        sg = seg[g % nd_]
                        for hf in range(2):
                            p = nb()
                            S.mm(p[:], ones, dg[:, hf * 4:(hf + 1) * 4, :], True, False, [cst, dg], [p])
                            S.mm(p[:], ident, negm8[:], False, True, [cst, negm8], [p])
                            S.tt("dve", sg[:, hf * 4:(hf + 1) * 4, :], p[:].rearrange("p (h t) -> p h t", h=4),
                                 sm[:, CS2, g * 8 + hf * 4:g * 8 + hf * 4 + 4].unsqueeze(2).to_broadcast([128, 4, 128]), ALU.subtract, [p, sm], [sg])
                        S.act(sg[:], sg[:], AF.Exp, [sg], [sg])
                        wt = WTb[g % nd_]
                        S.tt("dve", wt[:], sg[:], pc[:, 0:128].unsqueeze(1).to_broadcast([128, 8, 128]), ALU.mult, [sg, pc], [wt])
                        py = nb()
                        for h in range(8):
                            hh = g * 8 + h
                            S.mm(py[:, h * 64:(h + 1) * 64], wt[:, h, :], xsb[:, hh * 64:(hh + 1) * 64], True, True, [wt, xsb], [py])
                        ecsb = sm[:, ECS, g * 8:(g + 1) * 8].unsqueeze(2).to_broadcast([128, 8, 64])
                        yg = yy[:, g * 512:(g + 1) * 512]
                        if sample:
                            S.tt("pool", yg.rearrange("p (h d) -> p h d", h=8), yg.rearrange("p (h d) -> p h d", h=8), ecsb, ALU.mult, [yy, sm], [yy])
                            S.tt("dve", yg, yg, py[:], ALU.add, [yy, py], [yy])
                        else:
                            pz = nb()
                            S.mm(pz[:], CTb[:, g, :], hTb[:, g * 512:(g + 1) * 512], True, True, [CTb, hTb], [pz])
                            S.tt("dve", tmp[:].rearrange("p (h d) -> p h d", h=8), pz[:].rearrange("p (h d) -> p h d", h=8), ecsb, ALU.mult, [pz, sm], [tmp])
                            S.tt("dve", yg, tmp[:], py[:], ALU.add, [tmp, py], [yy])
                    if not sample:
                        p = nb()
                        S.mm(p[:, 0:64], cst[:, K_ROWP, :], sm[:, CS, :], True, True, [cst, sm], [p])
                        S.act(sm[:, DEC, :], p[:, 0:64], AF.Exp, [p], [sm])
                        for g in range(8):
                            p = nb()
                            S.mm(p[:], Bb[:, g * 128:(g + 1) * 128], xdl[:, g * 512:(g + 1) * 512], True, True, [Bb, xdl], [p])
                            hv = hT[:, g * 512:(g + 1) * 512].rearrange("p (h d) -> p h d", h=8)
                            S.tt("pool", hv, hv, sm[:, DEC, g * 8:(g + 1) * 8].unsqueeze(2).to_broadcast([128, 8, 64]), ALU.mult, [hT, sm], [hT])
                            S.tt("dve", hT[:, g * 512:(g + 1) * 512], hT[:, g * 512:(g + 1) * 512], p[:], ALU.add, [hT, p], [hT])
                        S.cp("pool", hTb[:], hT[:], [hT], [hTb])
                    if sample:
                        ssd_post(l, i, yy, xc[:, 0:4096], xc, zt, dsk, ssn, junk, ss, yTb, nb)
                    else:
                        S.dma("sp", y_loc[l][rows(i), :], yy[:], sb_r=yy)
                        S.dma("sp", ct_d[l][i], CTb[:], sb_r=CTb)
                        S.tt("dve", egt[:], sm[:, ECS, :], gamh[:], ALU.mult, [sm, gamh], [egt])
                        S.dma("sp", eg_d[l][i], egt[:], sb_r=egt)
                        S.tt("dve", gamh[:], gamh[:], sm[:, DEC, :], ALU.mult, [gamh, sm], [gamh])
                if not sample:
                    S.dma("sp", hloc_d[l], hT[:], sb_r=hT)
                    S.dma("sp", gamh_d[l], gamh[:], sb_r=gamh)
                    S.ts("dve", hT[:], hT[:], flg[:, 1:2], None, ALU.mult, None, [hT, flg], [hT])
                    S.dma("sp", hs_snd[l].ap(), hT[:], sb_r=hT)
            if not sample:
                S.collective(hs_snd[l], hs_rcv[l])
            else:
                S.barrier()

        def phase_ssd2(l):
            with ExitStack() as pes:
                hin = S.sb(pes, "d2hin", [128, 4096], F32, dma=True)
                hT = S.sb(pes, "d2hT", [128, 4096], F32, dma=True)
                hinb = S.sb(pes, "d2hinb", [128, 4096], BF16)
                gamh = S.sb(pes, "d2gamh", [128, 64], F32, dma=True)
                dsk = S.sb(pes, "d2dsk", [128, 64], F32, dma=True)
                ssn = S.sb(pes, "d2ssn", [128, 4096], F32, dma=True)
                yy = S.sb(pes, "d2y", [128, 4096], F32, dma=True)
                xs_ = S.sb(pes, "d2xs", [128, 4096], F32, dma=True)
                zt = S.sb(pes, "d2z", [128, 4096], F32, dma=True)
                ctb = S.sb(pes, "d2ct", [128, 8, 128], BF16, dma=True)
                egt = S.sb(pes, "d2eg", [128, 64], F32, dma=True)
                tmp = S.sb(pes, "d2tmp", [128, 512], F32)
                junk = S.sb(pes, "d2junk", [128, 512], F32)
                ss = S.sb(pes, "d2ss", [128, 16], F32)
                yTb = S.sb(pes, "d2yTb", [128, 32, 128], BF16, dma=True)
                hj = S.sb(pes, "d2hj", [128, 32, 128], F32, dma=True)
                P = [S.ps(pes, "d2P%d" % i, [128, 512], F32) for i in range(8)]
                rot = [0]

                def nb():
                    b_ = P[rot[0] % 8]
                    rot[0] += 1
                    return b_
                S.dma("sp", dsk[:], d_skip[l:l + 1, :].partition_broadcast(128), sb_w=dsk)
                S.dma("sp", ssn[:], ssd_norm[l:l + 1, :].partition_broadcast(128), sb_w=ssn)
                S.dma("sp", hin[:], hs_rcv[l].ap(), sb_w=hin)
                S.dma("sp", hT[:], hloc_d[l], sb_w=hT)
                S.dma("sp", gamh[:], gamh_d[l], sb_w=gamh)
                S.ts("dve", hin[:], hin[:], flg[:, 0:1], None, ALU.mult, None, [hin, flg], [hin])
                S.cp("pool", hinb[:], hin[:], [hin], [hinb])
                hv = hin[:].rearrange("p (h d) -> p h d", h=64)
                S.tt("dve", hv, hv, gamh[:].unsqueeze(2).to_broadcast([128, 64, 64]), ALU.mult, [hin, gamh], [hin])
                S.tt("pool", hT[:], hT[:], hin[:], ALU.add, [hT, hin], [hT])
                for q4 in range(8):
                    p = nb()
                    for c in range(4):
                        cc_ = q4 * 4 + c
                        S.tr(p[:, c * 128:(c + 1) * 128], hT[:, cc_ * 128:(cc_ + 1) * 128], ident, [hT, cst], [p])
                    S.cp("act" if q4 % 2 else "dve", hj[:, q4 * 4:(q4 + 1) * 4, :], p[:].rearrange("p (c n) -> p c n", c=4), [p], [hj])
                S.dma("sp", ssm_p[l].rearrange("h p n -> (h p) n").rearrange("(c q) n -> q c n", q=128), hj[:], sb_r=hj)
                for i in range(NPT):
                    S.dma("sp", yy[:], y_loc[l][rows(i), :], sb_w=yy)
                    S.dma("sp", xs_[:], xcs[l][rows(i), 0:4096], sb_w=xs_)
                    S.dma("sp", zt[:], proj[l][rows(i), C_Z:C_Z + 4096], sb_w=zt)
                    S.dma("sp", ctb[:], ct_d[l][i], sb_w=ctb)
                    S.dma("sp", egt[:], eg_d[l][i], sb_w=egt)
                    for g in range(8):
                        pz = nb()
                        S.mm(pz[:], ctb[:, g, :], hinb[:, g * 512:(g + 1) * 512], True, True, [ctb, hinb], [pz])
                        S.tt("dve", tmp[:].rearrange("p (h d) -> p h d", h=8), pz[:].rearrange("p (h d) -> p h d", h=8),
                             egt[:, g * 8:(g + 1) * 8].unsqueeze(2).to_broadcast([128, 8, 64]), ALU.mult, [pz, egt], [tmp])
                        S.tt("pool", yy[:, g * 512:(g + 1) * 512], yy[:, g * 512:(g + 1) * 512], tmp[:], ALU.add, [yy, tmp], [yy])
                    ssd_post(l, i, yy, xs_[:], xs_, zt, dsk, ssn, junk, ss, yTb, nb)
            S.barrier()

        def phase_o1(l, mT):
            with ExitStack() as pes:
                wg = [S.sb(pes, "owg%d" % i, [128, 16, 512], BF16, dma=True) for i in range(1)]
                ws = [S.sb(pes, "ows%d" % i, [128, 32, 512], BF16, dma=True) for i in range(1)]
                oa = [S.sb(pes, "ooa%d" % i, [128, 16, 128], BF16, dma=True) for i in range(2)]
                yb = [S.sb(pes, "oyb%d" % i, [128, 32, 128], BF16, dma=True) for i in range(2)]
                gA = [S.sb(pes, "ogA%d" % i, [128, 512], F32, dma=True) for i in range(2)]
                gB = [S.sb(pes, "ogB%d" % i, [128, 512], F32, dma=True) for i in range(2)]
                mg = [S.sb(pes, "omg%d" % i, [128, 512], F32) for i in range(2)]
                mgb = [S.sb(pes, "omgb%d" % i, [128, 512], BF16) for i in range(2)]
                pa = [S.ps(pes, "opa%d" % i, [128, 512], F32) for i in range(2)]
                pb = [S.ps(pes, "opb%d" % i, [128, 512], F32) for i in range(2)]
                pt = [S.ps(pes, "opt%d" % i, [128, 4, 128], BF16) for i in range(2)]
                wgv = w_gproj[l].rearrange("(kt p) n -> p kt n", p=128)
                wsv = w_sproj[l].rearrange("(kt p) n -> p kt n", p=128)
                n = 0
                for cb in range(4):
                    cs_ = slice(cb * 512, (cb + 1) * 512)
                    Wg = wg[0]
                    Ws = ws[0]
                    S.dma("pool", Wg[:], wgv[:, :, cs_], sb_w=Wg)
                    S.dma("pool", Ws[:], wsv[:, :, cs_], sb_w=Ws)
                    for i in range(NT):
                        b2 = n % 2
                        n += 1
                        S.dma("sp", oa[b2][:], oaT[l][i], sb_w=oa[b2])
                        S.dma("sp", yb[b2][:], ybT[l][i], sb_w=yb[b2])
                        S.dma("sp", gA[b2][:], proj[l][rows(i), C_G + cb * 512:C_G + (cb + 1) * 512], sb_w=gA[b2])
                        S.dma("sp", gB[b2][:], proj[l][rows(i), C_G + 2048 + cb * 512:C_G + 2048 + (cb + 1) * 512], sb_w=gB[b2])
                        for k in range(16):
                            S.mm(pa[b2][:], oa[b2][:, k, :], Wg[:, k, :], k == 0, k == 15, [oa[b2], Wg], [pa[b2]])
                        for k in range(32):
                            S.mm(pb[b2][:], yb[b2][:, k, :], Ws[:, k, :], k == 0, k == 31, [yb[b2], Ws], [pb[b2]])
                        S.act(gA[b2][:], gA[b2][:], AF.Sigmoid, [gA[b2]], [gA[b2]])
                        S.act(gB[b2][:], gB[b2][:], AF.Sigmoid, [gB[b2]], [gB[b2]])
                        S.tt("dve", mg[b2][:], pa[b2][:], gA[b2][:], ALU.mult, [pa[b2], gA[b2]], [mg[b2]])
                        S.tt("dve", gB[b2][:], pb[b2][:], gB[b2][:], ALU.mult, [pb[b2], gB[b2]], [gB[b2]])
                        S.tt("pool", mgb[b2][:], mg[b2][:], gB[b2][:], ALU.add, [mg[b2], gB[b2]], [mgb[b2]])
                        for c in range(4):
                            S.tr(pt[b2][:, c, :], mgb[b2][:, c * 128:(c + 1) * 128], identb[:], [mgb[b2], identb], [pt[b2]])
                        g_, o_ = hgi(i)
                        S.cp("act", mT[g_][:, cb * 4:(cb + 1) * 4, o_:o_ + 128], pt[b2][:], [pt[b2]], [mT[g_]])
            S.barrier()

        def phase_res(l, mT, wsrc, nk, gate_i, xsrc, xdst, actsrc=None):
            with ExitStack() as pes:
                wm = [S.sb(pes, "rwm%d" % i, [128, nk, 512], BF16, dma=True) for i in range(2)]
                xb = [S.sb(pes, "rxb%d" % i, [128, 512], F32, dma=True) for i in range(2)]
                gb = [S.sb(pes, "rgb%d" % i, [128, 512], F32, dma=True) for i in range(2)]
                pp = [S.ps(pes, "rpp%d" % i, [128, 512], F32) for i in range(2)]
                if actsrc is not None:
                    ab = [S.sb(pes, "rab%d" % i, [128, nk, 128], BF16, dma=True) for i in range(2)]
                wv = wsrc.rearrange("(kt p) n -> p kt n", p=128)
                n = 0
                for cb in range(4):
                    cs_ = slice(cb * 512, (cb + 1) * 512)
                    W = wm[cb % 2]
                    S.dma("pool", W[:], wv[:, :, cs_], sb_w=W)
                    for i in range(NT):
                        b2 = n % 2
                        n += 1
                        g = grp(i)
                        S.dma("sp", xb[b2][:], xsrc[rows(i), cs_], sb_w=xb[b2])
                        S.dma("sp", gb[b2][:], mod[l][g, :, gate_i * D + cb * 512:gate_i * D + (cb + 1) * 512], sb_w=gb[b2])
                        if actsrc is not None:
                            S.dma("sp", ab[b2][:], actsrc.rearrange("j p t -> p j t")[:, :, rows(i)], sb_w=ab[b2])
                            for k in range(nk):
                                S.mm(pp[b2][:], ab[b2][:, k, :], W[:, k, :], k == 0, k == nk - 1, [ab[b2], W], [pp[b2]])
                        else:
                            for k in range(nk):
                                S.mm(pp[b2][:], hview(mT, i, k), W[:, k, :], k == 0, k == nk - 1, [mT[hgi(i)[0]], W], [pp[b2]])
                        S.tt("dve", gb[b2][:], pp[b2][:], gb[b2][:], ALU.mult, [pp[b2], gb[b2]], [gb[b2]])
                        S.tt("pool", xb[b2][:], xb[b2][:], gb[b2][:], ALU.add, [xb[b2], gb[b2]], [xb[b2]])
                        S.dma("sp", xdst[rows(i), cs_], xb[b2][:], sb_r=xb[b2])
            S.barrier()

        def phase_f1(l, hT):
            with ExitStack() as pes:
                wgt = [S.sb(pes, "fwg%d" % i, [128, 16, 512], BF16, dma=True) for i in range(2)]
                wup = [S.sb(pes, "fwu%d" % i, [128, 16, 512], BF16, dma=True) for i in range(2)]
                pg = [S.ps(pes, "fpg%d" % i, [128, 512], F32) for i in range(2)]
                pu = [S.ps(pes, "fpu%d" % i, [128, 512], F32) for i in range(2)]
                sg = [S.sb(pes, "fsg%d" % i, [128, 512], F32) for i in range(2)]
                at = [S.sb(pes, "fat%d" % i, [128, 512], BF16, dma=True) for i in range(2)]
                wv = w_fin[l].rearrange("(kt p) n -> p kt n", p=128)
                n = 0
                for jb in range(11):
                    Wg = wgt[jb % 2]
                    Wu = wup[jb % 2]
                    S.dma("pool", Wg[:], wv[:, :, jb * 512:(jb + 1) * 512], sb_w=Wg)
                    S.dma("pool", Wu[:], wv[:, :, DFF + jb * 512:DFF + (jb + 1) * 512], sb_w=Wu)
                    for jj in range(4):
                        j = jb * 4 + jj
                        for tg in range(NG + 1):
                            N = 512 if tg < NG else 128
                            b2 = n % 2
                            n += 1
                            for k in range(16):
                                S.mm(pg[b2][:, 0:N], Wg[:, k, jj * 128:(jj + 1) * 128], hT[tg][:, k, :], k == 0, k == 15, [Wg, hT[tg]], [pg[b2]])
                            for k in range(16):
                                S.mm(pu[b2][:, 0:N], Wu[:, k, jj * 128:(jj + 1) * 128], hT[tg][:, k, :], k == 0, k == 15, [Wu, hT[tg]], [pu[b2]])
                            S.act(sg[b2][:, 0:N], pg[b2][:, 0:N], AF.Silu, [pg[b2]], [sg[b2]])
                            S.tt("dve", at[b2][:, 0:N], sg[b2][:, 0:N], pu[b2][:, 0:N], ALU.mult, [sg[b2], pu[b2]], [at[b2]])
                            S.dma("sp", actT[l][j, :, tg * 512:tg * 512 + N], at[b2][:, 0:N], sb_r=at[b2])
            S.barrier()

        def phase_final(xsrc):
            with ExitStack() as pes:
                xt = [S.sb(pes, "zx%d" % i, [128, D], F32, dma=True) for i in range(2)]
                fn = S.sb(pes, "zfn", [128, D], F32, dma=True)
                junk = S.sb(pes, "zjunk", [128, D], F32)
                ss = S.sb(pes, "zss", [128, 2], F32)
                S.dma("sp", fn[:], fnorm.partition_broadcast(128), sb_w=fn)
                for i in range(NT):
                    x = xt[i % 2]
                    S.dma("sp", x[:], xsrc[rows(i), :], sb_w=x)
                    S.memset("pool", ss[:], 0.0, [ss])
                    S.act(junk[:], x[:], AF.Square, [x, ss], [junk, ss], accum=ss[:, 0:1])
                    S.ts("dve", ss[:, 1:2], ss[:, 0:1], 1.0 / D, EPS, ALU.mult, ALU.add, [ss], [ss])
                    S.act(ss[:, 1:2], ss[:, 1:2], AF.Sqrt, [ss], [ss])
                    S.recip(ss[:, 1:2], ss[:, 1:2], [ss], [ss])
                    S.stt("dve", x[:], x[:], ss[:, 1:2], fn[:], ALU.mult, ALU.mult, [x, ss, fn], [x])
                    S.dma("sp", y_out[rows(i), :], x[:], sb_r=x)
            S.barrier()

        S.phase = "mod"
        phase_mod()
        xcur = xin
        for l in range(DEPTH):
            with ExitStack() as ges:
                hT = [S.sb(ges, "hT%d_%d" % (l, g), [128, 16, 512 if g < NG else 128], BF16) for g in range(NG + 1)]
                S.phase = "norm%d" % l
                phase_norm(xcur, l, 0, 1, hT)
                S.phase = "proj%d" % l
                phase_proj(l, hT)
            S.phase = "conv%d" % l
            phase_conv(l)
            S.phase = "gla%d" % l
            phase_gla(l, False)
            S.phase = "gla_s%d" % l
            phase_gla(l, True)
            S.phase = "ssd%d" % l
            phase_ssd(l, False)
            S.phase = "ssd_s%d" % l
            phase_ssd(l, True)
            S.phase = "gla2%d" % l
            phase_gla2(l)
            S.phase = "ssd2%d" % l
            phase_ssd2(l)
            with ExitStack() as ges:
                mT = [S.sb(ges, "mT%d_%d" % (l, g), [128, 16, 512 if g < NG else 128], BF16) for g in range(NG + 1)]
                S.phase = "o1%d" % l
                phase_o1(l, mT)
                S.phase = "res_mix%d" % l
                phase_res(l, mT, w_mix[l], 16, 2, xcur, xv[l][0])
            with ExitStack() as ges:
                hT = [S.sb(ges, "h2T%d_%d" % (l, g), [128, 16, 512 if g < NG else 128], BF16) for g in range(NG + 1)]
                S.phase = "norm2%d" % l
                phase_norm(xv[l][0], l, 3, 4, hT)
                S.phase = "f1%d" % l
                phase_f1(l, hT)
            S.phase = "res_ffo%d" % l
            phase_res(l, None, w_fout[l], 44, 5, xv[l][0], xv[l][1], actsrc=actT[l])
            xcur = xv[l][1]
        S.phase = "final"
        phase_final(xcur)
        import os
        if os.environ.get("KTRACE_MAP"):
            S.namemap = {}
        S.emit()
        if S.namemap is not None:
            import json
            json.dump(S.namemap, open(os.environ["KTRACE_MAP"], "w"))
        print("ops:", S.nops, {e: len(S.prog[e]) for e in S.prog})
    return nc


_NC_CACHE = {}


def kernel(x_prompt, x_sample, c_prompt, c_sample, state_gla, state_ssm, state_conv,
           w_ada, b_ada, w_in, w_gla_gate, b_gla_gate, gla_norm, w_gla_proj, conv_w, conv_b,
           dt_bias, A_log, d_skip, ssd_norm, w_ssd_proj, w_mix_out, w_ffn_in, w_ffn_out, final_norm):
    f = lambda a: np.ascontiguousarray(np.asarray(a, dtype=np.float32))
    x_prompt, x_sample, c_prompt, c_sample = f(x_prompt), f(x_sample), f(c_prompt), f(c_sample)
    state_gla, state_ssm, state_conv = f(state_gla), f(state_ssm), f(state_conv)
    if "nc" not in _NC_CACHE:
        _NC_CACHE["nc"] = build_nc()
    nc = _NC_CACHE["nc"]
    consts = make_consts()
    shared = {
        "consts": consts, "w_ada": f(w_ada), "b_ada": f(b_ada), "w_in": f(w_in), "w_gla_gate": f(w_gla_gate),
        "b_gla_gate": f(b_gla_gate), "gla_norm": f(gla_norm), "w_gla_proj": f(w_gla_proj), "conv_w": f(conv_w),
        "conv_b": f(conv_b), "dt_bias": f(dt_bias), "A_log": f(A_log), "d_skip": f(d_skip), "ssd_norm": f(ssd_norm),
        "w_ssd_proj": f(w_ssd_proj), "w_mix_out": f(w_mix_out), "w_ffn_in": f(w_ffn_in), "w_ffn_out": f(w_ffn_out),
        "final_norm": f(final_norm).reshape(1, D),
    }
    in_maps = []
    for c in range(8):
        ps = c // 2
        hf = c % 2
        s0 = c * NSEG
        xin = np.concatenate([x_prompt[ps][hf * NPT * 128:(hf + 1) * NPT * 128], x_sample[s0:s0 + NSEG].reshape(NSEG * SL, D)], axis=0)
        cc = np.stack([np.broadcast_to(c_prompt[ps][None, :], (128, D)),
                       np.repeat(c_sample[s0:s0 + NSEG], SL, axis=0)], axis=0)
        m = dict(shared)
        m["xin"] = np.ascontiguousarray(xin)
        fl = np.zeros((128, 2), np.float32)
        fl[:, 0] = float(hf)
        fl[:, 1] = float(1 - hf)
        m["flags"] = fl
        m["cc"] = np.ascontiguousarray(cc)
        m["st_gla"] = np.ascontiguousarray(state_gla[:, s0:s0 + NSEG])
        m["st_ssm"] = np.ascontiguousarray(state_ssm[:, s0:s0 + NSEG])
        m["st_conv"] = np.ascontiguousarray(state_conv[:, s0:s0 + NSEG])
        in_maps.append(m)
    res = run_bass_kernel_spmd(nc, in_maps, core_ids=list(range(8)))
    R = res.results
    y_prompt = np.stack([np.concatenate([R[2 * p]["y"][:NPT * 128], R[2 * p + 1]["y"][:NPT * 128]], axis=0) for p in range(4)], axis=0)
    y_sample = np.concatenate([R[c]["y"][NPT * 128:].reshape(NSEG, SL, D) for c in range(8)], axis=0)
    gla_p = np.stack([R[2 * p + 1]["gla_p"] for p in range(4)], axis=1)
    ssm_p = np.stack([R[2 * p + 1]["ssm_p"] for p in range(4)], axis=1)
    conv_p = np.stack([R[2 * p + 1]["conv_p"] for p in range(4)], axis=1)
    gla_s = np.concatenate([R[c]["gla_s"] for c in range(8)], axis=1)
    ssm_s = np.concatenate([R[c]["ssm_s"] for c in range(8)], axis=1)
    conv_s = np.concatenate([R[c]["conv_s"] for c in range(8)], axis=1)
    return (y_prompt.astype(np.float32), y_sample.astype(np.float32), gla_p.astype(np.float32), ssm_p.astype(np.float32),
            conv_p.astype(np.float32), gla_s.astype(np.float32), ssm_s.astype(np.float32), conv_s.astype(np.float32))
```

```python
import numpy as np
import concourse.bass as bass
import concourse.mybir as mybir
from concourse.bass_utils import run_bass_kernel_spmd
from contextlib import ExitStack

F32 = mybir.dt.float32
BF16 = mybir.dt.bfloat16
AF = mybir.ActivationFunctionType
ALU = mybir.AluOpType
AX = mybir.AxisListType

D = 2048
NPT = 8
NT = NPT + 1
NG = NPT // 4
TOK = NT * 128
NSEG = 16
SL = 8
DEPTH = 2
DFF = 5632
NIN = 20560
EPS = 1e-6
C_Q, C_K, C_V, C_R, C_GLR, C_Z, C_XBC, C_DT, C_G = 0, 1024, 2048, 4096, 6144, 6160, 10256, 16400, 16464
NEG = -30000.0


class DSem:
    def __init__(self, sem, name):
        self.sem = sem
        self.cnt = 0
        self.name = name


class Buf:
    def __init__(self, name, t, dsem=None):
        self.name = name
        self.t = t
        self.wr = {}
        self.rd = {}
        self.dsem = dsem

    def __getitem__(self, k):
        return self.t[k]


class Sched:
    ENGS = ("pe", "act", "dve", "pool", "sp")

    def __init__(self, nc, es, ndsem=40):
        self.nc = nc
        self.sem = {e: es.enter_context(nc.semaphore("s_" + e)) for e in ("pe", "act", "dve", "pool")}
        self.cnt = {e: 0 for e in self.ENGS}
        self.seen = {e: {} for e in self.ENGS}
        self.prog = {e: [] for e in self.ENGS}
        self.dram = {}
        self.dpool = [DSem(es.enter_context(nc.semaphore("d%d" % i)), "d%d" % i) for i in range(ndsem)]
        self.dfree = list(self.dpool)
        self.nops = 0
        self.cc_sem = es.enter_context(nc.semaphore("s_cc"))
        self.cc_cnt = 0
        self.phase = "init"
        self.namemap = None

    def sb(self, pes, name, shape, dtype=F32, dma=False):
        self.nops += 1
        name = "%s_u%d" % (name, self.nops)
        t = pes.enter_context(self.nc.sbuf_tensor(name, list(shape), dtype))
        ds = None
        if dma:
            ds = self.dfree.pop()
            pes.callback(self.dfree.append, ds)
        return Buf(name, t, ds)

    def ps(self, pes, name, shape, dtype=F32):
        self.nops += 1
        name = "%s_u%d" % (name, self.nops)
        t = pes.enter_context(self.nc.psum_tensor(name, list(shape), dtype))
        return Buf(name, t)

    def dbuf(self, pes, name):
        ds = self.dfree.pop()
        pes.callback(self.dfree.append, ds)
        return Buf(name, None, ds)

    @staticmethod
    def _add(need, d, skip=None, pe=False):
        for k, (sem, val) in d.items():
            if pe and k == "pe":
                continue
            if skip is not None and k == skip:
                continue
            if k not in need or need[k][1] < val:
                need[k] = (sem, val)

    def _waits(self, eng, need):
        waits = []
        seen = self.seen[eng]
        for k, (sem, val) in need.items():
            if seen.get(k, 0) >= val:
                continue
            seen[k] = val
            waits.append((sem, val))
        return waits

    def op(self, eng, fn, reads=(), writes=()):
        need = {}
        pe = eng == "pe"
        for b in reads:
            self._add(need, b.wr, pe=pe)
        for b in writes:
            self._add(need, b.wr, pe=pe)
            self._add(need, b.rd, skip=eng, pe=pe)
        waits = self._waits(eng, need)
        self.cnt[eng] += 1
        ev = (self.sem[eng], self.cnt[eng])
        self.prog[eng].append((waits, fn, ev[0], 1, self.phase))
        for b in writes:
            b.wr = {eng: ev}
            b.rd = {}
        for b in reads:
            if b not in writes:
                b.rd[eng] = ev
        self.nops += 1

    def dma(self, q, out, in_, sb_w=None, sb_r=None, after=(), produces=None, evbuf=None):
        need = {}
        if sb_r is not None:
            self._add(need, sb_r.wr)
        if sb_w is not None:
            self._add(need, sb_w.wr)
            self._add(need, sb_w.rd)
        for key in after:
            for (k, sem, val) in self.dram.get(key, ()):
                if k not in need or need[k][1] < val:
                    need[k] = (sem, val)
        waits = self._waits(q, need)
        eb = evbuf if evbuf is not None else (sb_w if sb_w is not None else sb_r)
        ds = eb.dsem
        ds.cnt += 1
        ev = (ds.sem, 16 * ds.cnt)
        k = ds.name
        self.prog[q].append((waits, (lambda e, o=out, i=in_: e.dma_start(out=o, in_=i)), ds.sem, 16, self.phase))
        if sb_w is not None:
            sb_w.wr = {k: ev}
            sb_w.rd = {}
        if sb_r is not None:
            sb_r.rd[k] = ev
        if produces is not None:
            self.dram.setdefault(produces, []).append((k, ev[0], ev[1]))
        self.nops += 1

    def barrier(self):
        need = {}
        for e in ("pe", "act", "dve", "pool"):
            if self.cnt[e] > 0:
                need[e] = (self.sem[e], self.cnt[e])
        for ds in self.dpool:
            if ds.cnt > 0:
                need[ds.name] = (ds.sem, 16 * ds.cnt)
        for e in self.ENGS:
            waits = self._waits(e, dict(need))
            if waits:
                self.prog[e].append((waits, None, None, 0, self.phase))

    def collective(self, snd, rcv):
        self.barrier()
        self.cc_cnt += 1
        groups = [[0, 1], [2, 3], [4, 5], [6, 7]]
        self.prog["pool"].append(([], (lambda e: e.collective_compute("AllReduce", ALU.add, replica_groups=groups,
                                                                      ins=[snd.ap().opt()], outs=[rcv.ap().opt()])),
                                  self.cc_sem, 1, self.phase))
        for e in self.ENGS:
            self.prog[e].append(([(self.cc_sem, self.cc_cnt)], None, None, 0, self.phase))

    def emit(self):
        nc = self.nc
        prog = self.prog
        with nc.Block() as block:
            def run(eng_obj, items):
                for (waits, fn, sem, inc, ph) in items:
                    for (ws, wv) in waits:
                        eng_obj.wait_ge(ws, wv)
                    if fn is not None:
                        ins = fn(eng_obj)
                        ins.then_inc(sem, inc)
                        if self.namemap is not None:
                            self.namemap[ins.ins.name] = ph

            @block.tensor
            def _(e):
                run(e, prog["pe"])

            @block.scalar
            def _(e):
                run(e, prog["act"])

            @block.vector
            def _(e):
                run(e, prog["dve"])

            @block.gpsimd
            def _(e):
                run(e, prog["pool"])

            @block.sync
            def _(e):
                run(e, prog["sp"])

    def mm(self, out, lhsT, rhs, start, stop, reads, writes):
        self.op("pe", lambda e: e.matmul(out, lhsT=lhsT, rhs=rhs, start=start, stop=stop), reads, writes)

    def tr(self, out, in_, ident, reads, writes):
        self.op("pe", lambda e: e.transpose(out=out, in_=in_, identity=ident), reads, writes)

    def act(self, out, in_, func, reads, writes, bias=0.0, scale=1.0, accum=None):
        if accum is None:
            self.op("act", lambda e: e.activation(out=out, in_=in_, func=func, bias=bias, scale=scale), reads, writes)
        else:
            self.op("act", lambda e: e.activation(out=out, in_=in_, func=func, bias=bias, scale=scale, accum_out=accum), reads, writes)

    def tt(self, eng, out, in0, in1, op, reads, writes):
        self.op(eng, lambda e: e.tensor_tensor(out=out, in0=in0, in1=in1, op=op), reads, writes)

    def ts(self, eng, out, in0, s1, s2, op0, op1, reads, writes):
        if s2 is None:
            self.op(eng, lambda e: e.tensor_scalar(out=out, in0=in0, scalar1=s1, scalar2=None, op0=op0), reads, writes)
        else:
            self.op(eng, lambda e: e.tensor_scalar(out=out, in0=in0, scalar1=s1, scalar2=s2, op0=op0, op1=op1), reads, writes)

    def stt(self, eng, out, in0, scalar, in1, op0, op1, reads, writes):
        self.op(eng, lambda e: e.scalar_tensor_tensor(out=out, in0=in0, scalar=scalar, in1=in1, op0=op0, op1=op1), reads, writes)

    def cp(self, eng, out, in_, reads, writes):
        if eng == "act":
            self.op("act", lambda e: e.copy(out=out, in_=in_), reads, writes)
        else:
            self.op(eng, lambda e: e.tensor_copy(out=out, in_=in_), reads, writes)

    def memset(self, eng, ap, val, writes):
        self.op(eng, lambda e: e.memset(ap, val), (), writes)

    def recip(self, out, in_, reads, writes):
        self.op("dve", lambda e: e.reciprocal(out=out, in_=in_), reads, writes)


K_ID, K_TRIP, K_TRIS, K_SELP, K_SELS, K_ONES, K_NEGP, K_NEGS, K_ROW0, K_ROWP, K_SEGCOL, K_NTRIP, K_NTRIS = 0, 1, 2, 3, 4, 5, 6, 7, 8, 24, 25, 26, 27
NCONST = 28


def make_consts():
    c = np.zeros((NCONST, 128, 128), np.float32)
    idx = np.arange(128)
    s = idx[:, None]
    t = idx[None, :]
    c[K_ID] = (s == t)
    c[K_TRIP] = (s <= t)
    same = (s // SL) == (t // SL)
    c[K_TRIS] = same & (s <= t)
    c[K_SELP] = (s == 127).astype(np.float32) - (s == t)
    last = (t // SL) * SL + SL - 1
    c[K_SELS] = (s == last).astype(np.float32) - (s == t)
    c[K_ONES] = 1.0
    c[K_NEGP] = np.where(s <= t, 0.0, NEG)
    c[K_NEGS] = np.where(same & (s <= t), 0.0, NEG)
    for j in range(NSEG):
        c[K_ROW0 + j] = (s == (SL * j + SL - 1)) * np.ones((1, 128))
    c[K_ROWP] = (s == 127) * np.ones((1, 128))
    c[K_SEGCOL][:, :NSEG] = ((idx[:, None] // SL) == np.arange(NSEG)[None, :])
    c[K_NTRIP] = -c[K_TRIP] / 16.0
    c[K_NTRIS] = -c[K_TRIS] / 16.0
    return np.ascontiguousarray(c.transpose(1, 0, 2))


def w_blocks():
    bl = []
    for (c0, n) in ((0, 6144), (C_GLR, 16), (C_Z, 4096), (C_XBC, 6144), (C_DT, 64), (C_G, 4096)):
        o = 0
        while o < n:
            w = min(512, n - o)
            bl.append((c0 + o, w))
            o += w
    return bl


def build_nc():
    nc = bass.Bass("TRN2", target_bir_lowering=False)

    def din(name, shape, dt=F32):
        return nc.dram_tensor(name, list(shape), dt, kind="ExternalInput").ap()

    def dout(name, shape, dt=F32):
        return nc.dram_tensor(name, list(shape), dt, kind="ExternalOutput").ap()

    def dint(name, shape, dt=F32):
        return nc.dram_tensor(name, list(shape), dt, kind="Internal").ap()

    xin = din("xin", [TOK, D])
    cc = din("cc", [2, 128, D])
    consts = din("consts", [128, NCONST, 128])
    flags = din("flags", [128, 2])
    st_gla = din("st_gla", [DEPTH, NSEG, 4, 256, 512])
    st_ssm = din("st_ssm", [DEPTH, NSEG, 64, 64, 128])
    st_conv = din("st_conv", [DEPTH, NSEG, 3, 6144])
    w_ada = din("w_ada", [DEPTH, D, 6 * D])
    b_ada = din("b_ada", [DEPTH, 6 * D])
    w_in = din("w_in", [DEPTH, D, NIN])
    w_gate = din("w_gla_gate", [DEPTH, 16, 1024])
    b_gate = din("b_gla_gate", [DEPTH, 1024])
    gla_norm = din("gla_norm", [DEPTH, 512])
    w_gproj = din("w_gla_proj", [DEPTH, D, D])
    conv_w = din("conv_w", [DEPTH, 4, 6144])
    conv_b = din("conv_b", [DEPTH, 6144])
    dt_bias = din("dt_bias", [DEPTH, 64])
    A_log = din("A_log", [DEPTH, 64])
    d_skip = din("d_skip", [DEPTH, 64])
    ssd_norm = din("ssd_norm", [DEPTH, 4096])
    w_sproj = din("w_ssd_proj", [DEPTH, 4096, D])
    w_mix = din("w_mix_out", [DEPTH, D, D])
    w_fin = din("w_ffn_in", [DEPTH, D, 2 * DFF])
    w_fout = din("w_ffn_out", [DEPTH, DFF, D])
    fnorm = din("final_norm", [1, D])

    y_out = dout("y", [TOK, D])
    gla_p = dout("gla_p", [DEPTH, 4, 256, 512])
    ssm_p = dout("ssm_p", [DEPTH, 64, 64, 128])
    conv_p = dout("conv_p", [DEPTH, 3, 6144])
    gla_s = dout("gla_s", [DEPTH, NSEG, 4, 256, 512])
    ssm_s = dout("ssm_s", [DEPTH, NSEG, 64, 64, 128])
    conv_s = dout("conv_s", [DEPTH, NSEG, 3, 6144])

    proj = [dint("proj%d" % l, [TOK, NIN]) for l in range(DEPTH)]
    xbp = [dint("xbp%d" % l, [3 + NPT * 128, 6144]) for l in range(DEPTH)]
    xbs = [dint("xbs%d" % l, [NSEG, 3 + SL, 6144]) for l in range(DEPTH)]
    xcs = [dint("xc%d" % l, [TOK, 6144]) for l in range(DEPTH)]
    mod = [dint("mod%d" % l, [2, 128, 6 * D]) for l in range(DEPTH)]
    xv = [[dint("xv%d_%d" % (l, v), [TOK, D]) for v in range(2)] for l in range(DEPTH)]
    oaT = [dint("oaT%d" % l, [NT, 128, 16, 128], BF16) for l in range(DEPTH)]
    ybT = [dint("ybT%d" % l, [NT, 128, 32, 128], BF16) for l in range(DEPTH)]
    actT = [dint("actT%d" % l, [NT, 128, 44, 128], BF16) for l in range(DEPTH)]
    o_loc = [dint("oloc%d" % l, [NPT * 128, 2048]) for l in range(DEPTH)]
    qg_d = [dint("qg%d" % l, [NPT, 128, 8, 128], BF16) for l in range(DEPTH)]
    y_loc = [dint("yloc%d" % l, [NPT * 128, 4096]) for l in range(DEPTH)]
    ct_d = [dint("ct%d" % l, [NPT, 128, 8, 128], BF16) for l in range(DEPTH)]
    eg_d = [dint("eg%d" % l, [NPT, 128, 64]) for l in range(DEPTH)]
    sloc_d = [dint("sloc%d" % l, [128, 4096]) for l in range(DEPTH)]
    gam_d = [dint("gam%d" % l, [128, 8]) for l in range(DEPTH)]
    hloc_d = [dint("hloc%d" % l, [128, 4096]) for l in range(DEPTH)]
    gamh_d = [dint("gamh%d" % l, [128, 64]) for l in range(DEPTH)]
    cv_snd = [nc.dram_tensor("cvsnd%d" % l, [128, 144], F32) for l in range(DEPTH)]
    cv_rcv = [nc.dram_tensor("cvrcv%d" % l, [128, 144], F32) for l in range(DEPTH)]
    gs_snd = [nc.dram_tensor("gssnd%d" % l, [128, 4096], F32) for l in range(DEPTH)]
    gs_rcv = [nc.dram_tensor("gsrcv%d" % l, [128, 4096], F32) for l in range(DEPTH)]
    hs_snd = [nc.dram_tensor("hssnd%d" % l, [128, 4096], F32) for l in range(DEPTH)]
    hs_rcv = [nc.dram_tensor("hsrcv%d" % l, [128, 4096], F32) for l in range(DEPTH)]

    with ExitStack() as es:
        S = Sched(nc, es)
        cst = S.sb(es, "cst", [128, NCONST, 128], F32, dma=True)
        identb = S.sb(es, "identb", [128, 128], BF16)
        S.dma("sp", cst[:], consts, sb_w=cst)
        flg = S.sb(es, "flg", [128, 2], F32, dma=True)
        S.dma("sp", flg[:], flags, sb_w=flg)
        S.cp("dve", identb[:], cst[:, K_ID, :], [cst], [identb])
        ident = cst[:, K_ID, :]

        def rows(i):
            return slice(i * 128, (i + 1) * 128)

        def grp(i):
            return 0 if i < NPT else 1

        def hgi(i):
            return (i // 4, (i % 4) * 128) if i < NPT else (NG, 0)

        def hview(hT, i, k):
            g_, o_ = hgi(i)
            return hT[g_][:, k, o_:o_ + 128]

        def phase_mod():
            with ExitStack() as pes:
                cct = S.sb(pes, "cct", [128, D], F32, dma=True)
                scb = S.sb(pes, "scb", [128, D], BF16)
                scT = [S.sb(pes, "scT%d" % g, [128, 16, 128], BF16) for g in range(2)]
                ptr = [S.ps(pes, "mptr%d" % i, [128, 8, 128], BF16) for i in range(2)]
                pm = [S.ps(pes, "pm%d" % i, [128, 512], F32) for i in range(4)]
                for g in range(2):
                    S.dma("sp", cct[:], cc[g], sb_w=cct)
                    S.act(scb[:], cct[:], AF.Silu, [cct], [scb])
                    for half in range(2):
                        p = ptr[half]
                        for k in range(8):
                            kk = half * 8 + k
                            S.tr(p[:, k, :], scb[:, kk * 128:(kk + 1) * 128], identb[:], [scb, identb], [p])
                        S.cp("dve" if half == 0 else "act", scT[g][:, half * 8:(half + 1) * 8, :], p[:], [p], [scT[g]])
                wa = [S.sb(pes, "wa%d" % i, [128, 16, 512], BF16, dma=True) for i in range(2)]
                bb = [S.sb(pes, "bb%d" % i, [128, 512], F32, dma=True) for i in range(2)]
                stg = [S.sb(pes, "mst%d" % i, [128, 512], F32, dma=True) for i in range(4)]
                it = 0
                for l in range(DEPTH):
                    wv = w_ada[l].rearrange("(kt p) n -> p kt n", p=128)
                    for cb in range(24):
                        cs_ = slice(cb * 512, (cb + 1) * 512)
                        w = wa[it % 2]
                        b = bb[it % 2]
                        S.dma("pool", w[:], wv[:, :, cs_], sb_w=w)
                        S.dma("sp", b[:], b_ada[l:l + 1, cs_].partition_broadcast(128), sb_w=b)
                        for g in range(2):
                            n = it * 2 + g
                            p = pm[n % 4]
                            s_ = stg[n % 4]
                            for k in range(16):
                                S.mm(p[:], scT[g][:, k, :], w[:, k, :], k == 0, k == 15, [scT[g], w], [p])
                            S.tt("dve", s_[:], p[:], b[:], ALU.add, [p, b], [s_])
                            S.dma("sp", mod[l][g, :, cs_], s_[:], sb_r=s_)
                        it += 1
            S.barrier()

        def phase_norm(xsrc, l, sh_i, sc_i, hT):
            with ExitStack() as pes:
                xt = [S.sb(pes, "nx%d" % i, [128, D], F32, dma=True) for i in range(2)]
                shb = S.sb(pes, "shb", [128, D], F32, dma=True)
                scb = S.sb(pes, "nscb", [128, D], F32, dma=True)
                junk = S.sb(pes, "njunk", [128, D], F32)
                ss = S.sb(pes, "nss", [128, 2], F32)
                hb = [S.sb(pes, "hb%d" % i, [128, D], BF16) for i in range(2)]
                ptr = [S.ps(pes, "nptr%d" % i, [128, 8, 128], BF16) for i in range(2)]
                n = 0
                for i in range(NT):
                    g = grp(i)
                    if i == 0 or i == NPT:
                        S.dma("sp", shb[:], mod[l][g, :, sh_i * D:(sh_i + 1) * D], sb_w=shb)
                        S.dma("sp", scb[:], mod[l][g, :, sc_i * D:(sc_i + 1) * D], sb_w=scb)
                        S.ts("dve", scb[:], scb[:], 1.0, None, ALU.add, None, [scb], [scb])
                    x = xt[i % 2]
                    S.dma("sp", x[:], xsrc[rows(i), :], sb_w=x)
                    S.memset("pool", ss[:], 0.0, [ss])
                    S.act(junk[:], x[:], AF.Square, [x, ss], [junk, ss], accum=ss[:, 0:1])
                    S.ts("dve", ss[:, 1:2], ss[:, 0:1], 1.0 / D, EPS, ALU.mult, ALU.add, [ss], [ss])
                    S.act(ss[:, 1:2], ss[:, 1:2], AF.Sqrt, [ss], [ss])
                    S.recip(ss[:, 1:2], ss[:, 1:2], [ss], [ss])
                    S.stt("dve", junk[:], x[:], ss[:, 1:2], scb[:], ALU.mult, ALU.mult, [x, ss, scb], [junk])
                    h = hb[i % 2]
                    S.tt("dve", h[:], junk[:], shb[:], ALU.add, [junk, shb], [h])
                    for half in range(2):
                        p = ptr[n % 2]
                        n += 1
                        for k in range(8):
                            kk = half * 8 + k
                            S.tr(p[:, k, :], h[:, kk * 128:(kk + 1) * 128], identb[:], [h, identb], [p])
                        g_, o_ = hgi(i)
                        S.cp("act" if half == 0 else "dve", hT[g_][:, half * 8:(half + 1) * 8, o_:o_ + 128],
                             p[:], [p], [hT[g_]])
            S.barrier()

        def phase_proj(l, hT):
            with ExitStack() as pes:
                wb = [S.sb(pes, "wb%d" % i, [128, 16, 512], BF16, dma=True) for i in range(2)]
                pp = [S.ps(pes, "pp%d" % i, [128, 512], F32) for i in range(4)]
                stg = [S.sb(pes, "pst%d" % i, [128, 512], F32, dma=True) for i in range(4)]
                dd = S.dbuf(pes, "dd_conv")
                S.dma("sp", xbs[l][:, 0:3, :], st_conv[l], evbuf=dd)
                wv = w_in[l].rearrange("(kt p) n -> p kt n", p=128)
                n = 0
                for bi, (c0, w) in enumerate(w_blocks()):
                    W = wb[bi % 2]
                    S.dma("pool", W[:, :, 0:w], wv[:, :, c0:c0 + w], sb_w=W)
                    isx = C_XBC <= c0 < C_DT
                    for i in range(NT):
                        p = pp[n % 4]
                        s_ = stg[n % 4]
                        n += 1
                        for k in range(16):
                            S.mm(p[:, 0:w], hview(hT, i, k), W[:, k, 0:w], k == 0, k == 15, [hT[hgi(i)[0]], W], [p])
                        S.cp("act" if n % 2 else "dve", s_[:, 0:w], p[:, 0:w], [p], [s_])
                        if isx:
                            xc0 = c0 - C_XBC
                            if i < NPT:
                                S.dma("sp", xbp[l][3 + i * 128:3 + (i + 1) * 128, xc0:xc0 + w], s_[:, 0:w], sb_r=s_)
                            else:
                                S.dma("sp", xbs[l][:, 3:3 + SL, xc0:xc0 + w], s_[:, 0:w], sb_r=s_)
                        else:
                            S.dma("sp", proj[l][rows(i), c0:c0 + w], s_[:, 0:w], sb_r=s_)
            S.barrier()
            fl3 = lambda ap: ap.rearrange("r c -> (r c)").rearrange("(p f) -> p f", p=128)
            with ExitStack() as pes:
                zt = S.sb(pes, "zt", [128, 144], F32, dma=True)
                S.dma("sp", zt[:], fl3(xbp[l][NPT * 128:NPT * 128 + 3, :]), sb_w=zt)
                S.ts("dve", zt[:], zt[:], flg[:, 1:2], None, ALU.mult, None, [zt, flg], [zt])
                S.dma("sp", cv_snd[l].ap(), zt[:], sb_r=zt)
            S.collective(cv_snd[l], cv_rcv[l])
            with ExitStack() as pes:
                zt = S.sb(pes, "zt2", [128, 144], F32, dma=True)
                S.dma("sp", zt[:], cv_rcv[l].ap(), sb_w=zt)
                S.ts("dve", zt[:], zt[:], flg[:, 0:1], None, ALU.mult, None, [zt, flg], [zt])
                S.dma("sp", fl3(xbp[l][0:3, :]), zt[:], sb_r=zt)
                dd = S.dbuf(pes, "dd_conv2")
                S.dma("sp", conv_p[l], xbp[l][NPT * 128:NPT * 128 + 3, :], evbuf=dd)
                S.dma("sp", conv_s[l], xbs[l][:, SL:SL + 3, :], evbuf=dd)

        def phase_conv(l):
            CH = 1536
            with ExitStack() as pes:
                cw = S.sb(pes, "cw", [128, 4, CH], F32, dma=True)
                cbs = S.sb(pes, "cbs", [128, CH], F32, dma=True)
                xs_ = [[S.sb(pes, "cx%d_%d" % (b, i), [128, CH], F32, dma=True) for i in range(4)] for b in range(2)]
                acc = [S.sb(pes, "cacc%d" % b, [128, CH], F32, dma=True) for b in range(2)]
                for c in range(4):
                    cs_ = slice(c * CH, (c + 1) * CH)
                    for t4 in range(4):
                        S.dma("sp", cw[:, t4, :], conv_w[l, t4:t4 + 1, cs_].partition_broadcast(128), sb_w=cw)
                    S.dma("sp", cbs[:], conv_b[l:l + 1, cs_].partition_broadcast(128), sb_w=cbs)
                    for i in range(NT):
                        X = xs_[i % 2]
                        a = acc[i % 2]
                        XA = Buf("xa", None)
                        XB = Buf("xb", None)
                        for t4 in range(4):
                            if i < NPT:
                                src = xbp[l][i * 128 + t4:i * 128 + t4 + 128, cs_]
                            else:
                                src = xbs[l][:, t4:t4 + SL, cs_]
                            S.dma("sp", X[t4][:], src, sb_w=X[t4])
                        for (eng_, c0_, c1_, Y) in (("dve", 0, 1024, XA), ("pool", 1024, CH, XB)):
                            sl_ = slice(c0_, c1_)
                            S.tt(eng_, X[3][:, sl_], X[3][:, sl_], cw[:, 3, sl_], ALU.mult, [X[3], cw], [Y])
                            for t4 in (2, 1, 0):
                                S.tt(eng_, X[t4][:, sl_], X[t4][:, sl_], cw[:, t4, sl_], ALU.mult, [X[t4], cw, Y], [Y])
                                S.tt(eng_, X[3][:, sl_], X[3][:, sl_], X[t4][:, sl_], ALU.add, [Y], [Y])
                            S.tt(eng_, X[3][:, sl_], X[3][:, sl_], cbs[:, sl_], ALU.add, [Y, cbs], [Y])
                        S.act(a[:], X[3][:], AF.Silu, [X[3], XA, XB], [a])
                        for t4 in range(4):
                            X[t4].rd.update(XA.wr)
                            X[t4].rd.update(XB.wr)
                        S.dma("act", xcs[l][rows(i), cs_], a[:], sb_r=a)
            S.barrier()

        def gla_post(l, i, osrc, obufs, rap, rbuf, gln, junk, ss, oa, oTb, nb):
            S.memset("pool", ss[:], 0.0, [ss])
            for h in range(4):
                S.act(junk[:], osrc[h], AF.Square, [obufs[h], ss], [junk, ss], accum=ss[:, h:h + 1])
            S.ts("dve", ss[:, 4:8], ss[:, 0:4], 1.0 / 512, EPS, ALU.mult, ALU.add, [ss], [ss])
            S.act(ss[:, 4:8], ss[:, 4:8], AF.Sqrt, [ss], [ss])
            S.recip(ss[:, 4:8], ss[:, 4:8], [ss], [ss])
            S.act(rap, rap, AF.Silu, [rbuf], [rbuf])
            S.tt("pool", rap.rearrange("p (h v) -> p h v", h=4), rap.rearrange("p (h v) -> p h v", h=4),
                 gln[:].unsqueeze(1).to_broadcast([128, 4, 512]), ALU.mult, [rbuf, gln], [rbuf])
            for h in range(4):
                S.stt("dve", oa[:, h * 512:(h + 1) * 512], osrc[h], ss[:, 4 + h:5 + h],
                      rap[:, h * 512:(h + 1) * 512], ALU.mult, ALU.mult, [obufs[h], ss, rbuf], [oa])
            for q4 in range(4):
                p = nb()
                for c in range(4):
                    cc_ = q4 * 4 + c
                    S.tr(p[:, c * 128:(c + 1) * 128], oa[:, cc_ * 128:(cc_ + 1) * 128], ident, [oa, cst], [p])
                S.cp("act", oTb[:, q4 * 4:(q4 + 1) * 4, :], p[:].rearrange("p (c t) -> p c t", c=4), [p], [oTb])
            S.dma("act", oaT[l][i], oTb[:], sb_r=oTb)

        def phase_gla(l, sample):
            tiles = [NPT] if sample else list(range(NPT))
            K_TRI = K_TRIS if sample else K_TRIP
            K_NTRI = K_NTRIS if sample else K_NTRIP
            K_SEL = K_SELS if sample else K_SELP
            nseg = NSEG if sample else 1
            with ExitStack() as pes:
                pin = S.sb(pes, "gpin", [128, 6160], F32, dma=True)
                wga = S.sb(pes, "wga", [33, 1024], F32, dma=True)
                gln = S.sb(pes, "gln", [128, 512], F32, dma=True)
                glrT = S.sb(pes, "glrT", [33, 128], F32)
                l1 = S.sb(pes, "gl1", [128, 1024], F32)
                btok = S.sb(pes, "gbtok", [128, 1024], F32)
                kkb = S.sb(pes, "gkkb", [128, 1024], BF16)
                eb = S.sb(pes, "geb", [128, 8, 128], F32)
                enb = S.sb(pes, "genb", [128, 8, 128], F32)
                qTb = S.sb(pes, "gqTb", [128, 8, 128], BF16)
                kTb = S.sb(pes, "gkTb", [128, 8, 128], BF16)
                attb = S.sb(pes, "gattb", [128, 4, 128], BF16)
                vb = S.sb(pes, "gvb", [128, 2048], BF16)
                junk = S.sb(pes, "gjunk", [128, 512], F32)
                ss = S.sb(pes, "gss", [128, 8], F32)
                oa = S.sb(pes, "goa", [128, 2048], F32, dma=True)
                oTb = S.sb(pes, "goTb", [128, 16, 128], BF16, dma=True)
                if not sample:
                    qgb = S.sb(pes, "gqgb", [128, 8, 128], BF16, dma=True)
                    gam = S.sb(pes, "ggam", [128, 8], F32, dma=True)
                    S.memset("pool", gam[:], 1.0, [gam])
                Sb = S.sb(pes, "gSb", [128, 8, 512], BF16)
                if sample:
                    Sst = [S.sb(pes, "gS%d" % i, [128, 8, 512], F32, dma=True) for i in range(3)]
                    Sb2 = S.sb(pes, "gSb2", [128, 8, 512], BF16)
                    qm = [S.sb(pes, "gqm%d" % i, [128, 8, 128], BF16) for i in range(2)]
                    kkm = [S.sb(pes, "gkkm%d" % i, [128, 1024], BF16) for i in range(2)]
                    for b_ in qm:
                        S.memset("pool", b_[:], 0.0, [b_])
                    Sbs = [Sb, Sb2]
                else:
                    Sst = [S.sb(pes, "gS", [128, 8, 512], F32, dma=True)]
                    S.memset("pool", Sst[0][:], 0.0, [Sst[0]])
                    S.memset("pool", Sb[:], 0.0, [Sb])
                P = [S.ps(pes, "gP%d" % i, [128, 512], F32) for i in range(8)]
                rot = [0]

                def nb():
                    b_ = P[rot[0] % 4]
                    rot[0] += 1
                    return b_
                PO = P[4:8]
                S.memset("pool", wga[:], 0.0, [wga])
                S.dma("sp", wga[0:16, :], w_gate[l], sb_w=wga)
                S.dma("sp", wga[32:33, :], b_gate[l:l + 1, :], sb_w=wga)
                S.dma("sp", gln[:], gla_norm[l:l + 1, :].partition_broadcast(128), sb_w=gln)
                S.memset("pool", glrT[:], 0.0, [glrT])
                S.memset("pool", glrT[32:33, :], 1.0, [glrT])
                tri = cst[:, K_TRI, :]
                ntri = cst[:, K_NTRI, :]
                selI = cst[:, K_SEL, :]
                for i in tiles:
                    S.dma("sp", pin[:], proj[l][rows(i), 0:6160], sb_w=pin)
                    q_ = lambda c: pin[:, C_Q + c * 128:C_Q + (c + 1) * 128]
                    k_ = lambda c: pin[:, C_K + c * 128:C_K + (c + 1) * 128]
                    p = nb()
                    S.tr(p[0:16, 0:128], pin[:, C_GLR:C_GLR + 16], ident, [pin, cst], [p])
                    S.cp("dve", glrT[0:16, :], p[0:16, 0:128], [p], [glrT])
                    for hf in range(2):
                        p = nb()
                        S.mm(p[:], glrT[:, :], wga[:, hf * 512:(hf + 1) * 512], True, True, [glrT, wga], [p])
                        S.act(l1[:, hf * 512:(hf + 1) * 512], p[:], AF.Exp, [p], [l1], scale=-1.0)
                    S.act(l1[:], l1[:], AF.Ln, [l1], [l1], bias=1.0)
                    for hf in range(2):
                        p = nb()
                        S.mm(p[:], ntri, l1[:, hf * 512:(hf + 1) * 512], True, True, [cst, l1], [p])
                        S.cp("act", btok[:, hf * 512:(hf + 1) * 512], p[:], [p], [btok])
                    for hf in range(2):
                        p = nb()
                        for c in range(4):
                            cc_ = hf * 4 + c
                            S.mm(p[:, c * 128:(c + 1) * 128], l1[:, cc_ * 128:(cc_ + 1) * 128], ntri, True, True, [l1, cst], [p])
                        pv = p[:].rearrange("p (c t) -> p c t", c=4)
                        S.act(eb[:, hf * 4:(hf + 1) * 4, :], pv, AF.Exp, [p], [eb])
                        S.act(enb[:, hf * 4:(hf + 1) * 4, :], pv, AF.Exp, [p], [enb], scale=-1.0)
                    for hf in range(2):
                        p = nb()
                        S.mm(p[:], selI, btok[:, hf * 512:(hf + 1) * 512], True, True, [cst, btok], [p])
                        S.act(l1[:, hf * 512:(hf + 1) * 512], p[:], AF.Exp, [p], [l1])
                    S.tt("dve", kkb[:], pin[:, C_K:C_K + 1024], l1[:], ALU.mult, [pin, l1], [kkb])
                    for hf in range(2):
                        p = nb()
                        for c in range(4):
                            S.tr(p[:, c * 128:(c + 1) * 128], q_(hf * 4 + c), ident, [pin, cst], [p])
                        S.stt("dve", qTb[:, hf * 4:(hf + 1) * 4, :], p[:].rearrange("p (c t) -> p c t", c=4), 0.0625,
                              eb[:, hf * 4:(hf + 1) * 4, :], ALU.mult, ALU.mult, [p, eb], [qTb])
                        p = nb()
                        for c in range(4):
                            S.tr(p[:, c * 128:(c + 1) * 128], k_(hf * 4 + c), ident, [pin, cst], [p])
                        S.tt("dve", kTb[:, hf * 4:(hf + 1) * 4, :], p[:].rearrange("p (c t) -> p c t", c=4),
                             enb[:, hf * 4:(hf + 1) * 4, :], ALU.mult, [p, enb], [kTb])
                    p = nb()
                    for h in range(4):
                        for dk in range(2):
                            S.mm(p[:, h * 128:(h + 1) * 128], kTb[:, h * 2 + dk, :], qTb[:, h * 2 + dk, :], dk == 0, dk == 1, [kTb, qTb], [p])
                    S.tt("dve", attb[:], p[:].rearrange("p (h t) -> p h t", h=4), tri.unsqueeze(1).to_broadcast([128, 4, 128]),
                         ALU.mult, [p, cst], [attb])
                    S.cp("act", vb[:], pin[:, C_V:C_V + 2048], [pin], [vb])
                    for h in range(4):
                        S.mm(PO[h][:], attb[:, h, :], vb[:, h * 512:(h + 1) * 512], True, False, [attb, vb], [PO[h]])
                    for j in range(nseg):
                        if sample:
                            Sj = Sst[j % 3]
                            if j == 0:
                                S.dma("sp", Sj[:].rearrange("p (h t) v -> p h t v", h=4),
                                      st_gla[l, 0].rearrange("h (t p) v -> p h t v", p=128), sb_w=Sj)
                            if j + 1 < nseg:
                                Sn = Sst[(j + 1) % 3]
                                S.dma("sp", Sn[:].rearrange("p (h t) v -> p h t v", h=4),
                                      st_gla[l, j + 1].rearrange("h (t p) v -> p h t v", p=128), sb_w=Sn)
                            Sb = Sbs[j % 2]
                            S.cp("pool", Sb[:], Sj[:], [Sj], [Sb])
                            qmj = qm[j % 2]
                            if j >= 2:
                                jo = j - 2
                                S.memset("pool", qmj[:, :, jo * SL:(jo + 1) * SL], 0.0, [qmj])
                            S.cp("pool", qmj[:, :, j * SL:(j + 1) * SL], qTb[:, :, j * SL:(j + 1) * SL], [qTb], [qmj])
                            kkj = kkm[j % 2]
                            S.ts("dve", kkj[:], kkb[:], cst[:, K_SEGCOL, j:j + 1], None, ALU.mult, None, [kkb, cst], [kkj])
                            ql, kl = qmj, kkj
                        else:
                            Sj = Sst[0]
                            ql, kl = qTb, kkb
                        last = (j == nseg - 1)
                        for h in range(4):
                            for dk in range(2):
                                S.mm(PO[h][:], ql[:, h * 2 + dk, :], Sb[:, h * 2 + dk, :], False, last and dk == 1, [ql, Sb], [PO[h]])
                        tl = (j * SL + SL - 1) if sample else 127
                        for c in range(8):
                            h = c // 2
                            p = nb()
                            S.mm(p[:], kl[:, c * 128:(c + 1) * 128], vb[:, h * 512:(h + 1) * 512], True, True, [kl, vb], [p])
                            S.stt("dve", Sj[:, c, :], Sj[:, c, :], eb[:, c, tl:tl + 1], p[:], ALU.mult, ALU.add, [Sj, eb, p], [Sj])
                        if sample:
                            S.dma("act", gla_s[l, j].rearrange("h (t p) v -> p h t v", p=128),
                                  Sj[:].rearrange("p (h t) v -> p h t v", h=4), sb_r=Sj)
                        else:
                            S.cp("pool", Sb[:], Sj[:], [Sj], [Sb])
                    if sample:
                        gla_post(l, i, [PO[h][:] for h in range(4)], PO, pin[:, C_R:C_R + 2048], pin, gln, junk, ss, oa, oTb, nb)
                    else:
                        for h in range(4):
                            S.cp("act", oa[:, h * 512:(h + 1) * 512], PO[h][:], [PO[h]], [oa])
                        S.dma("sp", o_loc[l][rows(i), :], oa[:], sb_r=oa)
                        S.tt("dve", qgb[:], qTb[:], gam[:].unsqueeze(2).to_broadcast([128, 8, 128]), ALU.mult, [qTb, gam], [qgb])
                        S.dma("sp", qg_d[l][i], qgb[:], sb_r=qgb)
                        S.tt("dve", gam[:], gam[:], eb[:, :, 127], ALU.mult, [gam, eb], [gam])
                if not sample:
                    Sj = Sst[0]
                    S.dma("sp", sloc_d[l], Sj[:].rearrange("p c v -> p (c v)"), sb_r=Sj)
                    S.dma("sp", gam_d[l], gam[:], sb_r=gam)
                    S.ts("dve", Sj[:], Sj[:], flg[:, 1:2], None, ALU.mult, None, [Sj, flg], [Sj])
                    S.dma("sp", gs_snd[l].ap(), Sj[:].rearrange("p c v -> p (c v)"), sb_r=Sj)
            if not sample:
                S.collective(gs_snd[l], gs_rcv[l])
            else:
                S.barrier()

        def phase_gla2(l):
            with ExitStack() as pes:
                Sin = S.sb(pes, "g2Sin", [128, 8, 512], F32, dma=True)
                Sloc = S.sb(pes, "g2Sloc", [128, 8, 512], F32, dma=True)
                Sinb = S.sb(pes, "g2Sinb", [128, 8, 512], BF16)
                gam = S.sb(pes, "g2gam", [128, 8], F32, dma=True)
                gln = S.sb(pes, "g2gln", [128, 512], F32, dma=True)
                ot = [S.sb(pes, "g2ot%d" % i, [128, 2048], F32, dma=True) for i in range(2)]
                rt = [S.sb(pes, "g2rt%d" % i, [128, 2048], F32, dma=True) for i in range(2)]
                qgt = [S.sb(pes, "g2qg%d" % i, [128, 8, 128], BF16, dma=True) for i in range(2)]
                junk = S.sb(pes, "g2junk", [128, 512], F32)
                ss = S.sb(pes, "g2ss", [128, 8], F32)
                oa = S.sb(pes, "g2oa", [128, 2048], F32)
                oTb = S.sb(pes, "g2oTb", [128, 16, 128], BF16, dma=True)
                P = [S.ps(pes, "g2P%d" % i, [128, 512], F32) for i in range(8)]
                rot = [0]

                def nb():
                    b_ = P[rot[0] % 4]
                    rot[0] += 1
                    return b_
                S.dma("sp", gln[:], gla_norm[l:l + 1, :].partition_broadcast(128), sb_w=gln)
                S.dma("sp", Sin[:].rearrange("p c v -> p (c v)"), gs_rcv[l].ap(), sb_w=Sin)
                S.dma("sp", Sloc[:].rearrange("p c v -> p (c v)"), sloc_d[l], sb_w=Sloc)
                S.dma("sp", gam[:], gam_d[l], sb_w=gam)
                S.ts("dve", Sin[:], Sin[:], flg[:, 0:1], None, ALU.mult, None, [Sin, flg], [Sin])
                S.cp("pool", Sinb[:], Sin[:], [Sin], [Sinb])
                for c in range(8):
                    S.stt("dve", Sloc[:, c, :], Sin[:, c, :], gam[:, c:c + 1], Sloc[:, c, :], ALU.mult, ALU.add, [Sin, gam, Sloc], [Sloc])
                S.dma("sp", gla_p[l].rearrange("h (t p) v -> p h t v", p=128), Sloc[:].rearrange("p (h t) v -> p h t v", h=4), sb_r=Sloc)
                for i in range(NPT):
                    o_ = ot[i % 2]
                    r_ = rt[i % 2]
                    q_ = qgt[i % 2]
                    S.dma("sp", o_[:], o_loc[l][rows(i), :], sb_w=o_)
                    S.dma("sp", r_[:], proj[l][rows(i), C_R:C_R + 2048], sb_w=r_)
                    S.dma("sp", q_[:], qg_d[l][i], sb_w=q_)
                    for h in range(4):
                        for dk in range(2):
                            S.mm(P[4 + h][:], q_[:, h * 2 + dk, :], Sinb[:, h * 2 + dk, :], dk == 0, dk == 1, [q_, Sinb], [P[4 + h]])
                        S.tt("dve", o_[:, h * 512:(h + 1) * 512], o_[:, h * 512:(h + 1) * 512], P[4 + h][:], ALU.add, [o_, P[4 + h]], [o_])
                    gla_post(l, i, [o_[:, h * 512:(h + 1) * 512] for h in range(4)], [o_] * 4, r_[:], r_, gln, junk, ss, oa, oTb, nb)
            S.barrier()

        def ssd_post(l, i, yy, xap, xbuf, zt, dsk, ssn, junk, ss, yTb, nb):
            xv_ = xap.rearrange("p (h d) -> p h d", h=64)
            S.tt("pool", xv_, xv_, dsk[:].unsqueeze(2).to_broadcast([128, 64, 64]), ALU.mult, [xbuf, dsk], [xbuf])
            S.tt("pool", yy[:], yy[:], xap, ALU.add, [yy, xbuf], [yy])
            S.act(zt[:], zt[:], AF.Silu, [zt], [zt])
            S.tt("dve", yy[:], yy[:], zt[:], ALU.mult, [yy, zt], [yy])
            S.memset("pool", ss[:], 0.0, [ss])
            for g in range(8):
                S.act(junk[:], yy[:, g * 512:(g + 1) * 512], AF.Square, [yy, ss], [junk, ss], accum=ss[:, g:g + 1])
            S.ts("dve", ss[:, 8:16], ss[:, 0:8], 1.0 / 512, EPS, ALU.mult, ALU.add, [ss], [ss])
            S.act(ss[:, 8:16], ss[:, 8:16], AF.Sqrt, [ss], [ss])
            S.recip(ss[:, 8:16], ss[:, 8:16], [ss], [ss])
            for g in range(8):
                S.stt("dve", yy[:, g * 512:(g + 1) * 512], yy[:, g * 512:(g + 1) * 512], ss[:, 8 + g:9 + g],
                      ssn[:, g * 512:(g + 1) * 512], ALU.mult, ALU.mult, [yy, ss, ssn], [yy])
            for q4 in range(8):
                p = nb()
                for c in range(4):
                    cc_ = q4 * 4 + c
                    S.tr(p[:, c * 128:(c + 1) * 128], yy[:, cc_ * 128:(cc_ + 1) * 128], ident, [yy, cst], [p])
                S.cp("act" if q4 % 2 else "dve", yTb[:, q4 * 4:(q4 + 1) * 4, :], p[:].rearrange("p (c t) -> p c t", c=4), [p], [yTb])
            S.dma("act", ybT[l][i], yTb[:], sb_r=yTb)

        def phase_ssd(l, sample):
            tiles = [NPT] if sample else list(range(NPT))
            K_TRI = K_TRIS if sample else K_TRIP
            K_SEL = K_SELS if sample else K_SELP
            K_NEGM = K_NEGS if sample else K_NEGP
            nseg = NSEG if sample else 1
            with ExitStack() as pes:
                xc = S.sb(pes, "dxc", [128, 6144], F32, dma=True)
                zt = S.sb(pes, "dz", [128, 4096], F32, dma=True)
                dtt = S.sb(pes, "ddt", [128, 64], F32, dma=True)
                dtb = S.sb(pes, "ddtb", [128, 64], F32, dma=True)
                aneg = S.sb(pes, "daneg", [128, 64], F32, dma=True)
                dsk = S.sb(pes, "ddsk", [128, 64], F32, dma=True)
                ssn = S.sb(pes, "dssn", [128, 4096], F32, dma=True)
                negm8 = S.sb(pes, "dnegm8", [128, 4, 128], F32)
                sm = S.sb(pes, "dsm", [128, 10, 64], F32)
                xsb = S.sb(pes, "dxsb", [128, 4096], BF16)
                Bb = S.sb(pes, "dBb", [128, 1024], BF16)
                BTb = S.sb(pes, "dBTb", [128, 8, 128], BF16)
                CTb = S.sb(pes, "dCTb", [128, 8, 128], BF16, dma=True)
                xdl = S.sb(pes, "dxdl", [128, 4096], BF16)
                nd_ = 1 if sample else 2
                diag = [S.sb(pes, "ddiag%d" % i, [128, 8, 128], F32) for i in range(nd_)]
                seg = [S.sb(pes, "dseg%d" % i, [128, 8, 128], F32) for i in range(nd_)]
                WTb = [S.sb(pes, "dWTb%d" % i, [128, 8, 128], BF16) for i in range(nd_)]
                yy = S.sb(pes, "dy", [128, 4096], F32, dma=True)
                tmp = S.sb(pes, "dtmp", [128, 512], F32)
                junk = S.sb(pes, "djunk", [128, 512], F32)
                ss = S.sb(pes, "dss", [128, 16], F32)
                yTb = S.sb(pes, "dyTb", [128, 32, 128], BF16, dma=True)
                hT = S.sb(pes, "dhT", [128, 4096], F32, dma=True) if not sample else None
                hTb = S.sb(pes, "dhTb", [128, 4096], BF16)
                if sample:
                    hjs = [S.sb(pes, "dhj%d" % i, [128, 32, 128], F32, dma=True) for i in range(2)]
                    hj = hjs[0]
                    hjb = S.sb(pes, "dhjb", [128, 32, 128], BF16)
                    decc = S.sb(pes, "ddecc", [128, 32], F32)
                    cm = [S.sb(pes, "dcm%d" % i, [128, 8, 128], BF16) for i in range(2)]
                    Bm = [S.sb(pes, "dBm%d" % i, [128, 1024], BF16) for i in range(2)]
                    for b_ in cm:
                        S.memset("pool", b_[:], 0.0, [b_])
                else:
                    hj = yy
                    egt = S.sb(pes, "degt", [128, 64], F32, dma=True)
                    gamh = S.sb(pes, "dgamh", [128, 64], F32, dma=True)
                    S.memset("pool", gamh[:], 1.0, [gamh])
                    S.memset("pool", hT[:], 0.0, [hT])
                    S.memset("pool", hTb[:], 0.0, [hTb])
                NPB = 6 if sample else 8
                P = [S.ps(pes, "dP%d" % i, [128, 512], F32) for i in range(NPB)]
                if sample:
                    PB = [S.ps(pes, "dPB%d" % i, [128, 8, 128], BF16) for i in range(2)]
                rot = [0]

                def nb():
                    b_ = P[rot[0] % NPB]
                    rot[0] += 1
                    return b_
                DTP, LND, AA, CS, CS2, ECS, DL, DEC, T1, T2 = range(10)
                S.dma("sp", dtb[:], dt_bias[l:l + 1, :].partition_broadcast(128), sb_w=dtb)
                S.dma("sp", aneg[:], A_log[l:l + 1, :].partition_broadcast(128), sb_w=aneg)
                S.dma("sp", dsk[:], d_skip[l:l + 1, :].partition_broadcast(128), sb_w=dsk)
                S.dma("sp", ssn[:], ssd_norm[l:l + 1, :].partition_broadcast(128), sb_w=ssn)
                S.act(aneg[:], aneg[:], AF.Exp, [aneg], [aneg])
                S.ts("dve", aneg[:], aneg[:], -1.0, None, ALU.mult, None, [aneg], [aneg])
                for c in range(4):
                    S.cp("dve", negm8[:, c, :], cst[:, K_NEGM, :], [cst], [negm8])
                tri = cst[:, K_TRI, :]
                selI = cst[:, K_SEL, :]
                ones = cst[:, K_ONES, :]

                def load_h(j):
                    hj = hjs[j % 2]
                    if j == 0:
                        S.dma("sp", hj[:], st_ssm[l, 0].rearrange("h p n -> (h p) n").rearrange("(c q) n -> q c n", q=128), sb_w=hj)
                    if j + 1 < NSEG:
                        hn = hjs[(j + 1) % 2]
                        S.dma("sp", hn[:], st_ssm[l, j + 1].rearrange("h p n -> (h p) n").rearrange("(c q) n -> q c n", q=128), sb_w=hn)
                    S.cp("act", hjb[:, 0:16, :], hj[:, 0:16, :], [hj], [hjb])
                    S.cp("pool", hjb[:, 16:32, :], hj[:, 16:32, :], [hj], [hjb])
                    for q8 in range(4):
                        pb_ = PB[q8 % 2]
                        for c in range(8):
                            S.tr(pb_[:, c, :], hjb[:, q8 * 8 + c, :], identb[:], [hjb, identb], [pb_])
                        S.cp("dve" if q8 % 2 else "act", hTb[:, q8 * 1024:(q8 + 1) * 1024], pb_[:].rearrange("p c q -> p (c q)"), [pb_], [hTb])

                def store_h(dst, j=0):
                    hj = hjs[j % 2] if sample else yy
                    for q4 in range(8):
                        p = nb()
                        for c in range(4):
                            cc_ = q4 * 4 + c
                            S.tr(p[:, c * 128:(c + 1) * 128], hT[:, cc_ * 128:(cc_ + 1) * 128], ident, [hT, cst], [p])
                        S.cp("act" if q4 % 2 else "dve", hj[:, q4 * 4:(q4 + 1) * 4, :] if sample else
                             hj[:, q4 * 512:(q4 + 1) * 512].rearrange("p (c n) -> p c n", c=4),
                             p[:].rearrange("p (c n) -> p c n", c=4), [p], [hj])
                    src = hj[:] if sample else hj[:].rearrange("p (c n) -> p c n", c=32)
                    S.dma("act", dst.rearrange("h p n -> (h p) n").rearrange("(c q) n -> q c n", q=128), src, sb_r=hj)

                for i in tiles:
                    S.dma("sp", xc[:], xcs[l][rows(i), :], sb_w=xc)
                    if sample:
                        S.dma("sp", zt[:], proj[l][rows(i), C_Z:C_Z + 4096], sb_w=zt)
                    S.dma("sp", dtt[:], proj[l][rows(i), C_DT:C_DT + 64], sb_w=dtt)
                    S.tt("dve", sm[:, T1, :], dtt[:], dtb[:], ALU.add, [dtt, dtb], [sm])
                    S.act(sm[:, T1, :], sm[:, T1, :], AF.Exp, [sm], [sm])
                    S.act(sm[:, DTP, :], sm[:, T1, :], AF.Ln, [sm], [sm], bias=1.0)
                    S.act(sm[:, LND, :], sm[:, DTP, :], AF.Ln, [sm], [sm])
                    S.tt("dve", sm[:, AA, :], sm[:, DTP, :], aneg[:], ALU.mult, [sm, aneg], [sm])
                    p = nb()
                    S.mm(p[:, 0:64], tri, sm[:, AA, :], True, True, [cst, sm], [p])
                    S.cp("dve", sm[:, CS, :], p[:, 0:64], [p], [sm])
                    S.tt("dve", sm[:, CS2, :], sm[:, CS, :], sm[:, LND, :], ALU.subtract, [sm], [sm])
                    S.act(sm[:, ECS, :], sm[:, CS, :], AF.Exp, [sm], [sm])
                    p = nb()
                    S.mm(p[:, 0:64], selI, sm[:, CS, :], True, True, [cst, sm], [p])
                    S.act(sm[:, DL, :], p[:, 0:64], AF.Exp, [p], [sm])
                    S.tt("dve", sm[:, DL, :], sm[:, DL, :], sm[:, DTP, :], ALU.mult, [sm], [sm])
                    S.cp("pool", xsb[:], xc[:, 0:4096], [xc], [xsb])
                    S.cp("act", Bb[:], xc[:, 4096:5120], [xc], [Bb])
                    for (src0, dstb) in ((4096, BTb), (5120, CTb)):
                        for hf in range(2):
                            p = nb()
                            for c in range(4):
                                g = hf * 4 + c
                                S.tr(p[:, c * 128:(c + 1) * 128], xc[:, src0 + g * 128:src0 + (g + 1) * 128], ident, [xc, cst], [p])
                            S.cp("act" if hf else "dve", dstb[:, hf * 4:(hf + 1) * 4, :], p[:].rearrange("p (c t) -> p c t", c=4), [p], [dstb])
                    S.tt("dve", xdl[:].rearrange("p (h d) -> p h d", h=64), xc[:, 0:4096].rearrange("p (h d) -> p h d", h=64),
                         sm[:, DL, :].unsqueeze(2).to_broadcast([128, 64, 64]), ALU.mult, [xc, sm], [xdl])
                    if sample:
                        S.memset("pool", yy[:], 0.0, [yy])
                    for j in range(nseg):
                        if sample:
                            load_h(j)
                            cmj = cm[j % 2]
                            if j >= 2:
                                jo = j - 2
                                S.memset("pool", cmj[:, :, jo * SL:(jo + 1) * SL], 0.0, [cmj])
                            S.cp("pool", cmj[:, :, j * SL:(j + 1) * SL], CTb[:, :, j * SL:(j + 1) * SL], [CTb], [cmj])
                            Bmj = Bm[j % 2]
                            S.ts("dve", Bmj[:], Bb[:], cst[:, K_SEGCOL, j:j + 1], None, ALU.mult, None, [Bb, cst], [Bmj])
                            for g in range(8):
                                p = nb()
                                S.mm(p[:], cmj[:, g, :], hTb[:, g * 512:(g + 1) * 512], True, True, [cmj, hTb], [p])
                                S.tt("dve", yy[:, g * 512:(g + 1) * 512], yy[:, g * 512:(g + 1) * 512], p[:], ALU.add, [yy, p], [yy])
                            Bl = Bmj
                            krow = K_ROW0 + j
                        else:
                            Bl = Bb
                            krow = K_ROWP
                        if sample:
                            hj = hjs[j % 2]
                            p = nb()
                            S.mm(p[:, 0:32], cst[:, krow, :], sm[:, CS, 0:64:2], True, True, [cst, sm], [p])
                            S.mm(p[:, 32:64], cst[:, krow, :], sm[:, CS, 1:64:2], True, True, [cst, sm], [p])
                            S.act(decc[0:64, :], p[0:64, 0:32], AF.Exp, [p], [decc])
                            S.act(decc[64:128, :], p[64:128, 32:64], AF.Exp, [p], [decc])
                            for k4 in range(8):
                                p = nb()
                                for c in range(4):
                                    cc_ = k4 * 4 + c
                                    S.mm(p[:, c * 128:(c + 1) * 128], xdl[:, cc_ * 128:(cc_ + 1) * 128], Bl[:, k4 * 128:(k4 + 1) * 128],
                                         True, True, [xdl, Bl], [p])
                                hv = hj[:, k4 * 4:(k4 + 1) * 4, :]
                                S.tt("pool" if k4 % 2 else "dve", hv, hv, decc[:, k4 * 4:(k4 + 1) * 4].unsqueeze(2).to_broadcast([128, 4, 128]),
                                     ALU.mult, [hj, decc], [hj])
                                S.tt("dve", hv, hv, p[:].rearrange("p (c n) -> p c n", c=4), ALU.add, [hj, p], [hj])
                            S.dma("act", ssm_s[l, j].rearrange("h p n -> (h p) n").rearrange("(c q) n -> q c n", q=128), hj[:], sb_r=hj)
                    for g in range(8):
                        pc = nb()
                        S.mm(pc[:, 0:128], BTb[:, g, :], CTb[:, g, :], True, True, [BTb, CTb], [pc])
                        dg = diag[g % nd_]
                        S.tt("pool", dg[:], ident.unsqueeze(1).to_broadcast([128, 8, 128]),
                             sm[:, CS, g * 8:(g + 1) * 8].unsqueeze(2).to_broadcast([128, 8, 128]), ALU.mult, [cst, sm], [dg])
                        sg = seg[g % nd_]
                        for hf in range(2):
                            p = nb()
                            S.mm(p[:], ones, dg[:, hf * 4:(hf + 1) * 4, :], True, False, [cst, dg], [p])
                            S.mm(p[:], ident, negm8[:], False, True, [cst, negm8], [p])
                            S.tt("dve", sg[:, hf * 4:(hf + 1) * 4, :], p[:].rearrange("p (h t) -> p h t", h=4),
                                 sm[:, CS2, g * 8 + hf * 4:g * 8 + hf * 4 + 4].unsqueeze(2).to_broadcast([128, 4, 128]), ALU.subtract, [p, sm], [sg])
                        S.act(sg[:], sg[:], AF.Exp, [sg], [sg])
                        wt = WTb[g % nd_]
                        S.tt("dve", wt[:], sg[:], pc[:, 0:128].unsqueeze(1).to_broadcast([128, 8, 128]), ALU.mult, [sg, pc], [wt])
                        py = nb()
                        for h in range(8):
                            hh = g * 8 + h
                            S.mm(py[:, h * 64:(h + 1) * 64], wt[:, h, :], xsb[:, hh * 64:(hh + 1) * 64], True, True, [wt, xsb], [py])
                        ecsb = sm[:, ECS, g * 8:(g + 1) * 8].unsqueeze(2).to_broadcast([128, 8, 64])
                        yg = yy[:, g * 512:(g + 1) * 512]
                        if sample:
                            S.tt("pool", yg.rearrange("p (h d) -> p h d", h=8), yg.rearrange("p (h d) -> p h d", h=8), ecsb, ALU.mult, [yy, sm], [yy])
                            S.tt("dve", yg, yg, py[:], ALU.add, [yy, py], [yy])
                        else:
                            pz = nb()
                            S.mm(pz[:], CTb[:, g, :], hTb[:, g * 512:(g + 1) * 512], True, True, [CTb, hTb], [pz])
                            S.tt("dve", tmp[:].rearrange("p (h d) -> p h d", h=8), pz[:].rearrange("p (h d) -> p h d", h=8), ecsb, ALU.mult, [pz, sm], [tmp])
                            S.tt("dve", yg, tmp[:], py[:], ALU.add, [tmp, py], [yy])
                    if not sample:
                        p = nb()
                        S.mm(p[:, 0:64], cst[:, K_ROWP, :], sm[:, CS, :], True, True, [cst, sm], [p])
                        S.act(sm[:, DEC, :], p[:, 0:64], AF.Exp, [p], [sm])
                        for g in range(8):
                            p = nb()
                            S.mm(p[:], Bb[:, g * 128:(g + 1) * 128], xdl[:, g * 512:(g + 1) * 512], True, True, [Bb, xdl], [p])
                            hv = hT[:, g * 512:(g + 1) * 512].rearrange("p (h d) -> p h d", h=8)
                            S.tt("pool", hv, hv, sm[:, DEC, g * 8:(g + 1) * 8].unsqueeze(2).to_broadcast([128, 8, 64]), ALU.mult, [hT, sm], [hT])
                            S.tt("dve", hT[:, g * 512:(g + 1) * 512], hT[:, g * 512:(g + 1) * 512], p[:], ALU.add, [hT, p], [hT])
                        S.cp("pool", hTb[:], hT[:], [hT], [hTb])
                    if sample:
                        ssd_post(l, i, yy, xc[:, 0:4096], xc, zt, dsk, ssn, junk, ss, yTb, nb)
                    else:
                        S.dma("sp", y_loc[l][rows(i), :], yy[:], sb_r=yy)
                        S.dma("sp", ct_d[l][i], CTb[:], sb_r=CTb)
                        S.tt("dve", egt[:], sm[:, ECS, :], gamh[:], ALU.mult, [sm, gamh], [egt])
                        S.dma("sp", eg_d[l][i], egt[:], sb_r=egt)
                        S.tt("dve", gamh[:], gamh[:], sm[:, DEC, :], ALU.mult, [gamh, sm], [gamh])
                if not sample:
                    S.dma("sp", hloc_d[l], hT[:], sb_r=hT)
                    S.dma("sp", gamh_d[l], gamh[:], sb_r=gamh)
                    S.ts("dve", hT[:], hT[:], flg[:, 1:2], None, ALU.mult, None, [hT, flg], [hT])
                    S.dma("sp", hs_snd[l].ap(), hT[:], sb_r=hT)
            if not sample:
                S.collective(hs_snd[l], hs_rcv[l])
            else:
                S.barrier()

        def phase_ssd2(l):
            with ExitStack() as pes0:
                hinb = S.sb(pes0, "d2hinb", [128, 4096], BF16)
                with ExitStack() as pes:
                    hin = S.sb(pes, "d2hin", [128, 4096], F32, dma=True)
                    hT = S.sb(pes, "d2hT", [128, 4096], F32, dma=True)
                    gamh = S.sb(pes, "d2gamh", [128, 64], F32, dma=True)
                    hj = S.sb(pes, "d2hj", [128, 32, 128], F32, dma=True)
                    P0 = [S.ps(pes, "d2Q%d" % i, [128, 512], F32) for i in range(4)]
                    S.dma("sp", hin[:], hs_rcv[l].ap(), sb_w=hin)
                    S.dma("sp", hT[:], hloc_d[l], sb_w=hT)
                    S.dma("sp", gamh[:], gamh_d[l], sb_w=gamh)
                    S.ts("dve", hin[:], hin[:], flg[:, 0:1], None, ALU.mult, None, [hin, flg], [hin])
                    S.cp("pool", hinb[:], hin[:], [hin], [hinb])
                    hv = hin[:].rearrange("p (h d) -> p h d", h=64)
                    S.tt("dve", hv, hv, gamh[:].unsqueeze(2).to_broadcast([128, 64, 64]), ALU.mult, [hin, gamh], [hin])
                    S.tt("pool", hT[:], hT[:], hin[:], ALU.add, [hT, hin], [hT])
                    for q4 in range(8):
                        p = P0[q4 % 4]
                        for c in range(4):
                            cc_ = q4 * 4 + c
                            S.tr(p[:, c * 128:(c + 1) * 128], hT[:, cc_ * 128:(cc_ + 1) * 128], ident, [hT, cst], [p])
                        S.cp("act" if q4 % 2 else "dve", hj[:, q4 * 4:(q4 + 1) * 4, :], p[:].rearrange("p (c n) -> p c n", c=4), [p], [hj])
                    S.dma("sp", ssm_p[l].rearrange("h p n -> (h p) n").rearrange("(c q) n -> q c n", q=128), hj[:], sb_r=hj)
                S.barrier()
                with ExitStack() as pes:
                    dsk = S.sb(pes, "d2dsk", [128, 64], F32, dma=True)
                    ssn = S.sb(pes, "d2ssn", [128, 4096], F32, dma=True)
                    yys = [S.sb(pes, "d2y%d" % i, [128, 4096], F32, dma=True) for i in range(2)]
                    xss = [S.sb(pes, "d2xs%d" % i, [128, 4096], F32, dma=True) for i in range(2)]
                    zts = [S.sb(pes, "d2z%d" % i, [128, 4096], F32, dma=True) for i in range(2)]
                    ctbs = [S.sb(pes, "d2ct%d" % i, [128, 8, 128], BF16, dma=True) for i in range(2)]
                    egts = [S.sb(pes, "d2eg%d" % i, [128, 64], F32, dma=True) for i in range(2)]
                    tmp = S.sb(pes, "d2tmp", [128, 512], F32)
                    junk = S.sb(pes, "d2junk", [128, 512], F32)
                    ss = S.sb(pes, "d2ss", [128, 16], F32)
                    yTb = S.sb(pes, "d2yTb", [128, 32, 128], BF16, dma=True)
                    P = [S.ps(pes, "d2P%d" % i, [128, 512], F32) for i in range(8)]
                    rot = [0]

                    def nb():
                        b_ = P[rot[0] % 8]
                        rot[0] += 1
                        return b_
                    S.dma("sp", dsk[:], d_skip[l:l + 1, :].partition_broadcast(128), sb_w=dsk)
                    S.dma("sp", ssn[:], ssd_norm[l:l + 1, :].partition_broadcast(128), sb_w=ssn)
                    for i in range(NPT):
                        yy, xs_, zt, ctb, egt = yys[i % 2], xss[i % 2], zts[i % 2], ctbs[i % 2], egts[i % 2]
                        S.dma("sp", yy[:], y_loc[l][rows(i), :], sb_w=yy)
                        S.dma("sp", xs_[:], xcs[l][rows(i), 0:4096], sb_w=xs_)
                        S.dma("sp", zt[:], proj[l][rows(i), C_Z:C_Z + 4096], sb_w=zt)
                        S.dma("sp", ctb[:], ct_d[l][i], sb_w=ctb)
                        S.dma("sp", egt[:], eg_d[l][i], sb_w=egt)
                        for g in range(8):
                            pz = nb()
                            S.mm(pz[:], ctb[:, g, :], hinb[:, g * 512:(g + 1) * 512], True, True, [ctb, hinb], [pz])
                            S.tt("dve", tmp[:].rearrange("p (h d) -> p h d", h=8), pz[:].rearrange("p (h d) -> p h d", h=8),
                                 egt[:, g * 8:(g + 1) * 8].unsqueeze(2).to_broadcast([128, 8, 64]), ALU.mult, [pz, egt], [tmp])
                            S.tt("pool", yy[:, g * 512:(g + 1) * 512], yy[:, g * 512:(g + 1) * 512], tmp[:], ALU.add, [yy, tmp], [yy])
                        ssd_post(l, i, yy, xs_[:], xs_, zt, dsk, ssn, junk, ss, yTb, nb)
            S.barrier()

        def phase_o1(l, mT):
            with ExitStack() as pes:
                wg = [S.sb(pes, "owg%d" % i, [128, 16, 512], BF16, dma=True) for i in range(1)]
                ws = [S.sb(pes, "ows%d" % i, [128, 32, 512], BF16, dma=True) for i in range(1)]
                oa = [S.sb(pes, "ooa%d" % i, [128, 16, 128], BF16, dma=True) for i in range(2)]
                yb = [S.sb(pes, "oyb%d" % i, [128, 32, 128], BF16, dma=True) for i in range(2)]
                gA = [S.sb(pes, "ogA%d" % i, [128, 512], F32, dma=True) for i in range(2)]
                gB = [S.sb(pes, "ogB%d" % i, [128, 512], F32, dma=True) for i in range(2)]
                mg = [S.sb(pes, "omg%d" % i, [128, 512], F32) for i in range(2)]
                mgb = [S.sb(pes, "omgb%d" % i, [128, 512], BF16) for i in range(2)]
                pa = [S.ps(pes, "opa%d" % i, [128, 512], F32) for i in range(2)]
                pb = [S.ps(pes, "opb%d" % i, [128, 512], F32) for i in range(2)]
                pt = [S.ps(pes, "opt%d" % i, [128, 4, 128], BF16) for i in range(2)]
                wgv = w_gproj[l].rearrange("(kt p) n -> p kt n", p=128)
                wsv = w_sproj[l].rearrange("(kt p) n -> p kt n", p=128)
                n = 0
                pend = []

                def o1_tr(b2, cb, i):
                    for c in range(4):
                        S.tr(pt[b2][:, c, :], mgb[b2][:, c * 128:(c + 1) * 128], identb[:], [mgb[b2], identb], [pt[b2]])
                    g_, o_ = hgi(i)
                    S.cp("act", mT[g_][:, cb * 4:(cb + 1) * 4, o_:o_ + 128], pt[b2][:], [pt[b2]], [mT[g_]])
                for cb in range(4):
                    cs_ = slice(cb * 512, (cb + 1) * 512)
                    Wg = wg[0]
                    Ws = ws[0]
                    S.dma("pool", Wg[:], wgv[:, :, cs_], sb_w=Wg)
                    S.dma("pool", Ws[:], wsv[:, :, cs_], sb_w=Ws)
                    for i in range(NT):
                        b2 = n % 2
                        n += 1
                        S.dma("sp", oa[b2][:], oaT[l][i], sb_w=oa[b2])
                        S.dma("sp", yb[b2][:], ybT[l][i], sb_w=yb[b2])
                        S.dma("sp", gA[b2][:], proj[l][rows(i), C_G + cb * 512:C_G + (cb + 1) * 512], sb_w=gA[b2])
                        S.dma("sp", gB[b2][:], proj[l][rows(i), C_G + 2048 + cb * 512:C_G + 2048 + (cb + 1) * 512], sb_w=gB[b2])
                        for k in range(16):
                            S.mm(pa[b2][:], oa[b2][:, k, :], Wg[:, k, :], k == 0, k == 15, [oa[b2], Wg], [pa[b2]])
                        for k in range(32):
                            S.mm(pb[b2][:], yb[b2][:, k, :], Ws[:, k, :], k == 0, k == 31, [yb[b2], Ws], [pb[b2]])
                        S.act(gA[b2][:], gA[b2][:], AF.Sigmoid, [gA[b2]], [gA[b2]])
                        S.act(gB[b2][:], gB[b2][:], AF.Sigmoid, [gB[b2]], [gB[b2]])
                        S.tt("dve", mg[b2][:], pa[b2][:], gA[b2][:], ALU.mult, [pa[b2], gA[b2]], [mg[b2]])
                        S.tt("dve", gB[b2][:], pb[b2][:], gB[b2][:], ALU.mult, [pb[b2], gB[b2]], [gB[b2]])
                        S.tt("pool", mgb[b2][:], mg[b2][:], gB[b2][:], ALU.add, [mg[b2], gB[b2]], [mgb[b2]])
                        pend.append((b2, cb, i))
                        if len(pend) > 1:
                            o1_tr(*pend.pop(0))
                while pend:
                    o1_tr(*pend.pop(0))
            S.barrier()

        def phase_res(l, mT, wsrc, nk, gate_i, xsrc, xdst, actsrc=None):
            with ExitStack() as pes:
                wm = [S.sb(pes, "rwm%d" % i, [128, nk, 512], BF16, dma=True) for i in range(2)]
                NB = 3
                xb = [S.sb(pes, "rxb%d" % i, [128, 512], F32, dma=True) for i in range(NB)]
                gb = [S.sb(pes, "rgb%d" % i, [128, 512], F32, dma=True) for i in range(NB)]
                pp = [S.ps(pes, "rpp%d" % i, [128, 512], F32) for i in range(NB)]
                if actsrc is not None:
                    ab = [S.sb(pes, "rab%d" % i, [128, nk, 128], BF16, dma=True) for i in range(NB)]
                wv = wsrc.rearrange("(kt p) n -> p kt n", p=128)
                n = 0
                for cb in range(4):
                    cs_ = slice(cb * 512, (cb + 1) * 512)
                    W = wm[cb % 2]
                    S.dma("pool", W[:], wv[:, :, cs_], sb_w=W)
                    for i in range(NT):
                        b2 = n % NB
                        n += 1
                        g = grp(i)
                        S.dma("sp", xb[b2][:], xsrc[rows(i), cs_], sb_w=xb[b2])
                        S.dma("sp", gb[b2][:], mod[l][g, :, gate_i * D + cb * 512:gate_i * D + (cb + 1) * 512], sb_w=gb[b2])
                        if actsrc is not None:
                            S.dma("sp", ab[b2][:], actsrc[i], sb_w=ab[b2])
                            for k in range(nk):
                                S.mm(pp[b2][:], ab[b2][:, k, :], W[:, k, :], k == 0, k == nk - 1, [ab[b2], W], [pp[b2]])
                        else:
                            for k in range(nk):
                                S.mm(pp[b2][:], hview(mT, i, k), W[:, k, :], k == 0, k == nk - 1, [mT[hgi(i)[0]], W], [pp[b2]])
                        S.tt("dve", gb[b2][:], pp[b2][:], gb[b2][:], ALU.mult, [pp[b2], gb[b2]], [gb[b2]])
                        S.tt("pool", xb[b2][:], xb[b2][:], gb[b2][:], ALU.add, [xb[b2], gb[b2]], [xb[b2]])
                        S.dma("act", xdst[rows(i), cs_], xb[b2][:], sb_r=xb[b2])
            S.barrier()

        def phase_f1(l, hT):
            with ExitStack() as pes:
                wgt = [S.sb(pes, "fwg%d" % i, [128, 16, 512], BF16, dma=True) for i in range(2)]
                wup = [S.sb(pes, "fwu%d" % i, [128, 16, 512], BF16, dma=True) for i in range(2)]
                pg = [S.ps(pes, "fpg%d" % i, [128, 512], F32) for i in range(2)]
                pu = [S.ps(pes, "fpu%d" % i, [128, 512], F32) for i in range(2)]
                sg = [S.sb(pes, "fsg%d" % i, [128, 512], F32) for i in range(2)]
                at = [S.sb(pes, "fat%d" % i, [128, 512], BF16, dma=True) for i in range(2)]
                wv = w_fin[l].rearrange("(kt p) n -> p kt n", p=128)
                n = 0
                for jb in range(11):
                    Wg = wgt[jb % 2]
                    Wu = wup[jb % 2]
                    S.dma("pool", Wg[:], wv[:, :, jb * 512:(jb + 1) * 512], sb_w=Wg)
                    S.dma("pool", Wu[:], wv[:, :, DFF + jb * 512:DFF + (jb + 1) * 512], sb_w=Wu)
                    for jj in range(4):
                        j = jb * 4 + jj
                        for tg in range(NG + 1):
                            N = 512 if tg < NG else 128
                            b2 = n % 2
                            n += 1
                            for k in range(16):
                                S.mm(pg[b2][:, 0:N], Wg[:, k, jj * 128:(jj + 1) * 128], hT[tg][:, k, :], k == 0, k == 15, [Wg, hT[tg]], [pg[b2]])
                            for k in range(16):
                                S.mm(pu[b2][:, 0:N], Wu[:, k, jj * 128:(jj + 1) * 128], hT[tg][:, k, :], k == 0, k == 15, [Wu, hT[tg]], [pu[b2]])
                            S.act(sg[b2][:, 0:N], pg[b2][:, 0:N], AF.Silu, [pg[b2]], [sg[b2]])
                            S.tt("dve", at[b2][:, 0:N], sg[b2][:, 0:N], pu[b2][:, 0:N], ALU.mult, [sg[b2], pu[b2]], [at[b2]])
                            if tg < NG:
                                S.dma("sp", actT[l][4 * tg:4 * tg + 4, :, j, :].rearrange("a p t -> p a t"),
                                      at[b2][:, 0:512].rearrange("p (a t) -> p a t", a=4), sb_r=at[b2])
                            else:
                                S.dma("sp", actT[l][NPT, :, j, :], at[b2][:, 0:128], sb_r=at[b2])
            S.barrier()

        def phase_final(xsrc):
            with ExitStack() as pes:
                xt = [S.sb(pes, "zx%d" % i, [128, D], F32, dma=True) for i in range(2)]
                fn = S.sb(pes, "zfn", [128, D], F32, dma=True)
                junk = S.sb(pes, "zjunk", [128, D], F32)
                ss = S.sb(pes, "zss", [128, 2], F32)
                S.dma("sp", fn[:], fnorm.partition_broadcast(128), sb_w=fn)
                for i in range(NT):
                    x = xt[i % 2]
                    S.dma("sp", x[:], xsrc[rows(i), :], sb_w=x)
                    S.memset("pool", ss[:], 0.0, [ss])
                    S.act(junk[:], x[:], AF.Square, [x, ss], [junk, ss], accum=ss[:, 0:1])
                    S.ts("dve", ss[:, 1:2], ss[:, 0:1], 1.0 / D, EPS, ALU.mult, ALU.add, [ss], [ss])
                    S.act(ss[:, 1:2], ss[:, 1:2], AF.Sqrt, [ss], [ss])
                    S.recip(ss[:, 1:2], ss[:, 1:2], [ss], [ss])
                    S.stt("dve", x[:], x[:], ss[:, 1:2], fn[:], ALU.mult, ALU.mult, [x, ss, fn], [x])
                    S.dma("sp", y_out[rows(i), :], x[:], sb_r=x)
            S.barrier()

        S.phase = "mod"
        phase_mod()
        xcur = xin
        for l in range(DEPTH):
            with ExitStack() as ges:
                hT = [S.sb(ges, "hT%d_%d" % (l, g), [128, 16, 512 if g < NG else 128], BF16) for g in range(NG + 1)]
                S.phase = "norm%d" % l
                phase_norm(xcur, l, 0, 1, hT)
                S.phase = "proj%d" % l
                phase_proj(l, hT)
            S.phase = "conv%d" % l
            phase_conv(l)
            S.phase = "gla%d" % l
            phase_gla(l, False)
            S.phase = "gla_s%d" % l
            phase_gla(l, True)
            S.phase = "ssd%d" % l
            phase_ssd(l, False)
            S.phase = "ssd_s%d" % l
            phase_ssd(l, True)
            S.phase = "gla2%d" % l
            phase_gla2(l)
            S.phase = "ssd2%d" % l
            phase_ssd2(l)
            with ExitStack() as ges:
                mT = [S.sb(ges, "mT%d_%d" % (l, g), [128, 16, 512 if g < NG else 128], BF16) for g in range(NG + 1)]
                S.phase = "o1%d" % l
                phase_o1(l, mT)
                S.phase = "res_mix%d" % l
                phase_res(l, mT, w_mix[l], 16, 2, xcur, xv[l][0])
            with ExitStack() as ges:
                hT = [S.sb(ges, "h2T%d_%d" % (l, g), [128, 16, 512 if g < NG else 128], BF16) for g in range(NG + 1)]
                S.phase = "norm2%d" % l
                phase_norm(xv[l][0], l, 3, 4, hT)
                S.phase = "f1%d" % l
                phase_f1(l, hT)
            S.phase = "res_ffo%d" % l
            phase_res(l, None, w_fout[l], 44, 5, xv[l][0], xv[l][1], actsrc=actT[l])
            xcur = xv[l][1]
        S.phase = "final"
        phase_final(xcur)
        import os
        if os.environ.get("KTRACE_MAP"):
            S.namemap = {}
        S.emit()
        if S.namemap is not None:
            import json
            json.dump(S.namemap, open(os.environ["KTRACE_MAP"], "w"))
        print("ops:", S.nops, {e: len(S.prog[e]) for e in S.prog})
    return nc


_NC_CACHE = {}


def kernel(x_prompt, x_sample, c_prompt, c_sample, state_gla, state_ssm, state_conv,
           w_ada, b_ada, w_in, w_gla_gate, b_gla_gate, gla_norm, w_gla_proj, conv_w, conv_b,
           dt_bias, A_log, d_skip, ssd_norm, w_ssd_proj, w_mix_out, w_ffn_in, w_ffn_out, final_norm):
    f = lambda a: np.ascontiguousarray(np.asarray(a, dtype=np.float32))
    x_prompt, x_sample, c_prompt, c_sample = f(x_prompt), f(x_sample), f(c_prompt), f(c_sample)
    state_gla, state_ssm, state_conv = f(state_gla), f(state_ssm), f(state_conv)
    if "nc" not in _NC_CACHE:
        _NC_CACHE["nc"] = build_nc()
    nc = _NC_CACHE["nc"]
    consts = make_consts()
    shared = {
        "consts": consts, "w_ada": f(w_ada), "b_ada": f(b_ada), "w_in": f(w_in), "w_gla_gate": f(w_gla_gate),
        "b_gla_gate": f(b_gla_gate), "gla_norm": f(gla_norm), "w_gla_proj": f(w_gla_proj), "conv_w": f(conv_w),
        "conv_b": f(conv_b), "dt_bias": f(dt_bias), "A_log": f(A_log), "d_skip": f(d_skip), "ssd_norm": f(ssd_norm),
        "w_ssd_proj": f(w_ssd_proj), "w_mix_out": f(w_mix_out), "w_ffn_in": f(w_ffn_in), "w_ffn_out": f(w_ffn_out),
        "final_norm": f(final_norm).reshape(1, D),
    }
    in_maps = []
    for c in range(8):
        ps = c // 2
        hf = c % 2
        s0 = c * NSEG
        xin = np.concatenate([x_prompt[ps][hf * NPT * 128:(hf + 1) * NPT * 128], x_sample[s0:s0 + NSEG].reshape(NSEG * SL, D)], axis=0)
        cc = np.stack([np.broadcast_to(c_prompt[ps][None, :], (128, D)),
                       np.repeat(c_sample[s0:s0 + NSEG], SL, axis=0)], axis=0)
        m = dict(shared)
        m["xin"] = np.ascontiguousarray(xin)
        fl = np.zeros((128, 2), np.float32)
        fl[:, 0] = float(hf)
        fl[:, 1] = float(1 - hf)
        m["flags"] = fl
        m["cc"] = np.ascontiguousarray(cc)
        m["st_gla"] = np.ascontiguousarray(state_gla[:, s0:s0 + NSEG])
        m["st_ssm"] = np.ascontiguousarray(state_ssm[:, s0:s0 + NSEG])
        m["st_conv"] = np.ascontiguousarray(state_conv[:, s0:s0 + NSEG])
        in_maps.append(m)
    res = run_bass_kernel_spmd(nc, in_maps, core_ids=list(range(8)))
    R = res.results
    y_prompt = np.stack([np.concatenate([R[2 * p]["y"][:NPT * 128], R[2 * p + 1]["y"][:NPT * 128]], axis=0) for p in range(4)], axis=0)
    y_sample = np.concatenate([R[c]["y"][NPT * 128:].reshape(NSEG, SL, D) for c in range(8)], axis=0)
    gla_p = np.stack([R[2 * p + 1]["gla_p"] for p in range(4)], axis=1)
    ssm_p = np.stack([R[2 * p + 1]["ssm_p"] for p in range(4)], axis=1)
    conv_p = np.stack([R[2 * p + 1]["conv_p"] for p in range(4)], axis=1)
    gla_s = np.concatenate([R[c]["gla_s"] for c in range(8)], axis=1)
    ssm_s = np.concatenate([R[c]["ssm_s"] for c in range(8)], axis=1)
    conv_s = np.concatenate([R[c]["conv_s"] for c in range(8)], axis=1)
    return (y_prompt.astype(np.float32), y_sample.astype(np.float32), gla_p.astype(np.float32), ssm_p.astype(np.float32),
            conv_p.astype(np.float32), gla_s.astype(np.float32), ssm_s.astype(np.float32), conv_s.astype(np.float32))
```

```python
import numpy as np
import concourse.bass as bass
import concourse.mybir as mybir
from concourse.bass_utils import run_bass_kernel_spmd
from contextlib import ExitStack

F32 = mybir.dt.float32
BF16 = mybir.dt.bfloat16
AF = mybir.ActivationFunctionType
ALU = mybir.AluOpType
AX = mybir.AxisListType

D = 2048
NPT = 8
NT = NPT + 1
NG = NPT // 4
TOK = NT * 128
NSEG = 16
SL = 8
DEPTH = 2
DFF = 5632
NIN = 20560
EPS = 1e-6
C_Q, C_K, C_V, C_R, C_GLR, C_Z, C_XBC, C_DT, C_G = 0, 1024, 2048, 4096, 6144, 6160, 10256, 16400, 16464
NEG = -30000.0


class DSem:
    def __init__(self, sem, name):
        self.sem = sem
        self.cnt = 0
        self.name = name


class Buf:
    def __init__(self, name, t, dsem=None):
        self.name = name
        self.t = t
        self.wr = {}
        self.rd = {}
        self.dsem = dsem

    def __getitem__(self, k):
        return self.t[k]


class Sched:
    ENGS = ("pe", "act", "dve", "pool", "sp")

    def __init__(self, nc, es, ndsem=40):
        self.nc = nc
        self.sem = {e: es.enter_context(nc.semaphore("s_" + e)) for e in ("pe", "act", "dve", "pool")}
        self.cnt = {e: 0 for e in self.ENGS}
        self.seen = {e: {} for e in self.ENGS}
        self.prog = {e: [] for e in self.ENGS}
        self.dram = {}
        self.dpool = [DSem(es.enter_context(nc.semaphore("d%d" % i)), "d%d" % i) for i in range(ndsem)]
        self.dfree = list(self.dpool)
        self.nops = 0
        self.cc_sem = es.enter_context(nc.semaphore("s_cc"))
        self.cc_cnt = 0
        self.phase = "init"
        self.namemap = None

    def sb(self, pes, name, shape, dtype=F32, dma=False):
        self.nops += 1
        name = "%s_u%d" % (name, self.nops)
        t = pes.enter_context(self.nc.sbuf_tensor(name, list(shape), dtype))
        ds = None
        if dma:
            ds = self.dfree.pop()
            pes.callback(self.dfree.append, ds)
        return Buf(name, t, ds)

    def ps(self, pes, name, shape, dtype=F32):
        self.nops += 1
        name = "%s_u%d" % (name, self.nops)
        t = pes.enter_context(self.nc.psum_tensor(name, list(shape), dtype))
        return Buf(name, t)

    def dbuf(self, pes, name):
        ds = self.dfree.pop()
        pes.callback(self.dfree.append, ds)
        return Buf(name, None, ds)

    @staticmethod
    def _add(need, d, skip=None, pe=False):
        for k, (sem, val) in d.items():
            if pe and k == "pe":
                continue
            if skip is not None and k == skip:
                continue
            if k not in need or need[k][1] < val:
                need[k] = (sem, val)

    def _waits(self, eng, need):
        waits = []
        seen = self.seen[eng]
        for k, (sem, val) in need.items():
            if seen.get(k, 0) >= val:
                continue
            seen[k] = val
            waits.append((sem, val))
        return waits

    def op(self, eng, fn, reads=(), writes=()):
        need = {}
        pe = eng == "pe"
        for b in reads:
            self._add(need, b.wr, pe=pe)
        for b in writes:
            self._add(need, b.wr, pe=pe)
            self._add(need, b.rd, skip=eng, pe=pe)
        waits = self._waits(eng, need)
        self.cnt[eng] += 1
        ev = (self.sem[eng], self.cnt[eng])
        self.prog[eng].append((waits, fn, ev[0], 1, self.phase))
        for b in writes:
            b.wr = {eng: ev}
            b.rd = {}
        for b in reads:
            if b not in writes:
                b.rd[eng] = ev
        self.nops += 1

    def dma(self, q, out, in_, sb_w=None, sb_r=None, after=(), produces=None, evbuf=None):
        need = {}
        if sb_r is not None:
            self._add(need, sb_r.wr)
        if sb_w is not None:
            self._add(need, sb_w.wr)
            self._add(need, sb_w.rd)
        for key in after:
            for (k, sem, val) in self.dram.get(key, ()):
                if k not in need or need[k][1] < val:
                    need[k] = (sem, val)
        waits = self._waits(q, need)
        eb = evbuf if evbuf is not None else (sb_w if sb_w is not None else sb_r)
        ds = eb.dsem
        ds.cnt += 1
        ev = (ds.sem, 16 * ds.cnt)
        k = ds.name
        self.prog[q].append((waits, (lambda e, o=out, i=in_: e.dma_start(out=o, in_=i)), ds.sem, 16, self.phase))
        if sb_w is not None:
            sb_w.wr = {k: ev}
            sb_w.rd = {}
        if sb_r is not None:
            sb_r.rd[k] = ev
        if produces is not None:
            self.dram.setdefault(produces, []).append((k, ev[0], ev[1]))
        self.nops += 1

    def barrier(self):
        need = {}
        for e in ("pe", "act", "dve", "pool"):
            if self.cnt[e] > 0:
                need[e] = (self.sem[e], self.cnt[e])
        for ds in self.dpool:
            if ds.cnt > 0:
                need[ds.name] = (ds.sem, 16 * ds.cnt)
        for e in self.ENGS:
            waits = self._waits(e, dict(need))
            if waits:
                self.prog[e].append((waits, None, None, 0, self.phase))

    def collective(self, snd, rcv):
        self.barrier()
        self.cc_cnt += 1
        groups = [[0, 1], [2, 3], [4, 5], [6, 7]]
        self.prog["pool"].append(([], (lambda e: e.collective_compute("AllReduce", ALU.add, replica_groups=groups,
                                                                      ins=[snd.ap().opt()], outs=[rcv.ap().opt()])),
                                  self.cc_sem, 1, self.phase))
        for e in self.ENGS:
            self.prog[e].append(([(self.cc_sem, self.cc_cnt)], None, None, 0, self.phase))

    def emit(self):
        nc = self.nc
        prog = self.prog
        with nc.Block() as block:
            def run(eng_obj, items):
                for (waits, fn, sem, inc, ph) in items:
                    for (ws, wv) in waits:
                        eng_obj.wait_ge(ws, wv)
                    if fn is not None:
                        ins = fn(eng_obj)
                        ins.then_inc(sem, inc)
                        if self.namemap is not None:
                            self.namemap[ins.ins.name] = ph

            @block.tensor
            def _(e):
                run(e, prog["pe"])

            @block.scalar
            def _(e):
                run(e, prog["act"])

            @block.vector
            def _(e):
                run(e, prog["dve"])

            @block.gpsimd
            def _(e):
                run(e, prog["pool"])

            @block.sync
            def _(e):
                run(e, prog["sp"])

    def mm(self, out, lhsT, rhs, start, stop, reads, writes):
        self.op("pe", lambda e: e.matmul(out, lhsT=lhsT, rhs=rhs, start=start, stop=stop), reads, writes)

    def tr(self, out, in_, ident, reads, writes):
        self.op("pe", lambda e: e.transpose(out=out, in_=in_, identity=ident), reads, writes)

    def act(self, out, in_, func, reads, writes, bias=0.0, scale=1.0, accum=None):
        if accum is None:
            self.op("act", lambda e: e.activation(out=out, in_=in_, func=func, bias=bias, scale=scale), reads, writes)
        else:
            self.op("act", lambda e: e.activation(out=out, in_=in_, func=func, bias=bias, scale=scale, accum_out=accum), reads, writes)

    def tt(self, eng, out, in0, in1, op, reads, writes):
        self.op(eng, lambda e: e.tensor_tensor(out=out, in0=in0, in1=in1, op=op), reads, writes)

    def ts(self, eng, out, in0, s1, s2, op0, op1, reads, writes):
        if s2 is None:
            self.op(eng, lambda e: e.tensor_scalar(out=out, in0=in0, scalar1=s1, scalar2=None, op0=op0), reads, writes)
        else:
            self.op(eng, lambda e: e.tensor_scalar(out=out, in0=in0, scalar1=s1, scalar2=s2, op0=op0, op1=op1), reads, writes)

    def stt(self, eng, out, in0, scalar, in1, op0, op1, reads, writes):
        self.op(eng, lambda e: e.scalar_tensor_tensor(out=out, in0=in0, scalar=scalar, in1=in1, op0=op0, op1=op1), reads, writes)

    def cp(self, eng, out, in_, reads, writes):
        if eng == "act":
            self.op("act", lambda e: e.copy(out=out, in_=in_), reads, writes)
        else:
            self.op(eng, lambda e: e.tensor_copy(out=out, in_=in_), reads, writes)

    def memset(self, eng, ap, val, writes):
        self.op(eng, lambda e: e.memset(ap, val), (), writes)

    def recip(self, out, in_, reads, writes):
        self.op("dve", lambda e: e.reciprocal(out=out, in_=in_), reads, writes)


K_ID, K_TRIP, K_TRIS, K_SELP, K_SELS, K_ONES, K_NEGP, K_NEGS, K_ROW0, K_ROWP, K_SEGCOL, K_NTRIP, K_NTRIS = 0, 1, 2, 3, 4, 5, 6, 7, 8, 24, 25, 26, 27
NCONST = 28


def make_consts():
    c = np.zeros((NCONST, 128, 128), np.float32)
    idx = np.arange(128)
    s = idx[:, None]
    t = idx[None, :]
    c[K_ID] = (s == t)
    c[K_TRIP] = (s <= t)
    same = (s // SL) == (t // SL)
    c[K_TRIS] = same & (s <= t)
    c[K_SELP] = (s == 127).astype(np.float32) - (s == t)
    last = (t // SL) * SL + SL - 1
    c[K_SELS] = (s == last).astype(np.float32) - (s == t)
    c[K_ONES] = 1.0
    c[K_NEGP] = np.where(s <= t, 0.0, NEG)
    c[K_NEGS] = np.where(same & (s <= t), 0.0, NEG)
    for j in range(NSEG):
        c[K_ROW0 + j] = (s == (SL * j + SL - 1)) * np.ones((1, 128))
    c[K_ROWP] = (s == 127) * np.ones((1, 128))
    c[K_SEGCOL][:, :NSEG] = ((idx[:, None] // SL) == np.arange(NSEG)[None, :])
    c[K_NTRIP] = -c[K_TRIP] / 16.0
    c[K_NTRIS] = -c[K_TRIS] / 16.0
    return np.ascontiguousarray(c.transpose(1, 0, 2))


def w_blocks():
    bl = []
    for (c0, n) in ((0, 6144), (C_GLR, 16), (C_Z, 4096), (C_XBC, 6144), (C_DT, 64), (C_G, 4096)):
        o = 0
        while o < n:
            w = min(512, n - o)
            bl.append((c0 + o, w))
            o += w
    return bl


def build_nc():
    nc = bass.Bass("TRN2", target_bir_lowering=False)

    def din(name, shape, dt=F32):
        return nc.dram_tensor(name, list(shape), dt, kind="ExternalInput").ap()

    def dout(name, shape, dt=F32):
        return nc.dram_tensor(name, list(shape), dt, kind="ExternalOutput").ap()

    def dint(name, shape, dt=F32):
        return nc.dram_tensor(name, list(shape), dt, kind="Internal").ap()

    xin = din("xin", [TOK, D])
    cc = din("cc", [2, 128, D])
    consts = din("consts", [128, NCONST, 128])
    flags = din("flags", [128, 2])
    st_gla = din("st_gla", [DEPTH, NSEG, 4, 256, 512])
    st_ssm = din("st_ssm", [DEPTH, NSEG, 64, 64, 128])
    st_conv = din("st_conv", [DEPTH, NSEG, 3, 6144])
    w_ada = din("w_ada", [DEPTH, D, 6 * D])
    b_ada = din("b_ada", [DEPTH, 6 * D])
    w_in = din("w_in", [DEPTH, D, NIN])
    w_gate = din("w_gla_gate", [DEPTH, 16, 1024])
    b_gate = din("b_gla_gate", [DEPTH, 1024])
    gla_norm = din("gla_norm", [DEPTH, 512])
    w_gproj = din("w_gla_proj", [DEPTH, D, D])
    conv_w = din("conv_w", [DEPTH, 4, 6144])
    conv_b = din("conv_b", [DEPTH, 6144])
    dt_bias = din("dt_bias", [DEPTH, 64])
    A_log = din("A_log", [DEPTH, 64])
    d_skip = din("d_skip", [DEPTH, 64])
    ssd_norm = din("ssd_norm", [DEPTH, 4096])
    w_sproj = din("w_ssd_proj", [DEPTH, 4096, D])
    w_mix = din("w_mix_out", [DEPTH, D, D])
    w_fin = din("w_ffn_in", [DEPTH, D, 2 * DFF])
    w_fout = din("w_ffn_out", [DEPTH, DFF, D])
    fnorm = din("final_norm", [1, D])

    y_out = dout("y", [TOK, D])
    gla_p = dout("gla_p", [DEPTH, 4, 256, 512])
    ssm_p = dout("ssm_p", [DEPTH, 64, 64, 128])
    conv_p = dout("conv_p", [DEPTH, 3, 6144])
    gla_s = dout("gla_s", [DEPTH, NSEG, 4, 256, 512])
    ssm_s = dout("ssm_s", [DEPTH, NSEG, 64, 64, 128])
    conv_s = dout("conv_s", [DEPTH, NSEG, 3, 6144])

    proj = [dint("proj%d" % l, [TOK, NIN]) for l in range(DEPTH)]
    xbp = [dint("xbp%d" % l, [3 + NPT * 128, 6144]) for l in range(DEPTH)]
    xbs = [dint("xbs%d" % l, [NSEG, 3 + SL, 6144]) for l in range(DEPTH)]
    xcs = [dint("xc%d" % l, [TOK, 6144]) for l in range(DEPTH)]
    mod = [dint("mod%d" % l, [2, 128, 6 * D]) for l in range(DEPTH)]
    xv = [[dint("xv%d_%d" % (l, v), [TOK, D]) for v in range(2)] for l in range(DEPTH)]
    oaT = [dint("oaT%d" % l, [NT, 128, 16, 128], BF16) for l in range(DEPTH)]
    ybT = [dint("ybT%d" % l, [NT, 128, 32, 128], BF16) for l in range(DEPTH)]
    actT = [dint("actT%d" % l, [NT, 128, 44, 128], BF16) for l in range(DEPTH)]
    o_loc = [dint("oloc%d" % l, [NPT * 128, 2048]) for l in range(DEPTH)]
    qg_d = [dint("qg%d" % l, [NPT, 128, 8, 128], BF16) for l in range(DEPTH)]
    y_loc = [dint("yloc%d" % l, [NPT * 128, 4096]) for l in range(DEPTH)]
    ct_d = [dint("ct%d" % l, [NPT, 128, 8, 128], BF16) for l in range(DEPTH)]
    eg_d = [dint("eg%d" % l, [NPT, 128, 64]) for l in range(DEPTH)]
    sloc_d = [dint("sloc%d" % l, [128, 4096]) for l in range(DEPTH)]
    gam_d = [dint("gam%d" % l, [128, 8]) for l in range(DEPTH)]
    hloc_d = [dint("hloc%d" % l, [128, 4096]) for l in range(DEPTH)]
    gamh_d = [dint("gamh%d" % l, [128, 64]) for l in range(DEPTH)]
    cv_snd = [nc.dram_tensor("cvsnd%d" % l, [128, 144], F32) for l in range(DEPTH)]
    cv_rcv = [nc.dram_tensor("cvrcv%d" % l, [128, 144], F32) for l in range(DEPTH)]
    gs_snd = [nc.dram_tensor("gssnd%d" % l, [128, 4096], F32) for l in range(DEPTH)]
    gs_rcv = [nc.dram_tensor("gsrcv%d" % l, [128, 4096], F32) for l in range(DEPTH)]
    hs_snd = [nc.dram_tensor("hssnd%d" % l, [128, 4096], F32) for l in range(DEPTH)]
    hs_rcv = [nc.dram_tensor("hsrcv%d" % l, [128, 4096], F32) for l in range(DEPTH)]

    with ExitStack() as es:
        S = Sched(nc, es)
        cst = S.sb(es, "cst", [128, NCONST, 128], F32, dma=True)
        identb = S.sb(es, "identb", [128, 128], BF16)
        S.dma("sp", cst[:], consts, sb_w=cst)
        flg = S.sb(es, "flg", [128, 2], F32, dma=True)
        S.dma("sp", flg[:], flags, sb_w=flg)
        S.cp("dve", identb[:], cst[:, K_ID, :], [cst], [identb])
        ident = cst[:, K_ID, :]

        def rows(i):
            return slice(i * 128, (i + 1) * 128)

        def grp(i):
            return 0 if i < NPT else 1

        def hgi(i):
            return (i // 4, (i % 4) * 128) if i < NPT else (NG, 0)

        def hview(hT, i, k):
            g_, o_ = hgi(i)
            return hT[g_][:, k, o_:o_ + 128]

        def phase_mod():
            with ExitStack() as pes:
                cct = S.sb(pes, "cct", [128, D], F32, dma=True)
                scb = S.sb(pes, "scb", [128, D], BF16)
                scT = [S.sb(pes, "scT%d" % g, [128, 16, 128], BF16) for g in range(2)]
                ptr = [S.ps(pes, "mptr%d" % i, [128, 8, 128], BF16) for i in range(2)]
                pm = [S.ps(pes, "pm%d" % i, [128, 512], F32) for i in range(4)]
                for g in range(2):
                    S.dma("sp", cct[:], cc[g], sb_w=cct)
                    S.act(scb[:], cct[:], AF.Silu, [cct], [scb])
                    for half in range(2):
                        p = ptr[half]
                        for k in range(8):
                            kk = half * 8 + k
                            S.tr(p[:, k, :], scb[:, kk * 128:(kk + 1) * 128], identb[:], [scb, identb], [p])
                        S.cp("dve" if half == 0 else "act", scT[g][:, half * 8:(half + 1) * 8, :], p[:], [p], [scT[g]])
                wa = [S.sb(pes, "wa%d" % i, [128, 16, 512], BF16, dma=True) for i in range(2)]
                bb = [S.sb(pes, "bb%d" % i, [128, 512], F32, dma=True) for i in range(2)]
                stg = [S.sb(pes, "mst%d" % i, [128, 512], F32, dma=True) for i in range(4)]
                it = 0
                for l in range(DEPTH):
                    wv = w_ada[l].rearrange("(kt p) n -> p kt n", p=128)
                    for cb in range(24):
                        cs_ = slice(cb * 512, (cb + 1) * 512)
                        w = wa[it % 2]
                        b = bb[it % 2]
                        S.dma("pool", w[:], wv[:, :, cs_], sb_w=w)
                        S.dma("sp", b[:], b_ada[l:l + 1, cs_].partition_broadcast(128), sb_w=b)
                        for g in range(2):
                            n = it * 2 + g
                            p = pm[n % 4]
                            s_ = stg[n % 4]
                            for k in range(16):
                                S.mm(p[:], scT[g][:, k, :], w[:, k, :], k == 0, k == 15, [scT[g], w], [p])
                            S.tt("dve", s_[:], p[:], b[:], ALU.add, [p, b], [s_])
                            S.dma("sp", mod[l][g, :, cs_], s_[:], sb_r=s_)
                        it += 1
            S.barrier()

        def phase_norm(xsrc, l, sh_i, sc_i, hT):
            with ExitStack() as pes:
                xt = [S.sb(pes, "nx%d" % i, [128, D], F32, dma=True) for i in range(2)]
                shb = S.sb(pes, "shb", [128, D], F32, dma=True)
                scb = S.sb(pes, "nscb", [128, D], F32, dma=True)
                junk = S.sb(pes, "njunk", [128, D], F32)
                ss = S.sb(pes, "nss", [128, 2], F32)
                hb = [S.sb(pes, "hb%d" % i, [128, D], BF16) for i in range(2)]
                ptr = [S.ps(pes, "nptr%d" % i, [128, 8, 128], BF16) for i in range(2)]
                n = 0
                for i in range(NT):
                    g = grp(i)
                    if i == 0 or i == NPT:
                        S.dma("sp", shb[:], mod[l][g, :, sh_i * D:(sh_i + 1) * D], sb_w=shb)
                        S.dma("sp", scb[:], mod[l][g, :, sc_i * D:(sc_i + 1) * D], sb_w=scb)
                        S.ts("dve", scb[:], scb[:], 1.0, None, ALU.add, None, [scb], [scb])
                    x = xt[i % 2]
                    S.dma("sp", x[:], xsrc[rows(i), :], sb_w=x)
                    S.memset("pool", ss[:], 0.0, [ss])
                    S.act(junk[:], x[:], AF.Square, [x, ss], [junk, ss], accum=ss[:, 0:1])
                    S.ts("dve", ss[:, 1:2], ss[:, 0:1], 1.0 / D, EPS, ALU.mult, ALU.add, [ss], [ss])
                    S.act(ss[:, 1:2], ss[:, 1:2], AF.Sqrt, [ss], [ss])
                    S.recip(ss[:, 1:2], ss[:, 1:2], [ss], [ss])
                    S.stt("dve", junk[:], x[:], ss[:, 1:2], scb[:], ALU.mult, ALU.mult, [x, ss, scb], [junk])
                    h = hb[i % 2]
                    S.tt("dve", h[:], junk[:], shb[:], ALU.add, [junk, shb], [h])
                    for half in range(2):
                        p = ptr[n % 2]
                        n += 1
                        for k in range(8):
                            kk = half * 8 + k
                            S.tr(p[:, k, :], h[:, kk * 128:(kk + 1) * 128], identb[:], [h, identb], [p])
                        g_, o_ = hgi(i)
                        S.cp("act" if half == 0 else "dve", hT[g_][:, half * 8:(half + 1) * 8, o_:o_ + 128],
                             p[:], [p], [hT[g_]])
            S.barrier()

        def phase_proj(l, hT):
            with ExitStack() as pes:
                wb = [S.sb(pes, "wb%d" % i, [128, 16, 512], BF16, dma=True) for i in range(2)]
                pp = [S.ps(pes, "pp%d" % i, [128, 512], F32) for i in range(4)]
                stg = [S.sb(pes, "pst%d" % i, [128, 512], F32, dma=True) for i in range(4)]
                dd = S.dbuf(pes, "dd_conv")
                S.dma("sp", xbs[l][:, 0:3, :], st_conv[l], evbuf=dd)
                wv = w_in[l].rearrange("(kt p) n -> p kt n", p=128)
                n = 0
                for bi, (c0, w) in enumerate(w_blocks()):
                    W = wb[bi % 2]
                    S.dma("pool", W[:, :, 0:w], wv[:, :, c0:c0 + w], sb_w=W)
                    isx = C_XBC <= c0 < C_DT
                    for i in range(NT):
                        p = pp[n % 4]
                        s_ = stg[n % 4]
                        n += 1
                        for k in range(16):
                            S.mm(p[:, 0:w], hview(hT, i, k), W[:, k, 0:w], k == 0, k == 15, [hT[hgi(i)[0]], W], [p])
                        S.cp("act" if n % 2 else "dve", s_[:, 0:w], p[:, 0:w], [p], [s_])
                        if isx:
                            xc0 = c0 - C_XBC
                            if i < NPT:
                                S.dma("sp", xbp[l][3 + i * 128:3 + (i + 1) * 128, xc0:xc0 + w], s_[:, 0:w], sb_r=s_)
                            else:
                                S.dma("sp", xbs[l][:, 3:3 + SL, xc0:xc0 + w], s_[:, 0:w], sb_r=s_)
                        else:
                            S.dma("sp", proj[l][rows(i), c0:c0 + w], s_[:, 0:w], sb_r=s_)
            S.barrier()
            fl3 = lambda ap: ap.rearrange("r c -> (r c)").rearrange("(p f) -> p f", p=128)
            with ExitStack() as pes:
                zt = S.sb(pes, "zt", [128, 144], F32, dma=True)
                S.dma("sp", zt[:], fl3(xbp[l][NPT * 128:NPT * 128 + 3, :]), sb_w=zt)
                S.ts("dve", zt[:], zt[:], flg[:, 1:2], None, ALU.mult, None, [zt, flg], [zt])
                S.dma("sp", cv_snd[l].ap(), zt[:], sb_r=zt)
            S.collective(cv_snd[l], cv_rcv[l])
            with ExitStack() as pes:
                zt = S.sb(pes, "zt2", [128, 144], F32, dma=True)
                S.dma("sp", zt[:], cv_rcv[l].ap(), sb_w=zt)
                S.ts("dve", zt[:], zt[:], flg[:, 0:1], None, ALU.mult, None, [zt, flg], [zt])
                S.dma("sp", fl3(xbp[l][0:3, :]), zt[:], sb_r=zt)
                dd = S.dbuf(pes, "dd_conv2")
                S.dma("sp", conv_p[l], xbp[l][NPT * 128:NPT * 128 + 3, :], evbuf=dd)
                S.dma("sp", conv_s[l], xbs[l][:, SL:SL + 3, :], evbuf=dd)

        def phase_conv(l):
            CH = 1536
            with ExitStack() as pes:
                cw = S.sb(pes, "cw", [128, 4, CH], F32, dma=True)
                cbs = S.sb(pes, "cbs", [128, CH], F32, dma=True)
                xs_ = [[S.sb(pes, "cx%d_%d" % (b, i), [128, CH], F32, dma=True) for i in range(4)] for b in range(2)]
                acc = [S.sb(pes, "cacc%d" % b, [128, CH], F32, dma=True) for b in range(2)]
                for c in range(4):
                    cs_ = slice(c * CH, (c + 1) * CH)
                    for t4 in range(4):
                        S.dma("sp", cw[:, t4, :], conv_w[l, t4:t4 + 1, cs_].partition_broadcast(128), sb_w=cw)
                    S.dma("sp", cbs[:], conv_b[l:l + 1, cs_].partition_broadcast(128), sb_w=cbs)
                    for i in range(NT):
                        X = xs_[i % 2]
                        a = acc[i % 2]
                        XA = Buf("xa", None)
                        XB = Buf("xb", None)
                        for t4 in range(4):
                            if i < NPT:
                                src = xbp[l][i * 128 + t4:i * 128 + t4 + 128, cs_]
                            else:
                                src = xbs[l][:, t4:t4 + SL, cs_]
                            S.dma("sp", X[t4][:], src, sb_w=X[t4])
                        for (eng_, c0_, c1_, Y) in (("dve", 0, 1024, XA), ("pool", 1024, CH, XB)):
                            sl_ = slice(c0_, c1_)
                            S.tt(eng_, X[3][:, sl_], X[3][:, sl_], cw[:, 3, sl_], ALU.mult, [X[3], cw], [Y])
                            for t4 in (2, 1, 0):
                                S.tt(eng_, X[t4][:, sl_], X[t4][:, sl_], cw[:, t4, sl_], ALU.mult, [X[t4], cw, Y], [Y])
                                S.tt(eng_, X[3][:, sl_], X[3][:, sl_], X[t4][:, sl_], ALU.add, [Y], [Y])
                            S.tt(eng_, X[3][:, sl_], X[3][:, sl_], cbs[:, sl_], ALU.add, [Y, cbs], [Y])
                        S.act(a[:], X[3][:], AF.Silu, [X[3], XA, XB], [a])
                        for t4 in range(4):
                            X[t4].rd.update(XA.wr)
                            X[t4].rd.update(XB.wr)
                        S.dma("act", xcs[l][rows(i), cs_], a[:], sb_r=a)
            S.barrier()

        def gla_post(l, i, osrc, obufs, rap, rbuf, gln, junk, ss, oa, oTb, nb):
            S.memset("pool", ss[:], 0.0, [ss])
            for h in range(4):
                S.act(junk[:], osrc[h], AF.Square, [obufs[h], ss], [junk, ss], accum=ss[:, h:h + 1])
            S.ts("dve", ss[:, 4:8], ss[:, 0:4], 1.0 / 512, EPS, ALU.mult, ALU.add, [ss], [ss])
            S.act(ss[:, 4:8], ss[:, 4:8], AF.Sqrt, [ss], [ss])
            S.recip(ss[:, 4:8], ss[:, 4:8], [ss], [ss])
            S.act(rap, rap, AF.Silu, [rbuf], [rbuf])
            S.tt("pool", rap.rearrange("p (h v) -> p h v", h=4), rap.rearrange("p (h v) -> p h v", h=4),
                 gln[:].unsqueeze(1).to_broadcast([128, 4, 512]), ALU.mult, [rbuf, gln], [rbuf])
            for h in range(4):
                S.stt("dve", oa[:, h * 512:(h + 1) * 512], osrc[h], ss[:, 4 + h:5 + h],
                      rap[:, h * 512:(h + 1) * 512], ALU.mult, ALU.mult, [obufs[h], ss, rbuf], [oa])
            for q4 in range(4):
                p = nb()
                for c in range(4):
                    cc_ = q4 * 4 + c
                    S.tr(p[:, c * 128:(c + 1) * 128], oa[:, cc_ * 128:(cc_ + 1) * 128], ident, [oa, cst], [p])
                S.cp("act", oTb[:, q4 * 4:(q4 + 1) * 4, :], p[:].rearrange("p (c t) -> p c t", c=4), [p], [oTb])
            S.dma("act", oaT[l][i], oTb[:], sb_r=oTb)

        def phase_gla(l, sample):
            tiles = [NPT] if sample else list(range(NPT))
            K_TRI = K_TRIS if sample else K_TRIP
            K_NTRI = K_NTRIS if sample else K_NTRIP
            K_SEL = K_SELS if sample else K_SELP
            nseg = NSEG if sample else 1
            with ExitStack() as pes:
                pin = S.sb(pes, "gpin", [128, 6160], F32, dma=True)
                wga = S.sb(pes, "wga", [33, 1024], F32, dma=True)
                gln = S.sb(pes, "gln", [128, 512], F32, dma=True)
                glrT = S.sb(pes, "glrT", [33, 128], F32)
                l1 = S.sb(pes, "gl1", [128, 1024], F32)
                btok = S.sb(pes, "gbtok", [128, 1024], F32)
                kkb = S.sb(pes, "gkkb", [128, 1024], BF16)
                eb = S.sb(pes, "geb", [128, 8, 128], F32)
                enb = S.sb(pes, "genb", [128, 8, 128], F32)
                qTb = S.sb(pes, "gqTb", [128, 8, 128], BF16)
                kTb = S.sb(pes, "gkTb", [128, 8, 128], BF16)
                attb = S.sb(pes, "gattb", [128, 4, 128], BF16)
                vb = S.sb(pes, "gvb", [128, 2048], BF16)
                junk = S.sb(pes, "gjunk", [128, 512], F32)
                ss = S.sb(pes, "gss", [128, 8], F32)
                oa = S.sb(pes, "goa", [128, 2048], F32, dma=True)
                oTb = S.sb(pes, "goTb", [128, 16, 128], BF16, dma=True)
                if not sample:
                    qgb = S.sb(pes, "gqgb", [128, 8, 128], BF16, dma=True)
                    gam = S.sb(pes, "ggam", [128, 8], F32, dma=True)
                    S.memset("pool", gam[:], 1.0, [gam])
                Sb = S.sb(pes, "gSb", [128, 8, 512], BF16)
                if sample:
                    Sst = [S.sb(pes, "gS%d" % i, [128, 8, 512], F32, dma=True) for i in range(3)]
                    Sb2 = S.sb(pes, "gSb2", [128, 8, 512], BF16)
                    qm = [S.sb(pes, "gqm%d" % i, [128, 8, 128], BF16) for i in range(2)]
                    kkm = [S.sb(pes, "gkkm%d" % i, [128, 1024], BF16) for i in range(2)]
                    for b_ in qm:
                        S.memset("pool", b_[:], 0.0, [b_])
                    Sbs = [Sb, Sb2]
                else:
                    Sst = [S.sb(pes, "gS", [128, 8, 512], F32, dma=True)]
                    S.memset("pool", Sst[0][:], 0.0, [Sst[0]])
                    S.memset("pool", Sb[:], 0.0, [Sb])
                P = [S.ps(pes, "gP%d" % i, [128, 512], F32) for i in range(8)]
                rot = [0]

                def nb():
                    b_ = P[rot[0] % 4]
                    rot[0] += 1
                    return b_
                PO = P[4:8]
                S.memset("pool", wga[:], 0.0, [wga])
                S.dma("sp", wga[0:16, :], w_gate[l], sb_w=wga)
                S.dma("sp", wga[32:33, :], b_gate[l:l + 1, :], sb_w=wga)
                S.dma("sp", gln[:], gla_norm[l:l + 1, :].partition_broadcast(128), sb_w=gln)
                S.memset("pool", glrT[:], 0.0, [glrT])
                S.memset("pool", glrT[32:33, :], 1.0, [glrT])
                tri = cst[:, K_TRI, :]
                ntri = cst[:, K_NTRI, :]
                selI = cst[:, K_SEL, :]
                for i in tiles:
                    S.dma("sp", pin[:], proj[l][rows(i), 0:6160], sb_w=pin)
                    q_ = lambda c: pin[:, C_Q + c * 128:C_Q + (c + 1) * 128]
                    k_ = lambda c: pin[:, C_K + c * 128:C_K + (c + 1) * 128]
                    p = nb()
                    S.tr(p[0:16, 0:128], pin[:, C_GLR:C_GLR + 16], ident, [pin, cst], [p])
                    S.cp("dve", glrT[0:16, :], p[0:16, 0:128], [p], [glrT])
                    for hf in range(2):
                        p = nb()
                        S.mm(p[:], glrT[:, :], wga[:, hf * 512:(hf + 1) * 512], True, True, [glrT, wga], [p])
                        S.act(l1[:, hf * 512:(hf + 1) * 512], p[:], AF.Exp, [p], [l1], scale=-1.0)
                    S.act(l1[:], l1[:], AF.Ln, [l1], [l1], bias=1.0)
                    for hf in range(2):
                        p = nb()
                        S.mm(p[:], ntri, l1[:, hf * 512:(hf + 1) * 512], True, True, [cst, l1], [p])
                        S.cp("act", btok[:, hf * 512:(hf + 1) * 512], p[:], [p], [btok])
                    for hf in range(2):
                        p = nb()
                        for c in range(4):
                            cc_ = hf * 4 + c
                            S.mm(p[:, c * 128:(c + 1) * 128], l1[:, cc_ * 128:(cc_ + 1) * 128], ntri, True, True, [l1, cst], [p])
                        pv = p[:].rearrange("p (c t) -> p c t", c=4)
                        S.act(eb[:, hf * 4:(hf + 1) * 4, :], pv, AF.Exp, [p], [eb])
                        S.act(enb[:, hf * 4:(hf + 1) * 4, :], pv, AF.Exp, [p], [enb], scale=-1.0)
                    for hf in range(2):
                        p = nb()
                        S.mm(p[:], selI, btok[:, hf * 512:(hf + 1) * 512], True, True, [cst, btok], [p])
                        S.act(l1[:, hf * 512:(hf + 1) * 512], p[:], AF.Exp, [p], [l1])
                    S.tt("dve", kkb[:], pin[:, C_K:C_K + 1024], l1[:], ALU.mult, [pin, l1], [kkb])
                    for hf in range(2):
                        p = nb()
                        for c in range(4):
                            S.tr(p[:, c * 128:(c + 1) * 128], q_(hf * 4 + c), ident, [pin, cst], [p])
                        S.stt("dve", qTb[:, hf * 4:(hf + 1) * 4, :], p[:].rearrange("p (c t) -> p c t", c=4), 0.0625,
                              eb[:, hf * 4:(hf + 1) * 4, :], ALU.mult, ALU.mult, [p, eb], [qTb])
                        p = nb()
                        for c in range(4):
                            S.tr(p[:, c * 128:(c + 1) * 128], k_(hf * 4 + c), ident, [pin, cst], [p])
                        S.tt("dve", kTb[:, hf * 4:(hf + 1) * 4, :], p[:].rearrange("p (c t) -> p c t", c=4),
                             enb[:, hf * 4:(hf + 1) * 4, :], ALU.mult, [p, enb], [kTb])
                    p = nb()
                    for h in range(4):
                        for dk in range(2):
                            S.mm(p[:, h * 128:(h + 1) * 128], kTb[:, h * 2 + dk, :], qTb[:, h * 2 + dk, :], dk == 0, dk == 1, [kTb, qTb], [p])
                    S.tt("dve", attb[:], p[:].rearrange("p (h t) -> p h t", h=4), tri.unsqueeze(1).to_broadcast([128, 4, 128]),
                         ALU.mult, [p, cst], [attb])
                    S.cp("act", vb[:], pin[:, C_V:C_V + 2048], [pin], [vb])
                    for h in range(4):
                        S.mm(PO[h][:], attb[:, h, :], vb[:, h * 512:(h + 1) * 512], True, False, [attb, vb], [PO[h]])
                    for j in range(nseg):
                        if sample:
                            Sj = Sst[j % 3]
                            if j == 0:
                                S.dma("sp", Sj[:].rearrange("p (h t) v -> p h t v", h=4),
                                      st_gla[l, 0].rearrange("h (t p) v -> p h t v", p=128), sb_w=Sj)
                            if j + 1 < nseg:
                                Sn = Sst[(j + 1) % 3]
                                S.dma("sp", Sn[:].rearrange("p (h t) v -> p h t v", h=4),
                                      st_gla[l, j + 1].rearrange("h (t p) v -> p h t v", p=128), sb_w=Sn)
                            Sb = Sbs[j % 2]
                            S.cp("pool", Sb[:], Sj[:], [Sj], [Sb])
                            qmj = qm[j % 2]
                            if j >= 2:
                                jo = j - 2
                                S.memset("pool", qmj[:, :, jo * SL:(jo + 1) * SL], 0.0, [qmj])
                            S.cp("pool", qmj[:, :, j * SL:(j + 1) * SL], qTb[:, :, j * SL:(j + 1) * SL], [qTb], [qmj])
                            kkj = kkm[j % 2]
                            S.ts("dve", kkj[:], kkb[:], cst[:, K_SEGCOL, j:j + 1], None, ALU.mult, None, [kkb, cst], [kkj])
                            ql, kl = qmj, kkj
                        else:
                            Sj = Sst[0]
                            ql, kl = qTb, kkb
                        last = (j == nseg - 1)
                        for h in range(4):
                            for dk in range(2):
                                S.mm(PO[h][:], ql[:, h * 2 + dk, :], Sb[:, h * 2 + dk, :], False, last and dk == 1, [ql, Sb], [PO[h]])
                        tl = (j * SL + SL - 1) if sample else 127
                        for c in range(8):
                            h = c // 2
                            p = nb()
                            S.mm(p[:], kl[:, c * 128:(c + 1) * 128], vb[:, h * 512:(h + 1) * 512], True, True, [kl, vb], [p])
                            S.stt("dve", Sj[:, c, :], Sj[:, c, :], eb[:, c, tl:tl + 1], p[:], ALU.mult, ALU.add, [Sj, eb, p], [Sj])
                        if sample:
                            S.dma("act", gla_s[l, j].rearrange("h (t p) v -> p h t v", p=128),
                                  Sj[:].rearrange("p (h t) v -> p h t v", h=4), sb_r=Sj)
                        else:
                            S.cp("pool", Sb[:], Sj[:], [Sj], [Sb])
                    if sample:
                        gla_post(l, i, [PO[h][:] for h in range(4)], PO, pin[:, C_R:C_R + 2048], pin, gln, junk, ss, oa, oTb, nb)
                    else:
                        for h in range(4):
                            S.cp("act", oa[:, h * 512:(h + 1) * 512], PO[h][:], [PO[h]], [oa])
                        S.dma("sp", o_loc[l][rows(i), :], oa[:], sb_r=oa)
                        S.tt("dve", qgb[:], qTb[:], gam[:].unsqueeze(2).to_broadcast([128, 8, 128]), ALU.mult, [qTb, gam], [qgb])
                        S.dma("sp", qg_d[l][i], qgb[:], sb_r=qgb)
                        S.tt("dve", gam[:], gam[:], eb[:, :, 127], ALU.mult, [gam, eb], [gam])
                if not sample:
                    Sj = Sst[0]
                    S.dma("sp", sloc_d[l], Sj[:].rearrange("p c v -> p (c v)"), sb_r=Sj)
                    S.dma("sp", gam_d[l], gam[:], sb_r=gam)
                    S.ts("dve", Sj[:], Sj[:], flg[:, 1:2], None, ALU.mult, None, [Sj, flg], [Sj])
                    S.dma("sp", gs_snd[l].ap(), Sj[:].rearrange("p c v -> p (c v)"), sb_r=Sj)
            if not sample:
                S.collective(gs_snd[l], gs_rcv[l])
            else:
                S.barrier()

        def phase_gla2(l):
            with ExitStack() as pes:
                Sin = S.sb(pes, "g2Sin", [128, 8, 512], F32, dma=True)
                Sloc = S.sb(pes, "g2Sloc", [128, 8, 512], F32, dma=True)
                Sinb = S.sb(pes, "g2Sinb", [128, 8, 512], BF16)
                gam = S.sb(pes, "g2gam", [128, 8], F32, dma=True)
                gln = S.sb(pes, "g2gln", [128, 512], F32, dma=True)
                ot = [S.sb(pes, "g2ot%d" % i, [128, 2048], F32, dma=True) for i in range(2)]
                rt = [S.sb(pes, "g2rt%d" % i, [128, 2048], F32, dma=True) for i in range(2)]
                qgt = [S.sb(pes, "g2qg%d" % i, [128, 8, 128], BF16, dma=True) for i in range(2)]
                junk = S.sb(pes, "g2junk", [128, 512], F32)
                ss = S.sb(pes, "g2ss", [128, 8], F32)
                oa = S.sb(pes, "g2oa", [128, 2048], F32)
                oTb = S.sb(pes, "g2oTb", [128, 16, 128], BF16, dma=True)
                P = [S.ps(pes, "g2P%d" % i, [128, 512], F32) for i in range(8)]
                rot = [0]

                def nb():
                    b_ = P[rot[0] % 4]
                    rot[0] += 1
                    return b_
                S.dma("sp", gln[:], gla_norm[l:l + 1, :].partition_broadcast(128), sb_w=gln)
                S.dma("sp", Sin[:].rearrange("p c v -> p (c v)"), gs_rcv[l].ap(), sb_w=Sin)
                S.dma("sp", Sloc[:].rearrange("p c v -> p (c v)"), sloc_d[l], sb_w=Sloc)
                S.dma("sp", gam[:], gam_d[l], sb_w=gam)
                S.ts("dve", Sin[:], Sin[:], flg[:, 0:1], None, ALU.mult, None, [Sin, flg], [Sin])
                S.cp("pool", Sinb[:], Sin[:], [Sin], [Sinb])
                for c in range(8):
                    S.stt("dve", Sloc[:, c, :], Sin[:, c, :], gam[:, c:c + 1], Sloc[:, c, :], ALU.mult, ALU.add, [Sin, gam, Sloc], [Sloc])
                S.dma("sp", gla_p[l].rearrange("h (t p) v -> p h t v", p=128), Sloc[:].rearrange("p (h t) v -> p h t v", h=4), sb_r=Sloc)
                for i in range(NPT):
                    o_ = ot[i % 2]
                    r_ = rt[i % 2]
                    q_ = qgt[i % 2]
                    S.dma("sp", o_[:], o_loc[l][rows(i), :], sb_w=o_)
                    S.dma("sp", r_[:], proj[l][rows(i), C_R:C_R + 2048], sb_w=r_)
                    S.dma("sp", q_[:], qg_d[l][i], sb_w=q_)
                    for h in range(4):
                        for dk in range(2):
                            S.mm(P[4 + h][:], q_[:, h * 2 + dk, :], Sinb[:, h * 2 + dk, :], dk == 0, dk == 1, [q_, Sinb], [P[4 + h]])
                        S.tt("dve", o_[:, h * 512:(h + 1) * 512], o_[:, h * 512:(h + 1) * 512], P[4 + h][:], ALU.add, [o_, P[4 + h]], [o_])
                    gla_post(l, i, [o_[:, h * 512:(h + 1) * 512] for h in range(4)], [o_] * 4, r_[:], r_, gln, junk, ss, oa, oTb, nb)
            S.barrier()

        def ssd_post(l, i, yy, xap, xbuf, zt, dsk, ssn, junk, ss, yTb, nb):
            xv_ = xap.rearrange("p (h d) -> p h d", h=64)
            S.tt("pool", xv_, xv_, dsk[:].unsqueeze(2).to_broadcast([128, 64, 64]), ALU.mult, [xbuf, dsk], [xbuf])
            S.tt("pool", yy[:], yy[:], xap, ALU.add, [yy, xbuf], [yy])
            S.act(zt[:], zt[:], AF.Silu, [zt], [zt])
            S.tt("dve", yy[:], yy[:], zt[:], ALU.mult, [yy, zt], [yy])
            S.memset("pool", ss[:], 0.0, [ss])
            for g in range(8):
                S.act(junk[:], yy[:, g * 512:(g + 1) * 512], AF.Square, [yy, ss], [junk, ss], accum=ss[:, g:g + 1])
            S.ts("dve", ss[:, 8:16], ss[:, 0:8], 1.0 / 512, EPS, ALU.mult, ALU.add, [ss], [ss])
            S.act(ss[:, 8:16], ss[:, 8:16], AF.Sqrt, [ss], [ss])
            S.recip(ss[:, 8:16], ss[:, 8:16], [ss], [ss])
            for g in range(8):
                S.stt("dve", yy[:, g * 512:(g + 1) * 512], yy[:, g * 512:(g + 1) * 512], ss[:, 8 + g:9 + g],
                      ssn[:, g * 512:(g + 1) * 512], ALU.mult, ALU.mult, [yy, ss, ssn], [yy])
            for q4 in range(8):
                p = nb()
                for c in range(4):
                    cc_ = q4 * 4 + c
                    S.tr(p[:, c * 128:(c + 1) * 128], yy[:, cc_ * 128:(cc_ + 1) * 128], ident, [yy, cst], [p])
                S.cp("act" if q4 % 2 else "dve", yTb[:, q4 * 4:(q4 + 1) * 4, :], p[:].rearrange("p (c t) -> p c t", c=4), [p], [yTb])
            S.dma("act", ybT[l][i], yTb[:], sb_r=yTb)

        def phase_ssd(l, sample):
            tiles = [NPT] if sample else list(range(NPT))
            K_TRI = K_TRIS if sample else K_TRIP
            K_SEL = K_SELS if sample else K_SELP
            K_NEGM = K_NEGS if sample else K_NEGP
            nseg = NSEG if sample else 1
            with ExitStack() as pes:
                xc = S.sb(pes, "dxc", [128, 6144], F32, dma=True)
                zt = S.sb(pes, "dz", [128, 4096], F32, dma=True)
                dtt = S.sb(pes, "ddt", [128, 64], F32, dma=True)
                dtb = S.sb(pes, "ddtb", [128, 64], F32, dma=True)
                aneg = S.sb(pes, "daneg", [128, 64], F32, dma=True)
                dsk = S.sb(pes, "ddsk", [128, 64], F32, dma=True)
                ssn = S.sb(pes, "dssn", [128, 4096], F32, dma=True)
                negm8 = S.sb(pes, "dnegm8", [128, 4, 128], F32)
                sm = S.sb(pes, "dsm", [128, 10, 64], F32)
                xsb = S.sb(pes, "dxsb", [128, 4096], BF16)
                Bb = S.sb(pes, "dBb", [128, 1024], BF16)
                BTb = S.sb(pes, "dBTb", [128, 8, 128], BF16)
                CTb = S.sb(pes, "dCTb", [128, 8, 128], BF16, dma=True)
                xdl = S.sb(pes, "dxdl", [128, 4096], BF16)
                nd_ = 2
                diag = [S.sb(pes, "ddiag%d" % i, [128, 8, 128], F32) for i in range(nd_)]
                seg = [S.sb(pes, "dseg%d" % i, [128, 8, 128], F32) for i in range(nd_)]
                WTb = [S.sb(pes, "dWTb%d" % i, [128, 8, 128], BF16) for i in range(nd_)]
                yy = S.sb(pes, "dy", [128, 4096], F32, dma=True)
                tmp = S.sb(pes, "dtmp", [128, 512], F32)
                tmps = [tmp, S.sb(pes, "dtmp2", [128, 512], F32)]
                junk = S.sb(pes, "djunk", [128, 512], F32)
                ss = S.sb(pes, "dss", [128, 16], F32)
                yTb = S.sb(pes, "dyTb", [128, 32, 128], BF16, dma=True)
                hT = S.sb(pes, "dhT", [128, 4096], F32, dma=True) if not sample else None
                hTb = S.sb(pes, "dhTb", [128, 4096], BF16)
                if sample:
                    hjs = [S.sb(pes, "dhj%d" % i, [128, 32, 128], F32, dma=True) for i in range(2)]
                    hj = hjs[0]
                    hjb = S.sb(pes, "dhjb", [128, 32, 128], BF16)
                    decc = S.sb(pes, "ddecc", [128, 32], F32)
                    cm = [S.sb(pes, "dcm%d" % i, [128, 8, 128], BF16) for i in range(2)]
                    Bm = [S.sb(pes, "dBm%d" % i, [128, 1024], BF16) for i in range(2)]
                    for b_ in cm:
                        S.memset("pool", b_[:], 0.0, [b_])
                else:
                    hj = yy
                    egt = S.sb(pes, "degt", [128, 64], F32, dma=True)
                    gamh = S.sb(pes, "dgamh", [128, 64], F32, dma=True)
                    S.memset("pool", gamh[:], 1.0, [gamh])
                    S.memset("pool", hT[:], 0.0, [hT])
                    S.memset("pool", hTb[:], 0.0, [hTb])
                NPB = 6 if sample else 8
                P = [S.ps(pes, "dP%d" % i, [128, 512], F32) for i in range(NPB)]
                if sample:
                    PB = [S.ps(pes, "dPB%d" % i, [128, 8, 128], BF16) for i in range(2)]
                rot = [0]

                def nb():
                    b_ = P[rot[0] % NPB]
                    rot[0] += 1
                    return b_
                DTP, LND, AA, CS, CS2, ECS, DL, DEC, T1, T2 = range(10)
                S.dma("sp", dtb[:], dt_bias[l:l + 1, :].partition_broadcast(128), sb_w=dtb)
                S.dma("sp", aneg[:], A_log[l:l + 1, :].partition_broadcast(128), sb_w=aneg)
                S.dma("sp", dsk[:], d_skip[l:l + 1, :].partition_broadcast(128), sb_w=dsk)
                S.dma("sp", ssn[:], ssd_norm[l:l + 1, :].partition_broadcast(128), sb_w=ssn)
                S.act(aneg[:], aneg[:], AF.Exp, [aneg], [aneg])
                S.ts("dve", aneg[:], aneg[:], -1.0, None, ALU.mult, None, [aneg], [aneg])
                for c in range(4):
                    S.cp("dve", negm8[:, c, :], cst[:, K_NEGM, :], [cst], [negm8])
                tri = cst[:, K_TRI, :]
                selI = cst[:, K_SEL, :]
                ones = cst[:, K_ONES, :]

                def load_h(j):
                    hj = hjs[j % 2]
                    if j == 0:
                        S.dma("sp", hj[:], st_ssm[l, 0].rearrange("h p n -> (h p) n").rearrange("(c q) n -> q c n", q=128), sb_w=hj)
                    if j + 1 < NSEG:
                        hn = hjs[(j + 1) % 2]
                        S.dma("sp", hn[:], st_ssm[l, j + 1].rearrange("h p n -> (h p) n").rearrange("(c q) n -> q c n", q=128), sb_w=hn)
                    S.cp("act", hjb[:, 0:16, :], hj[:, 0:16, :], [hj], [hjb])
                    S.cp("pool", hjb[:, 16:32, :], hj[:, 16:32, :], [hj], [hjb])
                    for q8 in range(4):
                        pb_ = PB[q8 % 2]
                        for c in range(8):
                            S.tr(pb_[:, c, :], hjb[:, q8 * 8 + c, :], identb[:], [hjb, identb], [pb_])
                        S.cp("dve" if q8 % 2 else "act", hTb[:, q8 * 1024:(q8 + 1) * 1024], pb_[:].rearrange("p c q -> p (c q)"), [pb_], [hTb])

                def store_h(dst, j=0):
                    hj = hjs[j % 2] if sample else yy
                    for q4 in range(8):
                        p = nb()
                        for c in range(4):
                            cc_ = q4 * 4 + c
                            S.tr(p[:, c * 128:(c + 1) * 128], hT[:, cc_ * 128:(cc_ + 1) * 128], ident, [hT, cst], [p])
                        S.cp("act" if q4 % 2 else "dve", hj[:, q4 * 4:(q4 + 1) * 4, :] if sample else
                             hj[:, q4 * 512:(q4 + 1) * 512].rearrange("p (c n) -> p c n", c=4),
                             p[:].rearrange("p (c n) -> p c n", c=4), [p], [hj])
                    src = hj[:] if sample else hj[:].rearrange("p (c n) -> p c n", c=32)
                    S.dma("act", dst.rearrange("h p n -> (h p) n").rearrange("(c q) n -> q c n", q=128), src, sb_r=hj)

                for i in tiles:
                    S.dma("sp", xc[:], xcs[l][rows(i), :], sb_w=xc)
                    if sample:
                        S.dma("sp", zt[:], proj[l][rows(i), C_Z:C_Z + 4096], sb_w=zt)
                    S.dma("sp", dtt[:], proj[l][rows(i), C_DT:C_DT + 64], sb_w=dtt)
                    S.tt("dve", sm[:, T1, :], dtt[:], dtb[:], ALU.add, [dtt, dtb], [sm])
                    S.act(sm[:, T1, :], sm[:, T1, :], AF.Exp, [sm], [sm])
                    S.act(sm[:, DTP, :], sm[:, T1, :], AF.Ln, [sm], [sm], bias=1.0)
                    S.act(sm[:, LND, :], sm[:, DTP, :], AF.Ln, [sm], [sm])
                    S.tt("dve", sm[:, AA, :], sm[:, DTP, :], aneg[:], ALU.mult, [sm, aneg], [sm])
                    p = nb()
                    S.mm(p[:, 0:64], tri, sm[:, AA, :], True, True, [cst, sm], [p])
                    S.cp("dve", sm[:, CS, :], p[:, 0:64], [p], [sm])
                    S.tt("dve", sm[:, CS2, :], sm[:, CS, :], sm[:, LND, :], ALU.subtract, [sm], [sm])
                    S.act(sm[:, ECS, :], sm[:, CS, :], AF.Exp, [sm], [sm])
                    p = nb()
                    S.mm(p[:, 0:64], selI, sm[:, CS, :], True, True, [cst, sm], [p])
                    S.act(sm[:, DL, :], p[:, 0:64], AF.Exp, [p], [sm])
                    S.tt("dve", sm[:, DL, :], sm[:, DL, :], sm[:, DTP, :], ALU.mult, [sm], [sm])
                    S.cp("pool", xsb[:], xc[:, 0:4096], [xc], [xsb])
                    S.cp("act", Bb[:], xc[:, 4096:5120], [xc], [Bb])
                    for (src0, dstb) in ((4096, BTb), (5120, CTb)):
                        for hf in range(2):
                            p = nb()
                            for c in range(4):
                                g = hf * 4 + c
                                S.tr(p[:, c * 128:(c + 1) * 128], xc[:, src0 + g * 128:src0 + (g + 1) * 128], ident, [xc, cst], [p])
                            S.cp("act" if hf else "dve", dstb[:, hf * 4:(hf + 1) * 4, :], p[:].rearrange("p (c t) -> p c t", c=4), [p], [dstb])
                    S.tt("dve", xdl[:].rearrange("p (h d) -> p h d", h=64), xc[:, 0:4096].rearrange("p (h d) -> p h d", h=64),
                         sm[:, DL, :].unsqueeze(2).to_broadcast([128, 64, 64]), ALU.mult, [xc, sm], [xdl])
                    if sample:
                        S.memset("pool", yy[:], 0.0, [yy])
                    for j in range(nseg):
                        if sample:
                            load_h(j)
                            cmj = cm[j % 2]
                            if j >= 2:
                                jo = j - 2
                                S.memset("pool", cmj[:, :, jo * SL:(jo + 1) * SL], 0.0, [cmj])
                            S.cp("pool", cmj[:, :, j * SL:(j + 1) * SL], CTb[:, :, j * SL:(j + 1) * SL], [CTb], [cmj])
                            Bmj = Bm[j % 2]
                            S.ts("dve", Bmj[:], Bb[:], cst[:, K_SEGCOL, j:j + 1], None, ALU.mult, None, [Bb, cst], [Bmj])
                            for g in range(8):
                                p = nb()
                                S.mm(p[:], cmj[:, g, :], hTb[:, g * 512:(g + 1) * 512], True, True, [cmj, hTb], [p])
                                S.tt("dve", yy[:, g * 512:(g + 1) * 512], yy[:, g * 512:(g + 1) * 512], p[:], ALU.add, [yy, p], [yy])
                            Bl = Bmj
                            krow = K_ROW0 + j
                        else:
                            Bl = Bb
                            krow = K_ROWP
                        if sample:
                            hj = hjs[j % 2]
                            p = nb()
                            S.mm(p[:, 0:32], cst[:, krow, :], sm[:, CS, 0:64:2], True, True, [cst, sm], [p])
                            S.mm(p[:, 32:64], cst[:, krow, :], sm[:, CS, 1:64:2], True, True, [cst, sm], [p])
                            S.act(decc[0:64, :], p[0:64, 0:32], AF.Exp, [p], [decc])
                            S.act(decc[64:128, :], p[64:128, 32:64], AF.Exp, [p], [decc])
                            for k4 in range(8):
                                p = nb()
                                for c in range(4):
                                    cc_ = k4 * 4 + c
                                    S.mm(p[:, c * 128:(c + 1) * 128], xdl[:, cc_ * 128:(cc_ + 1) * 128], Bl[:, k4 * 128:(k4 + 1) * 128],
                                         True, True, [xdl, Bl], [p])
                                hv = hj[:, k4 * 4:(k4 + 1) * 4, :]
                                S.tt("pool" if k4 % 2 else "dve", hv, hv, decc[:, k4 * 4:(k4 + 1) * 4].unsqueeze(2).to_broadcast([128, 4, 128]),
                                     ALU.mult, [hj, decc], [hj])
                                S.tt("dve", hv, hv, p[:].rearrange("p (c n) -> p c n", c=4), ALU.add, [hj, p], [hj])
                            S.dma("act", ssm_s[l, j].rearrange("h p n -> (h p) n").rearrange("(c q) n -> q c n", q=128), hj[:], sb_r=hj)
                    def stage_a(g):
                        pc = nb()
                        S.mm(pc[:, 0:128], BTb[:, g, :], CTb[:, g, :], True, True, [BTb, CTb], [pc])
                        dg = diag[g % nd_]
                        S.tt("pool", dg[:], ident.unsqueeze(1).to_broadcast([128, 8, 128]),
                             sm[:, CS, g * 8:(g + 1) * 8].unsqueeze(2).to_broadcast([128, 8, 128]), ALU.mult, [cst, sm], [dg])
                        sg = seg[g % nd_]
                        for hf in range(2):
                            p = nb()
                            S.mm(p[:], ones, dg[:, hf * 4:(hf + 1) * 4, :], True, False, [cst, dg], [p])
                            S.mm(p[:], ident, negm8[:], False, True, [cst, negm8], [p])
                            S.tt("dve", sg[:, hf * 4:(hf + 1) * 4, :], p[:].rearrange("p (h t) -> p h t", h=4),
                                 sm[:, CS2, g * 8 + hf * 4:g * 8 + hf * 4 + 4].unsqueeze(2).to_broadcast([128, 4, 128]), ALU.subtract, [p, sm], [sg])
                        S.act(sg[:], sg[:], AF.Exp, [sg], [sg])
                        wt = WTb[g % nd_]
                        S.tt("dve", wt[:], sg[:], pc[:, 0:128].unsqueeze(1).to_broadcast([128, 8, 128]), ALU.mult, [sg, pc], [wt])
                        return wt

                    def stage_b(g, wt):
                        py = nb()
                        for h in range(8):
                            hh = g * 8 + h
                            S.mm(py[:, h * 64:(h + 1) * 64], wt[:, h, :], xsb[:, hh * 64:(hh + 1) * 64], True, True, [wt, xsb], [py])
                        ecsb = sm[:, ECS, g * 8:(g + 1) * 8].unsqueeze(2).to_broadcast([128, 8, 64])
                        yg = yy[:, g * 512:(g + 1) * 512]
                        if sample:
                            S.tt("pool", yg.rearrange("p (h d) -> p h d", h=8), yg.rearrange("p (h d) -> p h d", h=8), ecsb, ALU.mult, [yy, sm], [yy])
                            S.tt("dve", yg, yg, py[:], ALU.add, [yy, py], [yy])
                        else:
                            pz = nb()
                            S.mm(pz[:], CTb[:, g, :], hTb[:, g * 512:(g + 1) * 512], True, True, [CTb, hTb], [pz])
                            tq = tmps[g % 2]
                            S.tt("dve", tq[:].rearrange("p (h d) -> p h d", h=8), pz[:].rearrange("p (h d) -> p h d", h=8), ecsb, ALU.mult, [pz, sm], [tq])
                            S.tt("dve", yg, tq[:], py[:], ALU.add, [tq, py], [yy])

                    wt_next = stage_a(0)
                    for g in range(8):
                        wt_cur = wt_next
                        if g + 1 < 8:
                            wt_next = stage_a(g + 1)
                        stage_b(g, wt_cur)
                    if not sample:
                        p = nb()
                        S.mm(p[:, 0:64], cst[:, K_ROWP, :], sm[:, CS, :], True, True, [cst, sm], [p])
                        S.act(sm[:, DEC, :], p[:, 0:64], AF.Exp, [p], [sm])
                        for g in range(8):
                            p = nb()
                            S.mm(p[:], Bb[:, g * 128:(g + 1) * 128], xdl[:, g * 512:(g + 1) * 512], True, True, [Bb, xdl], [p])
                            hv = hT[:, g * 512:(g + 1) * 512].rearrange("p (h d) -> p h d", h=8)
                            S.tt("pool", hv, hv, sm[:, DEC, g * 8:(g + 1) * 8].unsqueeze(2).to_broadcast([128, 8, 64]), ALU.mult, [hT, sm], [hT])
                            S.tt("dve", hT[:, g * 512:(g + 1) * 512], hT[:, g * 512:(g + 1) * 512], p[:], ALU.add, [hT, p], [hT])
                        S.cp("pool", hTb[:], hT[:], [hT], [hTb])
                    if sample:
                        ssd_post(l, i, yy, xc[:, 0:4096], xc, zt, dsk, ssn, junk, ss, yTb, nb)
                    else:
                        S.dma("sp", y_loc[l][rows(i), :], yy[:], sb_r=yy)
                        S.dma("sp", ct_d[l][i], CTb[:], sb_r=CTb)
                        S.tt("dve", egt[:], sm[:, ECS, :], gamh[:], ALU.mult, [sm, gamh], [egt])
                        S.dma("sp", eg_d[l][i], egt[:], sb_r=egt)
                        S.tt("dve", gamh[:], gamh[:], sm[:, DEC, :], ALU.mult, [gamh, sm], [gamh])
                if not sample:
                    S.dma("sp", hloc_d[l], hT[:], sb_r=hT)
                    S.dma("sp", gamh_d[l], gamh[:], sb_r=gamh)
                    S.ts("dve", hT[:], hT[:], flg[:, 1:2], None, ALU.mult, None, [hT, flg], [hT])
                    S.dma("sp", hs_snd[l].ap(), hT[:], sb_r=hT)
            if not sample:
                S.collective(hs_snd[l], hs_rcv[l])
            else:
                S.barrier()

        def phase_ssd2(l):
            with ExitStack() as pes0:
                hinb = S.sb(pes0, "d2hinb", [128, 4096], BF16)
                with ExitStack() as pes:
                    hin = S.sb(pes, "d2hin", [128, 4096], F32, dma=True)
                    hT = S.sb(pes, "d2hT", [128, 4096], F32, dma=True)
                    gamh = S.sb(pes, "d2gamh", [128, 64], F32, dma=True)
                    hj = S.sb(pes, "d2hj", [128, 32, 128], F32, dma=True)
                    P0 = [S.ps(pes, "d2Q%d" % i, [128, 512], F32) for i in range(4)]
                    S.dma("sp", hin[:], hs_rcv[l].ap(), sb_w=hin)
                    S.dma("sp", hT[:], hloc_d[l], sb_w=hT)
                    S.dma("sp", gamh[:], gamh_d[l], sb_w=gamh)
                    S.ts("dve", hin[:], hin[:], flg[:, 0:1], None, ALU.mult, None, [hin, flg], [hin])
                    S.cp("pool", hinb[:], hin[:], [hin], [hinb])
                    hv = hin[:].rearrange("p (h d) -> p h d", h=64)
                    S.tt("dve", hv, hv, gamh[:].unsqueeze(2).to_broadcast([128, 64, 64]), ALU.mult, [hin, gamh], [hin])
                    S.tt("pool", hT[:], hT[:], hin[:], ALU.add, [hT, hin], [hT])
                    for q4 in range(8):
                        p = P0[q4 % 4]
                        for c in range(4):
                            cc_ = q4 * 4 + c
                            S.tr(p[:, c * 128:(c + 1) * 128], hT[:, cc_ * 128:(cc_ + 1) * 128], ident, [hT, cst], [p])
                        S.cp("act" if q4 % 2 else "dve", hj[:, q4 * 4:(q4 + 1) * 4, :], p[:].rearrange("p (c n) -> p c n", c=4), [p], [hj])
                    S.dma("sp", ssm_p[l].rearrange("h p n -> (h p) n").rearrange("(c q) n -> q c n", q=128), hj[:], sb_r=hj)
                S.barrier()
                with ExitStack() as pes:
                    dsk = S.sb(pes, "d2dsk", [128, 64], F32, dma=True)
                    ssn = S.sb(pes, "d2ssn", [128, 4096], F32, dma=True)
                    yys = [S.sb(pes, "d2y%d" % i, [128, 4096], F32, dma=True) for i in range(2)]
                    xss = [S.sb(pes, "d2xs%d" % i, [128, 4096], F32, dma=True) for i in range(2)]
                    zts = [S.sb(pes, "d2z%d" % i, [128, 4096], F32, dma=True) for i in range(2)]
                    ctbs = [S.sb(pes, "d2ct%d" % i, [128, 8, 128], BF16, dma=True) for i in range(2)]
                    egts = [S.sb(pes, "d2eg%d" % i, [128, 64], F32, dma=True) for i in range(2)]
                    tmp = S.sb(pes, "d2tmp", [128, 512], F32)
                    junk = S.sb(pes, "d2junk", [128, 512], F32)
                    ss = S.sb(pes, "d2ss", [128, 16], F32)
                    yTb = S.sb(pes, "d2yTb", [128, 32, 128], BF16, dma=True)
                    P = [S.ps(pes, "d2P%d" % i, [128, 512], F32) for i in range(8)]
                    rot = [0]

                    def nb():
                        b_ = P[rot[0] % 8]
                        rot[0] += 1
                        return b_
                    S.dma("sp", dsk[:], d_skip[l:l + 1, :].partition_broadcast(128), sb_w=dsk)
                    S.dma("sp", ssn[:], ssd_norm[l:l + 1, :].partition_broadcast(128), sb_w=ssn)
                    for i in range(NPT):
                        yy, xs_, zt, ctb, egt = yys[i % 2], xss[i % 2], zts[i % 2], ctbs[i % 2], egts[i % 2]
                        S.dma("sp", yy[:], y_loc[l][rows(i), :], sb_w=yy)
                        S.dma("sp", xs_[:], xcs[l][rows(i), 0:4096], sb_w=xs_)
                        S.dma("sp", zt[:], proj[l][rows(i), C_Z:C_Z + 4096], sb_w=zt)
                        S.dma("sp", ctb[:], ct_d[l][i], sb_w=ctb)
                        S.dma("sp", egt[:], eg_d[l][i], sb_w=egt)
                        for g in range(8):
                            pz = nb()
                            S.mm(pz[:], ctb[:, g, :], hinb[:, g * 512:(g + 1) * 512], True, True, [ctb, hinb], [pz])
                            S.tt("dve", tmp[:].rearrange("p (h d) -> p h d", h=8), pz[:].rearrange("p (h d) -> p h d", h=8),
                                 egt[:, g * 8:(g + 1) * 8].unsqueeze(2).to_broadcast([128, 8, 64]), ALU.mult, [pz, egt], [tmp])
                            S.tt("pool", yy[:, g * 512:(g + 1) * 512], yy[:, g * 512:(g + 1) * 512], tmp[:], ALU.add, [yy, tmp], [yy])
                        ssd_post(l, i, yy, xs_[:], xs_, zt, dsk, ssn, junk, ss, yTb, nb)
            S.barrier()

        def phase_o1(l, mT):
            with ExitStack() as pes:
                wg = [S.sb(pes, "owg%d" % i, [128, 16, 512], BF16, dma=True) for i in range(2)]
                ws = [S.sb(pes, "ows%d" % i, [128, 32, 512], BF16, dma=True) for i in range(2)]
                oa = [S.sb(pes, "ooa%d" % i, [128, 16, 128], BF16, dma=True) for i in range(2)]
                yb = [S.sb(pes, "oyb%d" % i, [128, 32, 128], BF16, dma=True) for i in range(2)]
                gA = [S.sb(pes, "ogA%d" % i, [128, 512], F32, dma=True) for i in range(2)]
                gB = [S.sb(pes, "ogB%d" % i, [128, 512], F32, dma=True) for i in range(2)]
                mg = [S.sb(pes, "omg%d" % i, [128, 512], F32) for i in range(2)]
                mgb = [S.sb(pes, "omgb%d" % i, [128, 512], BF16) for i in range(2)]
                pa = [S.ps(pes, "opa%d" % i, [128, 512], F32) for i in range(2)]
                pb = [S.ps(pes, "opb%d" % i, [128, 512], F32) for i in range(2)]
                pt = [S.ps(pes, "opt%d" % i, [128, 4, 128], BF16) for i in range(2)]
                wgv = w_gproj[l].rearrange("(kt p) n -> p kt n", p=128)
                wsv = w_sproj[l].rearrange("(kt p) n -> p kt n", p=128)
                n = 0
                pend = []

                def o1_tr(b2, cb, i):
                    for c in range(4):
                        S.tr(pt[b2][:, c, :], mgb[b2][:, c * 128:(c + 1) * 128], identb[:], [mgb[b2], identb], [pt[b2]])
                    g_, o_ = hgi(i)
                    S.cp("act", mT[g_][:, cb * 4:(cb + 1) * 4, o_:o_ + 128], pt[b2][:], [pt[b2]], [mT[g_]])
                for cb in range(4):
                    cs_ = slice(cb * 512, (cb + 1) * 512)
                    Wg = wg[cb % 2]
                    Ws = ws[cb % 2]
                    S.dma("pool", Wg[:], wgv[:, :, cs_], sb_w=Wg)
                    S.dma("pool", Ws[:], wsv[:, :, cs_], sb_w=Ws)
                    for i in range(NT):
                        b2 = n % 2
                        n += 1
                        S.dma("sp", oa[b2][:], oaT[l][i], sb_w=oa[b2])
                        S.dma("sp", yb[b2][:], ybT[l][i], sb_w=yb[b2])
                        S.dma("sp", gA[b2][:], proj[l][rows(i), C_G + cb * 512:C_G + (cb + 1) * 512], sb_w=gA[b2])
                        S.dma("sp", gB[b2][:], proj[l][rows(i), C_G + 2048 + cb * 512:C_G + 2048 + (cb + 1) * 512], sb_w=gB[b2])
                        for k in range(16):
                            S.mm(pa[b2][:], oa[b2][:, k, :], Wg[:, k, :], k == 0, k == 15, [oa[b2], Wg], [pa[b2]])
                        for k in range(32):
                            S.mm(pb[b2][:], yb[b2][:, k, :], Ws[:, k, :], k == 0, k == 31, [yb[b2], Ws], [pb[b2]])
                        S.act(gA[b2][:], gA[b2][:], AF.Sigmoid, [gA[b2]], [gA[b2]])
                        S.act(gB[b2][:], gB[b2][:], AF.Sigmoid, [gB[b2]], [gB[b2]])
                        S.tt("dve", mg[b2][:], pa[b2][:], gA[b2][:], ALU.mult, [pa[b2], gA[b2]], [mg[b2]])
                        S.tt("dve", gB[b2][:], pb[b2][:], gB[b2][:], ALU.mult, [pb[b2], gB[b2]], [gB[b2]])
                        S.tt("pool", mgb[b2][:], mg[b2][:], gB[b2][:], ALU.add, [mg[b2], gB[b2]], [mgb[b2]])
                        pend.append((b2, cb, i))
                        if len(pend) > 1:
                            o1_tr(*pend.pop(0))
                while pend:
                    o1_tr(*pend.pop(0))
            S.barrier()

        def phase_res(l, mT, wsrc, nk, gate_i, xsrc, xdst, actsrc=None):
            with ExitStack() as pes:
                wm = [S.sb(pes, "rwm%d" % i, [128, nk, 512], BF16, dma=True) for i in range(2)]
                NB = 3
                xb = [S.sb(pes, "rxb%d" % i, [128, 512], F32, dma=True) for i in range(NB)]
                gb = [S.sb(pes, "rgb%d" % i, [128, 512], F32, dma=True) for i in range(NB)]
                pp = [S.ps(pes, "rpp%d" % i, [128, 512], F32) for i in range(NB)]
                if actsrc is not None:
                    ab = [S.sb(pes, "rab%d" % i, [128, nk, 128], BF16, dma=True) for i in range(NB)]
                wv = wsrc.rearrange("(kt p) n -> p kt n", p=128)
                n = 0
                for cb in range(4):
                    cs_ = slice(cb * 512, (cb + 1) * 512)
                    W = wm[cb % 2]
                    S.dma("pool", W[:], wv[:, :, cs_], sb_w=W)
                    for i in range(NT):
                        b2 = n % NB
                        n += 1
                        g = grp(i)
                        S.dma("sp", xb[b2][:], xsrc[rows(i), cs_], sb_w=xb[b2])
                        S.dma("sp", gb[b2][:], mod[l][g, :, gate_i * D + cb * 512:gate_i * D + (cb + 1) * 512], sb_w=gb[b2])
                        if actsrc is not None:
                            S.dma("sp", ab[b2][:], actsrc[i], sb_w=ab[b2])
                            for k in range(nk):
                                S.mm(pp[b2][:], ab[b2][:, k, :], W[:, k, :], k == 0, k == nk - 1, [ab[b2], W], [pp[b2]])
                        else:
                            for k in range(nk):
                                S.mm(pp[b2][:], hview(mT, i, k), W[:, k, :], k == 0, k == nk - 1, [mT[hgi(i)[0]], W], [pp[b2]])
                        S.tt("dve", gb[b2][:], pp[b2][:], gb[b2][:], ALU.mult, [pp[b2], gb[b2]], [gb[b2]])
                        S.tt("pool", xb[b2][:], xb[b2][:], gb[b2][:], ALU.add, [xb[b2], gb[b2]], [xb[b2]])
                        S.dma("act", xdst[rows(i), cs_], xb[b2][:], sb_r=xb[b2])
            S.barrier()

        def phase_f1(l, hT):
            with ExitStack() as pes:
                wgt = [S.sb(pes, "fwg%d" % i, [128, 16, 512], BF16, dma=True) for i in range(2)]
                wup = [S.sb(pes, "fwu%d" % i, [128, 16, 512], BF16, dma=True) for i in range(2)]
                pg = [S.ps(pes, "fpg%d" % i, [128, 512], F32) for i in range(2)]
                pu = [S.ps(pes, "fpu%d" % i, [128, 512], F32) for i in range(2)]
                sg = [S.sb(pes, "fsg%d" % i, [128, 512], F32) for i in range(2)]
                at = [S.sb(pes, "fat%d" % i, [128, 512], BF16, dma=True) for i in range(2)]
                wv = w_fin[l].rearrange("(kt p) n -> p kt n", p=128)
                n = 0
                for jb in range(11):
                    Wg = wgt[jb % 2]
                    Wu = wup[jb % 2]
                    S.dma("pool", Wg[:], wv[:, :, jb * 512:(jb + 1) * 512], sb_w=Wg)
                    S.dma("pool", Wu[:], wv[:, :, DFF + jb * 512:DFF + (jb + 1) * 512], sb_w=Wu)
                    for jj in range(4):
                        j = jb * 4 + jj
                        for tg in range(NG + 1):
                            N = 512 if tg < NG else 128
                            b2 = n % 2
                            n += 1
                            for k in range(16):
                                S.mm(pg[b2][:, 0:N], Wg[:, k, jj * 128:(jj + 1) * 128], hT[tg][:, k, :], k == 0, k == 15, [Wg, hT[tg]], [pg[b2]])
                            for k in range(16):
                                S.mm(pu[b2][:, 0:N], Wu[:, k, jj * 128:(jj + 1) * 128], hT[tg][:, k, :], k == 0, k == 15, [Wu, hT[tg]], [pu[b2]])
                            S.act(sg[b2][:, 0:N], pg[b2][:, 0:N], AF.Silu, [pg[b2]], [sg[b2]])
                            S.tt("dve", at[b2][:, 0:N], sg[b2][:, 0:N], pu[b2][:, 0:N], ALU.mult, [sg[b2], pu[b2]], [at[b2]])
                            if tg < NG:
                                S.dma("sp", actT[l][4 * tg:4 * tg + 4, :, j, :].rearrange("a p t -> p a t"),
                                      at[b2][:, 0:512].rearrange("p (a t) -> p a t", a=4), sb_r=at[b2])
                            else:
                                S.dma("sp", actT[l][NPT, :, j, :], at[b2][:, 0:128], sb_r=at[b2])
            S.barrier()

        def phase_final(xsrc):
            with ExitStack() as pes:
                xt = [S.sb(pes, "zx%d" % i, [128, D], F32, dma=True) for i in range(2)]
                fn = S.sb(pes, "zfn", [128, D], F32, dma=True)
                junk = S.sb(pes, "zjunk", [128, D], F32)
                ss = S.sb(pes, "zss", [128, 2], F32)
                S.dma("sp", fn[:], fnorm.partition_broadcast(128), sb_w=fn)
                for i in range(NT):
                    x = xt[i % 2]
                    S.dma("sp", x[:], xsrc[rows(i), :], sb_w=x)
                    S.memset("pool", ss[:], 0.0, [ss])
                    S.act(junk[:], x[:], AF.Square, [x, ss], [junk, ss], accum=ss[:, 0:1])
                    S.ts("dve", ss[:, 1:2], ss[:, 0:1], 1.0 / D, EPS, ALU.mult, ALU.add, [ss], [ss])
                    S.act(ss[:, 1:2], ss[:, 1:2], AF.Sqrt, [ss], [ss])
                    S.recip(ss[:, 1:2], ss[:, 1:2], [ss], [ss])
                    S.stt("dve", x[:], x[:], ss[:, 1:2], fn[:], ALU.mult, ALU.mult, [x, ss, fn], [x])
                    S.dma("sp", y_out[rows(i), :], x[:], sb_r=x)
            S.barrier()

        S.phase = "mod"
        phase_mod()
        xcur = xin
        for l in range(DEPTH):
            with ExitStack() as ges:
                hT = [S.sb(ges, "hT%d_%d" % (l, g), [128, 16, 512 if g < NG else 128], BF16) for g in range(NG + 1)]
                S.phase = "norm%d" % l
                phase_norm(xcur, l, 0, 1, hT)
                S.phase = "proj%d" % l
                phase_proj(l, hT)
            S.phase = "conv%d" % l
            phase_conv(l)
            S.phase = "gla%d" % l
            phase_gla(l, False)
            S.phase = "gla_s%d" % l
            phase_gla(l, True)
            S.phase = "ssd%d" % l
            phase_ssd(l, False)
            S.phase = "ssd_s%d" % l
            phase_ssd(l, True)
            S.phase = "gla2%d" % l
            phase_gla2(l)
            S.phase = "ssd2%d" % l
            phase_ssd2(l)
            with ExitStack() as ges:
                mT = [S.sb(ges, "mT%d_%d" % (l, g), [128, 16, 512 if g < NG else 128], BF16) for g in range(NG + 1)]
                S.phase = "o1%d" % l
                phase_o1(l, mT)
                S.phase = "res_mix%d" % l
                phase_res(l, mT, w_mix[l], 16, 2, xcur, xv[l][0])
            with ExitStack() as ges:
                hT = [S.sb(ges, "h2T%d_%d" % (l, g), [128, 16, 512 if g < NG else 128], BF16) for g in range(NG + 1)]
                S.phase = "norm2%d" % l
                phase_norm(xv[l][0], l, 3, 4, hT)
                S.phase = "f1%d" % l
                phase_f1(l, hT)
            S.phase = "res_ffo%d" % l
            phase_res(l, None, w_fout[l], 44, 5, xv[l][0], xv[l][1], actsrc=actT[l])
            xcur = xv[l][1]
        S.phase = "final"
        phase_final(xcur)
        import os
        if os.environ.get("KTRACE_MAP"):
            S.namemap = {}
        S.emit()
        if S.namemap is not None:
            import json
            json.dump(S.namemap, open(os.environ["KTRACE_MAP"], "w"))
        print("ops:", S.nops, {e: len(S.prog[e]) for e in S.prog})
    return nc


_NC_CACHE = {}


def kernel(x_prompt, x_sample, c_prompt, c_sample, state_gla, state_ssm, state_conv,
           w_ada, b_ada, w_in, w_gla_gate, b_gla_gate, gla_norm, w_gla_proj, conv_w, conv_b,
           dt_bias, A_log, d_skip, ssd_norm, w_ssd_proj, w_mix_out, w_ffn_in, w_ffn_out, final_norm):
    f = lambda a: np.ascontiguousarray(np.asarray(a, dtype=np.float32))
    x_prompt, x_sample, c_prompt, c_sample = f(x_prompt), f(x_sample), f(c_prompt), f(c_sample)
    state_gla, state_ssm, state_conv = f(state_gla), f(state_ssm), f(state_conv)
    if "nc" not in _NC_CACHE:
        _NC_CACHE["nc"] = build_nc()
    nc = _NC_CACHE["nc"]
    consts = make_consts()
    shared = {
        "consts": consts, "w_ada": f(w_ada), "b_ada": f(b_ada), "w_in": f(w_in), "w_gla_gate": f(w_gla_gate),
        "b_gla_gate": f(b_gla_gate), "gla_norm": f(gla_norm), "w_gla_proj": f(w_gla_proj), "conv_w": f(conv_w),
        "conv_b": f(conv_b), "dt_bias": f(dt_bias), "A_log": f(A_log), "d_skip": f(d_skip), "ssd_norm": f(ssd_norm),
        "w_ssd_proj": f(w_ssd_proj), "w_mix_out": f(w_mix_out), "w_ffn_in": f(w_ffn_in), "w_ffn_out": f(w_ffn_out),
        "final_norm": f(final_norm).reshape(1, D),
    }
    in_maps = []
    for c in range(8):
        ps = c // 2
        hf = c % 2
        s0 = c * NSEG
        xin = np.concatenate([x_prompt[ps][hf * NPT * 128:(hf + 1) * NPT * 128], x_sample[s0:s0 + NSEG].reshape(NSEG * SL, D)], axis=0)
        cc = np.stack([np.broadcast_to(c_prompt[ps][None, :], (128, D)),
                       np.repeat(c_sample[s0:s0 + NSEG], SL, axis=0)], axis=0)
        m = dict(shared)
        m["xin"] = np.ascontiguousarray(xin)
        fl = np.zeros((128, 2), np.float32)
        fl[:, 0] = float(hf)
        fl[:, 1] = float(1 - hf)
        m["flags"] = fl
        m["cc"] = np.ascontiguousarray(cc)
        m["st_gla"] = np.ascontiguousarray(state_gla[:, s0:s0 + NSEG])
        m["st_ssm"] = np.ascontiguousarray(state_ssm[:, s0:s0 + NSEG])
        m["st_conv"] = np.ascontiguousarray(state_conv[:, s0:s0 + NSEG])
        in_maps.append(m)
    res = run_bass_kernel_spmd(nc, in_maps, core_ids=list(range(8)))
    R = res.results
    y_prompt = np.stack([np.concatenate([R[2 * p]["y"][:NPT * 128], R[2 * p + 1]["y"][:NPT * 128]], axis=0) for p in range(4)], axis=0)
    y_sample = np.concatenate([R[c]["y"][NPT * 128:].reshape(NSEG, SL, D) for c in range(8)], axis=0)
    gla_p = np.stack([R[2 * p + 1]["gla_p"] for p in range(4)], axis=1)
    ssm_p = np.stack([R[2 * p + 1]["ssm_p"] for p in range(4)], axis=1)
    conv_p = np.stack([R[2 * p + 1]["conv_p"] for p in range(4)], axis=1)
    gla_s = np.concatenate([R[c]["gla_s"] for c in range(8)], axis=1)
    ssm_s = np.concatenate([R[c]["ssm_s"] for c in range(8)], axis=1)
    conv_s = np.concatenate([R[c]["conv_s"] for c in range(8)], axis=1)
    return (y_prompt.astype(np.float32), y_sample.astype(np.float32), gla_p.astype(np.float32), ssm_p.astype(np.float32),
            conv_p.astype(np.float32), gla_s.astype(np.float32), ssm_s.astype(np.float32), conv_s.astype(np.float32))
```
